# Optimizing a Trainium2 kernel written in Bass

```python
import math
import jax, jax.numpy as jnp
from jax import lax
import numpy as np

D_MODEL = 1024
BATCH = 16
SEQ = 2048
DEPTH = 1
DEC_BATCH = 128
DEC_SEQ = 8
PAST_LEN = 8192
PAGE_SIZE = 128

ATT_WIDTH = D_MODEL // 2
HEAD_DIM = 64
N_HEADS = ATT_WIDTH // HEAD_DIM
DILATED_BRANCHES = ((128, 1), (512, 4), (2048, 16))
MAX_WINDOW = max(w for w, _ in DILATED_BRANCHES)
NUM_BUCKETS = 32
MAX_DISTANCE = MAX_WINDOW
POOL_WIDTH = D_MODEL - ATT_WIDTH
POOL_WINDOWS = (2, 4, 8, 16)
N_POOL_GROUPS = len(POOL_WINDOWS)
POOL_GROUP = POOL_WIDTH // N_POOL_GROUPS
POOL_CTX = max(POOL_WINDOWS) - 1
D_FF = 4 * D_MODEL
N_ADA = 6
EPS = 1e-6
NEG_INF = -1e30

kernel_name = "hymba_dilated_pool_adaln_decoder_step"


def rms_norm(x, g):
    xf = x.astype(jnp.float32)
    y = xf * lax.rsqrt(jnp.mean(xf * xf, axis=-1, keepdims=True) + EPS)
    return (y * g.astype(jnp.float32)).astype(x.dtype)


def rel_bucket(dist):
    exact = NUM_BUCKETS // 2
    d = jnp.maximum(dist.astype(jnp.float32), 1.0)
    large = exact + (jnp.log(d / exact) / math.log(MAX_DISTANCE / exact)
                     * (NUM_BUCKETS - exact)).astype(jnp.int32)
    large = jnp.minimum(large, NUM_BUCKETS - 1)
    return jnp.where(dist < exact, dist, large)


def _adaln(c, w_ada, b_ada):
    a = jax.nn.silu(c) @ w_ada + b_ada
    return jnp.split(a[:, None, :], N_ADA, axis=-1)


def _mixer_in(x, shift, scale, norm_g, w_in, q_g, k_g):
    B, T, _ = x.shape
    h = rms_norm(x, norm_g) * (1 + scale) + shift
    z = h @ w_in
    q = z[..., :ATT_WIDTH].reshape(B, T, N_HEADS, HEAD_DIM)
    k = z[..., ATT_WIDTH:2 * ATT_WIDTH].reshape(B, T, N_HEADS, HEAD_DIM)
    v = z[..., 2 * ATT_WIDTH:3 * ATT_WIDTH].reshape(B, T, N_HEADS, HEAD_DIM)
    u = z[..., 3 * ATT_WIDTH:]
    q = rms_norm(q, q_g) * (HEAD_DIM ** -0.5)
    k = rms_norm(k, k_g)
    return q, k, v, u


def _dilated_prompt(q, k, v, window, dil, rel_bias):
    B, S, H, E = q.shape
    n_sub = window // dil
    L = S // dil
    blk = min(n_sub, L)
    nb = -(-L // blk)
    Lp = nb * blk

    def split(a):
        a = a.reshape(B, L, dil, H, E)
        a = jnp.pad(a, ((0, 0), (0, Lp - L), (0, 0), (0, 0), (0, 0)))
        return a.reshape(B, nb, blk, dil, H, E)

    def with_prev(a):
        prev = jnp.concatenate([jnp.zeros_like(a[:, :1]), a[:, :-1]], axis=1)
        return jnp.concatenate([prev, a], axis=2)

    qb = split(q)
    kk = with_prev(split(k))
    vv = with_prev(split(v))
    logits = jnp.einsum('bnqrhe,bnkrhe->bnrhqk', qb, kk).astype(jnp.float32)
    qi = jnp.arange(blk)[:, None]
    ki = jnp.arange(2 * blk)[None, :]
    dist = qi + blk - ki
    key_sub = jnp.arange(nb)[:, None, None] * blk - blk + ki[None]
    valid = (dist >= 0)[None] & (dist <= n_sub)[None] & (key_sub >= 0)
    bias = rel_bias[rel_bucket(jnp.clip(dist, 0, n_sub) * dil)]
    logits = logits + jnp.transpose(bias, (2, 0, 1)).astype(jnp.float32)
    logits = jnp.where(valid[None, :, None, None], logits, NEG_INF)
    lse = jax.nn.logsumexp(logits, axis=-1)
    p = jnp.exp(logits - lse[..., None])
    o = jnp.einsum('bnrhqk,bnkrhe->bnqrhe', p, vv.astype(jnp.float32))
    o = o.reshape(B, Lp, dil, H, E)[:, :L].reshape(B, S, H, E)
    lse = jnp.transpose(lse, (0, 1, 4, 2, 3)).reshape(B, Lp, dil, H)[:, :L].reshape(B, S, H)
    return o, lse


def _dilated_sample(q, k_all, v_all, window, dil, rel_bias):
    B, T, H, E = q.shape
    n_past = k_all.shape[1] - T
    n_sub = window // dil
    j = jnp.arange(n_sub + 1)
    idx = n_past + jnp.arange(T)[:, None] - dil * j[None, :]
    valid = idx >= 0
    idx = jnp.maximum(idx, 0)
    kg = k_all[:, idx]
    vg = v_all[:, idx]
    logits = jnp.einsum('bthe,btjhe->bthj', q, kg).astype(jnp.float32)
    bias = rel_bias[rel_bucket(j * dil)]
    logits = logits + jnp.transpose(bias).astype(jnp.float32)[None, None]
    logits = jnp.where(valid[None, :, None, :], logits, NEG_INF)
    lse = jax.nn.logsumexp(logits, axis=-1)
    p = jnp.exp(logits - lse[..., None])
    o = jnp.einsum('bthj,btjhe->bthe', p, vg.astype(jnp.float32))
    return o, lse


def _merge_branches(results):
    outs = jnp.stack([o for o, _ in results])
    lses = jnp.stack([s for _, s in results])
    wts = jax.nn.softmax(lses, axis=0)
    o = jnp.sum(wts[..., None] * outs, axis=0)
    return o.reshape(o.shape[0], o.shape[1], ATT_WIDTH)


def _pool_mix(u_ext, pos0, w_pool, pool_scale):
    B, n, C = u_ext.shape
    T = n - POOL_CTX
    uf = u_ext.astype(jnp.float32)
    cs = jnp.pad(jnp.cumsum(uf, axis=1), ((0, 0), (1, 0), (0, 0)))
    end = POOL_CTX + 1 + jnp.arange(T)
    pos = (pos0 + jnp.arange(T)).astype(jnp.float32)
    cur = uf[:, POOL_CTX:]
    groups = []
    for g, w in enumerate(POOL_WINDOWS):
        lo, hi = g * POOL_GROUP, (g + 1) * POOL_GROUP
        win_sum = cs[:, end, lo:hi] - cs[:, end - w, lo:hi]
        cnt = jnp.minimum(float(w), pos + 1.0)[None, :, None]
        groups.append(win_sum / cnt - cur[:, :, lo:hi])
    pooled = jnp.stack(groups, axis=2)
    y = jnp.einsum('btgc,gcd->btgd', pooled, w_pool.astype(jnp.float32)).reshape(B, T, C)
    return y * pool_scale


def _layer_tail(x, attn, pool, gate1, shift2, scale2, gate2, norm2_g, w_out, w_up, w_down):
    mix = jnp.concatenate([attn, pool], axis=-1).astype(x.dtype) @ w_out
    x = x + gate1 * mix
    h = rms_norm(x, norm2_g) * (1 + scale2) + shift2
    f = jnp.square(jax.nn.relu(h @ w_up)) @ w_down
    return x + gate2 * f


def setup_inputs(seed: int = 0) -> dict:
    key = jax.random.key(seed)
    ks = jax.random.split(key, 20)
    nrm = jax.random.normal
    wb = min(MAX_WINDOW, PAST_LEN)
    d_in = 3 * ATT_WIDTH + POOL_WIDTH
    return {
        "x_prompt": nrm(ks[0], (BATCH, SEQ, D_MODEL), jnp.float32),
        "x_sample": nrm(ks[1], (DEC_BATCH, DEC_SEQ, D_MODEL), jnp.float32),
        "c_prompt": nrm(ks[2], (BATCH, D_MODEL), jnp.float32),
        "c_sample": nrm(ks[3], (DEC_BATCH, D_MODEL), jnp.float32),
        "cache_k": nrm(ks[4], (DEPTH, DEC_BATCH, wb, N_HEADS, HEAD_DIM), jnp.float32),
        "cache_v": nrm(ks[5], (DEPTH, DEC_BATCH, wb, N_HEADS, HEAD_DIM), jnp.float32),
        "state_pool": nrm(ks[6], (DEPTH, DEC_BATCH, POOL_CTX, POOL_WIDTH), jnp.float32),
        "w_ada": nrm(ks[7], (DEPTH, D_MODEL, N_ADA * D_MODEL), jnp.float32) * (0.5 * D_MODEL ** -0.5),
        "b_ada": 0.01 * nrm(ks[8], (DEPTH, N_ADA * D_MODEL), jnp.float32),
        "norm1_g": 1.0 + 0.05 * nrm(ks[9], (DEPTH, D_MODEL), jnp.float32),
        "norm2_g": 1.0 + 0.05 * nrm(ks[10], (DEPTH, D_MODEL), jnp.float32),
        "w_in": nrm(ks[11], (DEPTH, D_MODEL, d_in), jnp.float32) * D_MODEL ** -0.5,
        "q_norm_g": 1.0 + 0.05 * nrm(ks[12], (DEPTH, HEAD_DIM), jnp.float32),
        "k_norm_g": 1.0 + 0.05 * nrm(ks[13], (DEPTH, HEAD_DIM), jnp.float32),
        "rel_bias": 0.5 * nrm(ks[14], (NUM_BUCKETS, N_HEADS), jnp.float32),
        "w_pool": nrm(ks[15], (DEPTH, N_POOL_GROUPS, POOL_GROUP, POOL_GROUP), jnp.float32) * POOL_GROUP ** -0.5,
        "pool_scale": 1.0 + 0.1 * nrm(ks[16], (DEPTH, POOL_WIDTH), jnp.float32),
        "w_out": nrm(ks[17], (DEPTH, D_MODEL, D_MODEL), jnp.float32) * D_MODEL ** -0.5,
        "w_up": nrm(ks[18], (DEPTH, D_MODEL, D_FF), jnp.float32) * D_MODEL ** -0.5,
        "w_down": nrm(ks[19], (DEPTH, D_FF, D_MODEL), jnp.float32) * D_FF ** -0.5,
    }


def reference(x_prompt, x_sample, c_prompt, c_sample, cache_k, cache_v, state_pool,
              w_ada, b_ada, norm1_g, norm2_g, w_in, q_norm_g, k_norm_g, rel_bias,
              w_pool, pool_scale, w_out, w_up, w_down):
    yp, ys = x_prompt, x_sample
    kp, vp, pp, ksm, vsm, psm = [], [], [], [], [], []
    for l in range(DEPTH):
        m = _adaln(c_prompt, w_ada[l], b_ada[l])
        q, k, v, u = _mixer_in(yp, m[0], m[1], norm1_g[l], w_in[l], q_norm_g[l], k_norm_g[l])
        attn = _merge_branches([_dilated_prompt(q, k, v, w, d, rel_bias) for w, d in DILATED_BRANCHES])
        u_ext = jnp.pad(u, ((0, 0), (POOL_CTX, 0), (0, 0)))
        pool = _pool_mix(u_ext, 0, w_pool[l], pool_scale[l])
        keep = min(MAX_WINDOW, k.shape[1])
        kp.append(k[:, -keep:])
        vp.append(v[:, -keep:])
        pp.append(u[:, -POOL_CTX:])
        yp = _layer_tail(yp, attn, pool, m[2], m[3], m[4], m[5],
                         norm2_g[l], w_out[l], w_up[l], w_down[l])

        m = _adaln(c_sample, w_ada[l], b_ada[l])
        q, k, v, u = _mixer_in(ys, m[0], m[1], norm1_g[l], w_in[l], q_norm_g[l], k_norm_g[l])
        k_all = jnp.concatenate([cache_k[l].astype(k.dtype), k], axis=1)
        v_all = jnp.concatenate([cache_v[l].astype(v.dtype), v], axis=1)
        attn = _merge_branches([_dilated_sample(q, k_all, v_all, w, d, rel_bias) for w, d in DILATED_BRANCHES])
        u_ext = jnp.concatenate([state_pool[l].astype(u.dtype), u], axis=1)
        pool = _pool_mix(u_ext, PAST_LEN, w_pool[l], pool_scale[l])
        wb = cache_k.shape[2]
        ksm.append(k_all[:, -wb:])
        vsm.append(v_all[:, -wb:])
        psm.append(u_ext[:, -POOL_CTX:])
        ys = _layer_tail(ys, attn, pool, m[2], m[3], m[4], m[5],
                         norm2_g[l], w_out[l], w_up[l], w_down[l])

    k_win_prompt = jnp.stack(kp)
    v_win_prompt = jnp.stack(vp)
    pool_prompt = jnp.stack(pp)
    k_win_sample = jnp.stack(ksm)
    v_win_sample = jnp.stack(vsm)
    pool_sample = jnp.stack(psm)
    return (yp, ys, k_win_prompt, v_win_prompt, pool_prompt, k_win_sample, v_win_sample, pool_sample)
```

```python
import math
from contextlib import ExitStack

import numpy as np
import ml_dtypes
import concourse.bass as bass
import concourse.mybir as mybir
from concourse.bass_utils import run_bass_kernel_spmd

F32 = mybir.dt.float32
BF16 = mybir.dt.bfloat16
AF = mybir.ActivationFunctionType
ALU = mybir.AluOpType
AX = mybir.AxisListType

NCORES = 8
NPS = 2
NSS = 16
SEQ = 2048
DM = 1024
AW = 512
NH = 8
HD = 64
FF = 4096
TSEQ = 8
WB = 2048
PCTX = 15
EPS = 1e-6
GW = 2432
WROW = 2560


class Ev:
    __slots__ = ("sem", "val", "eng", "sid")

    def __init__(self, sem, val, eng, sid):
        self.sem, self.val, self.eng, self.sid = sem, val, eng, sid


class Buf:
    def __init__(self, name):
        self.name = name
        self.w = None
        self.rd = {}
        self.dsem = None
        self.dsid = None
        self.dcount = 0


class Eng:
    def __init__(self, name):
        self.name = name
        self.ops = []
        self.sem = None
        self.sid = None
        self.count = 0
        self.waited = {}


class Ctx:
    def __init__(self, nc):
        self.nc = nc
        self.engs = {n: Eng(n) for n in ("pe", "act", "dve", "pool", "sp")}
        self.bufs = []
        self.nsem = 0
        self.free_dsems = []
        for e in self.engs.values():
            e.sem, e.sid = self._new_sem("e_" + e.name)

    def _new_sem(self, name):
        self.nsem += 1
        return self.nc.alloc_semaphore(f"{name}_{self.nsem}"), self.nsem

    def buf(self, name):
        b = Buf(name)
        self.bufs.append(b)
        return b

    def retire(self, bufs):
        for b in bufs:
            if b.dsem is not None:
                self.free_dsems.append((b.dsem, b.dsid, b.dcount))
                b.dsem = None
            if b in self.bufs:
                self.bufs.remove(b)

    def _wait(self, E, ev):
        if ev is None:
            return
        if E.waited.get(ev.sid, 0) >= ev.val:
            return
        E.waited[ev.sid] = ev.val
        sem, val = ev.sem, ev.val
        E.ops.append(lambda h: h.wait_ge(sem, val))

    def op(self, ename, fn, reads=(), writes=()):
        E = self.engs[ename]
        for b in reads:
            if b.w is not None and not (ename == "pe" and b.w.eng == "pe"):
                self._wait(E, b.w)
        for b in writes:
            if b.w is not None and not (ename == "pe" and b.w.eng == "pe"):
                self._wait(E, b.w)
            for ev in b.rd.values():
                if ev.eng != ename:
                    self._wait(E, ev)
        if E.count >= 30000:
            E.sem, E.sid = self._new_sem("e_" + E.name)
            E.count = 0
        E.count += 1
        sem, val = E.sem, E.count
        ev = Ev(sem, val, ename, E.sid)

        def run(h):
            inst = fn(h)
            inst.then_inc(sem, 1)

        E.ops.append(run)
        for b in reads:
            b.rd[ename] = ev
        for b in writes:
            b.w = ev
            b.rd = {}
        return ev

    def dma(self, q, out_ap, in_ap, reads=(), writes=(), sembuf=None):
        E = self.engs[q]
        for b in reads:
            self._wait(E, b.w)
        for b in writes:
            self._wait(E, b.w)
            for ev in b.rd.values():
                self._wait(E, ev)
        sb = sembuf or (writes[0] if writes else reads[0])
        if sb.dsem is None:
            if self.free_dsems:
                sb.dsem, sb.dsid, sb.dcount = self.free_dsems.pop()
            else:
                sb.dsem, sb.dsid = self._new_sem("d")
        sb.dcount += 16
        sem = sb.dsem
        ev = Ev(sem, sb.dcount, "dma", sb.dsid)
        E.ops.append(lambda h: h.dma_start(out=out_ap, in_=in_ap).then_inc(sem, 16))
        for b in reads:
            b.rd[("dma", sb.dsid)] = ev
        for b in writes:
            b.w = ev
            b.rd = {}
        return ev

    def barrier(self, engines=None):
        evs = []
        for e in self.engs.values():
            if e.count > 0:
                evs.append(Ev(e.sem, e.count, e.name, e.sid))
        for b in self.bufs:
            if b.dsem is not None and b.dcount > 0:
                evs.append(Ev(b.dsem, b.dcount, "dma", b.dsid))
        for s, sid, cnt in self.free_dsems:
            if cnt > 0:
                evs.append(Ev(s, cnt, "dma", sid))
        for en, E in self.engs.items():
            if engines is not None and en not in engines:
                continue
            for ev in evs:
                if ev.sid == E.sid:
                    continue
                self._wait(E, ev)

    def emit(self):
        nc = self.nc
        self.barrier(engines=("sp",))
        with nc.Block() as block:
            @block.tensor
            def _(h):
                for f in self.engs["pe"].ops:
                    f(h)

            @block.scalar
            def _(h):
                for f in self.engs["act"].ops:
                    f(h)

            @block.vector
            def _(h):
                for f in self.engs["dve"].ops:
                    f(h)

            @block.gpsimd
            def _(h):
                for f in self.engs["pool"].ops:
                    f(h)

            @block.sync
            def _(h):
                for f in self.engs["sp"].ops:
                    f(h)


class T:
    def __init__(self, t, b):
        self.t, self.b = t, b

    def __getitem__(self, k):
        return self.t[k]


class K:
    pass


def AP(t, off, dims):
    th = t.t if isinstance(t, T) else t
    return bass.AP(tensor=th, offset=off, ap=[list(d) for d in dims])


def act(k, out, in_, func, reads, writes, **kw):
    return k.c.op("act", lambda h: h.activation(out=out, in_=in_, func=func, **kw), reads, writes)


def tt(k, eng, out, in0, in1, op, reads, writes):
    return k.c.op(eng, lambda h: h.tensor_tensor(out=out, in0=in0, in1=in1, op=op), reads, writes)


def ts(k, eng, out, in0, s1, s2, op0, op1, reads, writes):
    if op1 is None:
        return k.c.op(eng, lambda h: h.tensor_scalar(out=out, in0=in0, scalar1=s1, scalar2=None, op0=op0), reads, writes)
    return k.c.op(eng, lambda h: h.tensor_scalar(out=out, in0=in0, scalar1=s1, scalar2=s2, op0=op0, op1=op1), reads, writes)


def stt(k, out, in0, scalar, in1, op0, op1, reads, writes):
    return k.c.op("dve", lambda h: h.scalar_tensor_tensor(out=out, in0=in0, scalar=scalar, in1=in1, op0=op0, op1=op1),
                  reads, writes)


def cp(k, eng, out, in_, reads, writes):
    if eng == "act":
        return k.c.op("act", lambda h: h.activation(out=out, in_=in_, func=AF.Copy), reads, writes)
    return k.c.op(eng, lambda h: h.tensor_copy(out=out, in_=in_), reads, writes)


def memset(k, eng, ap, val, writes):
    return k.c.op(eng, lambda h: h.memset(ap, val), (), writes)


def recip(k, out, in_, reads, writes):
    return k.c.op("dve", lambda h: h.reciprocal(out=out, in_=in_), reads, writes)


def reduce_add(k, out, in_, reads, writes):
    return k.c.op("dve", lambda h: h.tensor_reduce(out=out, in_=in_, axis=AX.X, op=ALU.add), reads, writes)


def mms(k, lst, reads, writes):
    lst = list(lst)

    def f(h):
        i = None
        for (o, l, r, st, sp) in lst:
            i = h.matmul(o, lhsT=l, rhs=r, start=st, stop=sp)
        return i

    return k.c.op("pe", f, reads, writes)


def trs(k, lst, ident, reads, writes):
    lst = list(lst)

    def f(h):
        i = None
        for (o, a) in lst:
            n = a.shape[0]
            i = h.transpose(out=o, in_=a, identity=ident[0:n, 0:n])
        return i

    return k.c.op("pe", f, reads, writes)


class Scope:
    def __init__(self, k):
        self.k = k
        self.es = ExitStack()
        self.bufs = []

    def __enter__(self):
        self.es.__enter__()
        return self

    def sb(self, name, shape, dt):
        t = self.es.enter_context(self.k.nc.sbuf_tensor(name + f"_{self.k.uid()}", list(shape), dt))
        b = self.k.c.buf(name)
        self.bufs.append(b)
        return T(t, b)

    def __exit__(self, *a):
        self.k.c.barrier()
        self.k.c.retire(self.bufs)
        return self.es.__exit__(*a)


def _bucket(d):
    d = np.asarray(d, dtype=np.int64)
    exact = 16
    df = np.maximum(d.astype(np.float32), np.float32(1.0))
    large = exact + (np.log(df / np.float32(exact)) / np.float32(math.log(2048 / exact))
                     * np.float32(32 - exact)).astype(np.int32)
    large = np.minimum(large, 31)
    return np.where(d < exact, d, large).astype(np.int64)


def _mult(d):
    d = np.asarray(d, dtype=np.int64)
    m = ((d >= 0) & (d <= 128)).astype(np.float32)
    m += ((d >= 0) & (d <= 512) & (d % 4 == 0)).astype(np.float32)
    m += ((d >= 0) & (d <= 2048) & (d % 16 == 0)).astype(np.float32)
    return m


def make_consts():
    cst = {}
    cst["ident"] = np.eye(128, dtype=np.float32)
    d = np.arange(WROW) - 511
    ohm = np.zeros((32, WROW), np.float32)
    dd = np.maximum(d, 0)
    ohm[_bucket(dd), np.arange(WROW)] = _mult(d)
    cst["ohm"] = ohm
    ohs = np.zeros((32, 14, 8, 128), np.float32)
    kr = np.arange(128)
    for t in range(8):
        dA = 128 + t - kr
        v = (dA >= 0) & (dA <= 128)
        ohs[_bucket(np.maximum(dA, 0))[v], 0, t, kr[v]] = 1.0
        for rho in range(4):
            if t % 4 != rho:
                continue
            d2 = 512 + t - rho - 4 * kr
            v = (d2 >= 0) & (d2 <= 512)
            ohs[_bucket(np.maximum(d2, 0))[v], 1 + rho, t, kr[v]] = 1.0
        d3 = 2048 - 16 * kr
        ohs[_bucket(d3), 5 + t, t, kr] = 1.0
        for tp in range(t + 1):
            dn = t - tp
            ohs[_bucket(dn), 13, t, tp] = _mult(dn)
    cst["ohs"] = ohs.reshape(32, 14 * 8 * 128)
    sel = np.zeros((18, 3, 128), np.float32)
    sel[0, 0, :] = 1.0
    sel[1, 1, :] = 1.0
    for b in range(NSS):
        sel[2 + b, 2, b * 8:(b + 1) * 8] = 1.0
    cst["sel"] = sel.reshape(18, 3 * 128)
    invc = np.zeros((128, 4, 15), np.float32)
    for g, w in enumerate((2, 4, 8, 16)):
        for t in range(15):
            invc[:, g, t] = 1.0 / min(w, t + 1)
    cst["invc"] = invc.reshape(128, 60)
    return cst


IN_SPECS = [
    ("xp", [NPS, SEQ, DM]), ("xs", [128, DM]), ("cT", [DM, 18]),
    ("ck", [NSS, WB, AW]), ("cv", [NSS, WB, AW]), ("spool", [NSS, PCTX, AW]),
    ("w_ada", [DM, 6 * DM]), ("b_adaT", [128, 48]), ("b_row", [1, 6 * DM]),
    ("g1T", [128, 8]), ("g2T", [128, 8]), ("w_in", [DM, 2048]), ("gq2", [128, 1]), ("gk_row", [1, HD]),
    ("rel_bias", [32, NH]), ("w_pool", [4, 128, 128]), ("pscT", [128, 4]),
    ("w_out", [DM, DM]), ("w_up", [DM, FF]), ("w_down", [FF, DM]),
    ("ident", [128, 128]), ("ohm", [32, WROW]), ("ohs", [32, 14 * 8 * 128]), ("sel", [18, 3 * 128]),
    ("invc", [128, 60]),
]
OUT_SPECS = [
    ("yp", [NPS, SEQ, DM]), ("ys", [128, DM]), ("kwp", [NPS, SEQ, AW]), ("vwp", [NPS, SEQ, AW]),
    ("pp", [NPS, PCTX, AW]), ("kws", [NSS, WB, AW]), ("vws", [NSS, WB, AW]), ("pso", [NSS, PCTX, AW]),
]


def build_program(stage="full"):
    nc = bass.Bass("TRN2", target_bir_lowering=False)
    k = K()
    k.nc = nc
    k.c = Ctx(nc)
    k._uid = 0

    def uid():
        k._uid += 1
        return k._uid

    k.uid = uid
    k.stage = stage
    c = k.c
    k.din = {}
    for name, shape in IN_SPECS:
        k.din[name] = nc.dram_tensor(name, shape, F32, kind="ExternalInput")
    k.dout = {}
    for name, shape in OUT_SPECS:
        k.dout[name] = T(nc.dram_tensor(name, shape, F32, kind="ExternalOutput"), c.buf("o_" + name))
    k.wscr = T(nc.dram_tensor("wscr", [128 * 8 * WROW], BF16, kind="Internal"), c.buf("wscr"))
    k.w_out_bf = T(nc.dram_tensor("w_out_bf", [DM, DM], BF16, kind="Internal"), c.buf("w_out_bf"))
    k.w_up_bf = T(nc.dram_tensor("w_up_bf", [DM, FF], BF16, kind="Internal"), c.buf("w_up_bf"))
    k.w_down_bf = T(nc.dram_tensor("w_down_bf", [FF, DM], BF16, kind="Internal"), c.buf("w_down_bf"))

    k.ps_t = [nc.alloc_psum_tensor(f"ps{i}", [128, 512], F32) for i in range(8)]
    k.ps_b = [c.buf(f"ps{i}") for i in range(8)]
    k.ps_i = 0

    k.pinned = set()

    def bank(pin=False):
        while k.ps_i in k.pinned:
            k.ps_i = (k.ps_i + 1) % 8
        i = k.ps_i
        k.ps_i = (i + 1) % 8
        if pin:
            k.pinned.add(i)
        t = T(k.ps_t[i], k.ps_b[i])
        t.idx = i
        return t

    def unpin(t):
        k.pinned.discard(t.idx)

    k.bank = bank
    k.unpin = unpin
    k.bg = c.buf("bg")

    with Scope(k) as P:
        k.P = P
        setup(k)
        if stage in ("full", "prompt", "p1", "p12"):
            for s in range(NPS):
                prompt_seq(k, s)
                if stage in ("p1", "p12"):
                    break
        if stage in ("full", "sample", "s1"):
            sample_group(k)
    c.emit()
    return nc


def setup(k):
    c, nc, P, din = k.c, k.nc, k.P, k.din
    full = k.stage in ("full", "sample")
    if full:
        for b in range(NSS):
            c.dma("act", k.dout["kws"][b, 0:WB - TSEQ, :], din["ck"][b, TSEQ:WB, :], reads=[k.bg])
            c.dma("act", k.dout["vws"][b, 0:WB - TSEQ, :], din["cv"][b, TSEQ:WB, :], reads=[k.bg])
        c.dma("act", k.dout["pso"][:, 0:PCTX - TSEQ, :], din["spool"][:, TSEQ:PCTX, :], reads=[k.bg])

    k.ident = P.sb("ident", [128, 128], F32)
    k.ones = P.sb("ones", [128, 128], F32)
    k.epsT = P.sb("epsT", [128, 1], F32)
    k.adaT = P.sb("adaT", [128, 5, 8, 18], F32)
    k.gsT = P.sb("gsT", [128, 2, 8, 18], F32)
    k.gate1 = P.sb("gate1", [128, 3, DM], F32)
    k.gq8 = P.sb("gq8", [128, 1], F32)
    k.gkb = P.sb("gkb", [128, HD], F32)
    k.pscT = P.sb("pscT", [128, 4], F32)
    k.invc = P.sb("invc", [128, 60], F32)
    k.WS = P.sb("WS", [128, 112, 8], F32)
    k.wpool = P.sb("wpool", [128, 4, 128], BF16)

    c.dma("sp", k.ident[:], din["ident"].ap(), writes=[k.ident.b])
    memset(k, "dve", k.ones[:], 1.0, [k.ones.b])
    memset(k, "dve", k.epsT[:], EPS, [k.epsT.b])
    c.dma("sp", k.pscT[:], din["pscT"].ap(), writes=[k.pscT.b])
    c.dma("sp", k.invc[:], din["invc"].ap(), writes=[k.invc.b])
    c.dma("sp", k.gkb[:], AP(din["gk_row"], 0, [[0, 128], [1, HD]]), writes=[k.gkb.b])
    c.dma("pool", k.wpool[:], din["w_pool"].ap().rearrange("g c d -> c g d"), writes=[k.wpool.b])

    if k.stage in ("full", "prompt", "sample"):
        for src, dst in ((din["w_out"], k.w_out_bf), (din["w_up"], k.w_up_bf), (din["w_down"], k.w_down_bf)):
            c.dma("pool", dst.t.ap().rearrange("(p a) n -> p (a n)", p=128),
                  src.ap().rearrange("(p a) n -> p (a n)", p=128), writes=[dst.b])

    with Scope(k) as Sx:
        cTs = Sx.sb("cTs", [128, 8, 18], F32)
        scT = Sx.sb("scT", [128, 8, 18], F32)
        badaT = Sx.sb("badaT", [128, 48], F32)
        brow = Sx.sb("brow", [1, 6 * DM], F32)
        g12 = Sx.sb("g12", [128, 2, 8], F32)
        gq2 = Sx.sb("gq2", [128, 1], F32)
        sel = Sx.sb("sel", [18, 3 * 128], F32)
        gtm = Sx.sb("gtm", [18, DM], F32)
        wa = [Sx.sb(f"wa{i}", [128, 8, 1024], F32) for i in range(2)]
        tmp18 = Sx.sb("tmp18", [128, 8, 18], F32)

        c.dma("sp", cTs[:], din["cT"].ap().rearrange("(k p) n -> p k n", p=128), writes=[cTs.b])
        c.dma("sp", badaT[:], din["b_adaT"].ap(), writes=[badaT.b])
        c.dma("sp", brow[:], din["b_row"].ap(), writes=[brow.b])
        c.dma("sp", g12[:, 0, :], din["g1T"].ap(), writes=[g12.b])
        c.dma("sp", g12[:, 1, :], din["g2T"].ap(), writes=[g12.b])
        c.dma("sp", gq2[:], din["gq2"].ap(), writes=[gq2.b])
        c.dma("sp", sel[:], din["sel"].ap(), writes=[sel.b])

        ts(k, "dve", k.gq8[:], gq2[:], HD ** -0.5, None, ALU.mult, None, [gq2.b], [k.gq8.b])
        act(k, scT[:], cTs[:], AF.Silu, [cTs.b], [scT.b])

        kind_of_blk = {0: 0, 1: 1, 3: 2, 4: 3, 5: 4}
        for blk in range(6):
            w = wa[blk % 2]
            c.dma("sp", w[:], din["w_ada"].ap()[:, blk * 1024:(blk + 1) * 1024].rearrange("(k p) n -> p k n", p=128),
                  writes=[w.b])
            if blk == 2:
                for half in range(2):
                    bk = k.bank()
                    lst = [(bk[0:18, :], scT[:, kc, :], w[:, kc, half * 512:(half + 1) * 512], kc == 0, False)
                           for kc in range(8)]
                    col0 = 2 * 1024 + half * 512
                    lst.append((bk[0:18, :], k.ones[0:1, 0:18], brow[0:1, col0:col0 + 512], False, True))
                    mms(k, lst, [scT.b, w.b, k.ones.b, brow.b], [bk.b])
                    cp(k, "dve", gtm[0:18, half * 512:(half + 1) * 512], bk[0:18, :], [bk.b], [gtm.b])
            else:
                kind = kind_of_blk[blk]
                bk = k.bank()
                lst = []
                for fcl in range(8):
                    for kc in range(8):
                        lst.append((bk[:, fcl * 18:(fcl + 1) * 18], w[:, kc, fcl * 128:(fcl + 1) * 128], scT[:, kc, :],
                                    kc == 0, kc == 7))
                mms(k, lst, [scT.b, w.b], [bk.b])
                tt(k, "dve", k.adaT[:, kind, :, :], bk[:, 0:144].rearrange("p (a b) -> p a b", b=18),
                   AP(badaT, blk * 8, [[48, 128], [1, 8], [0, 18]]), ALU.add, [bk.b, badaT.b], [k.adaT.b])
        for j, kind in enumerate((1, 3)):
            ts(k, "dve", tmp18[:], k.adaT[:, kind, :, :], 1.0, None, ALU.add, None, [k.adaT.b], [tmp18.b])
            tt(k, "dve", k.gsT[:, j, :, :], tmp18[:], AP(g12, j * 8, [[16, 128], [1, 8], [0, 18]]), ALU.mult,
               [tmp18.b, g12.b], [k.gsT.b])
        for G in range(3):
            for half in range(2):
                bk = k.bank()
                mms(k, [(bk[:, :], sel[0:18, G * 128:(G + 1) * 128], gtm[0:18, half * 512:(half + 1) * 512], True, True)],
                    [sel.b, gtm.b], [bk.b])
                cp(k, "act", k.gate1[:, G, half * 512:(half + 1) * 512], bk[:, :], [bk.b], [k.gate1.b])

    with Scope(k) as Sx:
        rb = Sx.sb("rb", [32, NH], F32)
        expb = Sx.sb("expb", [32, NH], F32)
        expr = Sx.sb("expr", [32, NH, 128], F32)
        ohm = Sx.sb("ohm", [32, WROW], F32)
        ohs = Sx.sb("ohs", [32, 112 * 128], F32)
        wrep = Sx.sb("wrep", [128, NH, WROW], BF16)
        c.dma("sp", rb[:], din["rel_bias"].ap(), writes=[rb.b])
        c.dma("sp", ohm[:], din["ohm"].ap(), writes=[ohm.b])
        c.dma("sp", ohs[:], din["ohs"].ap(), writes=[ohs.b])
        act(k, expb[:], rb[:], AF.Exp, [rb.b], [expb.b])
        cp(k, "dve", expr[:], AP(expb, 0, [[NH, 32], [1, NH], [0, 128]]), [expb.b], [expr.b])
        if k.stage != "s1":
            n = 0
            for h in range(NH):
                for blk in range(WROW // 512):
                    bk = k.bank()
                    mms(k, [(bk[:, :], expr[:, h, :], ohm[:, blk * 512:(blk + 1) * 512], True, True)],
                        [expr.b, ohm.b], [bk.b])
                    cp(k, "act" if n % 2 == 0 else "dve", wrep[:, h, blk * 512:(blk + 1) * 512], bk[:, :], [bk.b], [wrep.b])
                    n += 1
            c.dma("sp", k.wscr.t.ap().rearrange("(p f) -> p f", p=128), wrep[:].rearrange("p h w -> p (h w)"),
                  reads=[wrep.b], writes=[k.wscr.b])
        for half in range(2):
            bk = k.bank()
            lst = []
            for i in range(56):
                ii = half * 56 + i
                lst.append((bk[:, i * 8:(i + 1) * 8], ohs[:, ii * 128:(ii + 1) * 128], expb[:, :], True, True))
            mms(k, lst, [ohs.b, expb.b], [bk.b])
            cp(k, "dve", k.WS[:, half * 56:(half + 1) * 56, :], bk[:, 0:448].rearrange("p (a b) -> p a b", b=8),
               [bk.b], [k.WS.b])


def rms_stats(k, x, junk, st, col):
    act(k, junk[:], x[:], AF.Square, [x.b], [junk.b, st.b], accum_out=st[:, col:col + 1])
    act(k, st[:, col + 1:col + 2], st[:, col:col + 1], AF.Sqrt, [st.b, k.epsT.b], [st.b], bias=k.epsT[:], scale=1.0 / DM)
    recip(k, st[:, col + 2:col + 3], st[:, col + 1:col + 2], [st.b], [st.b])


def head_norm(k, bk, sq, rs, out, extra_gain=None):
    act(k, sq[:], bk[:, :], AF.Square, [bk.b], [sq.b])
    reduce_add(k, rs[:, 0:8], sq[:].rearrange("p (h e) -> p h e", e=HD), [sq.b], [rs.b])
    act(k, rs[:, 8:16], rs[:, 0:8], AF.Sqrt, [rs.b, k.epsT.b], [rs.b], bias=k.epsT[:], scale=1.0 / HD)
    recip(k, rs[:, 16:24], rs[:, 8:16], [rs.b], [rs.b])
    tt(k, "dve", out[:].rearrange("p (h e) -> p h e", e=HD), bk[:, :].rearrange("p (h e) -> p h e", e=HD),
       AP(rs, 16, [[24, 128], [1, 8], [0, HD]]), ALU.mult, [bk.b, rs.b], [out.b])
    if extra_gain is not None:
        tt(k, "dve", out[:].rearrange("p (h e) -> p h e", e=HD), out[:].rearrange("p (h e) -> p h e", e=HD),
           AP(extra_gain, 0, [[HD, 128], [0, 8], [1, HD]]), ALU.mult, [out.b, extra_gain.b], [out.b])


def evac_hT(k, bk, hT, cc, ncols, kind_scale, kind_shift, col, sample, tmp=None):
    if not sample:
        act(k, hT[:, cc, 0:ncols], bk[:, 0:ncols], AF.Identity, [bk.b, k.gsT.b, k.adaT.b], [hT.b],
            scale=k.gsT[:, kind_scale, cc, col:col + 1], bias=k.adaT[:, kind_shift, cc, col:col + 1])
    else:
        gs_off = ((kind_scale * 8 + cc) * 18 + 2)
        sh_off = ((kind_shift * 8 + cc) * 18 + 2)
        tt(k, "dve", tmp[:, 0:128].rearrange("p (b t) -> p b t", t=TSEQ), bk[:, 0:128].rearrange("p (b t) -> p b t", t=TSEQ),
           AP(k.gsT, gs_off, [[2 * 8 * 18, 128], [1, NSS], [0, TSEQ]]), ALU.mult, [bk.b, k.gsT.b], [tmp.b])
        tt(k, "dve", hT[:, cc, 0:128].rearrange("p (b t) -> p b t", t=TSEQ), tmp[:, 0:128].rearrange("p (b t) -> p b t", t=TSEQ),
           AP(k.adaT, sh_off, [[5 * 8 * 18, 128], [1, NSS], [0, TSEQ]]), ALU.add, [tmp.b, k.adaT.b], [hT.b])


def mixer_in_group(k, S, g, ntiles, xsrc, col, sample, R):
    c = k.c
    W = S["w_in_bf"]
    hT = S["hT"][g % 2]
    ncols = ntiles * 128
    xts = []
    for j in range(ntiles):
        T0 = g * 4 + j
        xt = S["xt"][T0 % 4]
        c.dma("sp", xt[:], xsrc(T0), writes=[xt.b])
        st = S["st"][T0 % 4]
        rms_stats(k, xt, S["junk"], st, 0)
        ts(k, "dve", xt[:], xt[:], st[:, 2:3], None, ALU.mult, None, [xt.b, st.b], [xt.b])
        xts.append(xt)
    for cc in range(8):
        bk = k.bank()
        trs(k, [(bk[:, j * 128:(j + 1) * 128], xts[j][:, cc * 128:(cc + 1) * 128]) for j in range(ntiles)], k.ident,
            [x.b for x in xts] + [k.ident.b], [bk.b])
        evac_hT(k, bk, hT, cc, ncols, 0, 0, col, sample, tmp=S["tmpf"])
    for j in range(ntiles):
        T0 = g * 4 + j
        tok = slice(j * 128, (j + 1) * 128)
        bq, bkk, bv = k.bank(), k.bank(), k.bank()
        lst = []
        for kc in range(8):
            lst.append((bq[:, :], hT[:, kc, tok], W[:, kc, 0:512], kc == 0, kc == 7))
            lst.append((bkk[:, :], hT[:, kc, tok], W[:, kc, 512:1024], kc == 0, kc == 7))
        mms(k, lst, [hT.b, W.b], [bq.b, bkk.b])
        mms(k, [(bv[:, :], hT[:, kc, tok], W[:, kc, 1024:1536], kc == 0, kc == 7) for kc in range(8)], [hT.b, W.b], [bv.b])
        qn = S["qn"][T0 % 2]
        head_norm(k, bq, S["sq"], S["rs"][0], qn)
        bt = k.bank()
        trs(k, [(bt[:, p * 128:(p + 1) * 128], qn[:, p * 128:(p + 1) * 128]) for p in range(4)], k.ident,
            [qn.b, k.ident.b], [bt.b])
        act(k, R["QT"][:, :, T0 * 128:(T0 + 1) * 128], bt[:, :].rearrange("p (a t) -> p a t", t=128), AF.Copy,
            [bt.b, k.gq8.b], [R["QT"].b], scale=k.gq8[:, 0:1])
        kn = S["kn"][T0 % 2]
        head_norm(k, bkk, S["sq2"], S["rs"][1], kn, extra_gain=k.gkb)
        R["store_k"](T0, kn)
        bt2 = k.bank()
        trs(k, [(bt2[:, p * 128:(p + 1) * 128], kn[:, p * 128:(p + 1) * 128]) for p in range(4)], k.ident,
            [kn.b, k.ident.b], [bt2.b])
        cp(k, "act", R["KT"][:, :, T0 * 128:(T0 + 1) * 128], bt2[:, :].rearrange("p (a t) -> p a t", t=128),
           [bt2.b], [R["KT"].b])
        vo = S["vo"][T0 % 2]
        cp(k, "act", vo[:], bv[:, :], [bv.b], [vo.b])
        R["store_v"](T0, vo)
    uT = S["uT"][g % 2]
    for uc in range(4):
        bk = k.bank()
        mms(k, [(bk[:, 0:ncols], W[:, kc, 1536 + uc * 128:1536 + (uc + 1) * 128], hT[:, kc, 0:ncols], kc == 0, kc == 7)
                for kc in range(8)], [hT.b, W.b], [bk.b])
        R["store_uT"](g, uc, bk, uT)
    return hT


def pool_adds(k, S, uT, ncols, lead, done):
    pa, pb, s4 = S["pa"], S["pb"], S["s4"]
    n = lead + ncols
    u = uT
    for uc in range(4):
        if uc == 0:
            tt(k, "pool", s4[:, 0:ncols], u[:, 0, lead:n], u[:, 0, lead - 1:n - 1], ALU.add, [u.b], [s4.b])
            done(uc)
            continue
        tt(k, "pool", pa[:, 1:n], u[:, uc, 1:n], u[:, uc, 0:n - 1], ALU.add, [u.b], [pa.b])
        if uc == 1:
            tt(k, "pool", s4[:, 0:ncols], pa[:, lead:n], pa[:, lead - 2:n - 2], ALU.add, [pa.b], [s4.b])
            done(uc)
            continue
        tt(k, "pool", pb[:, 3:n], pa[:, 3:n], pa[:, 1:n - 2], ALU.add, [pa.b], [pb.b])
        if uc == 2:
            tt(k, "pool", s4[:, 0:ncols], pb[:, lead:n], pb[:, lead - 4:n - 4], ALU.add, [pb.b], [s4.b])
            done(uc)
            continue
        tt(k, "pool", pa[:, 7:n], pb[:, 7:n], pb[:, 3:n - 4], ALU.add, [pb.b], [pa.b])
        tt(k, "pool", s4[:, 0:ncols], pa[:, lead:n], pa[:, lead - 8:n - 8], ALU.add, [pa.b], [s4.b])
        done(uc)


def prompt_seq(k, s):
    c, nc, din = k.c, k.nc, k.din
    with Scope(k) as Q0:
      R = {}
      R["poolT"] = Q0.sb("poolT", [128, 4, SEQ], BF16)
      R["attnT"] = Q0.sb("attnT", [128, 4, SEQ], BF16)
      with Scope(k) as Q:
        R["QT"] = Q.sb("QT", [128, 4, SEQ], BF16)
        R["KT"] = Q.sb("KT", [128, 4, SEQ], BF16)
        R["Vx"] = Q.sb("Vx", [128, 16, 4, 192], BF16)
        Vx = R["Vx"]
        memset(k, "pool", Vx[:], 0.0, [Vx.b])
        memset(k, "pool", Vx[:, :, :, 64:65], 1.0, [Vx.b])

        with Scope(k) as S1:
            S = {}
            S["w_in_bf"] = S1.sb("w_in_bf", [128, 8, 2048], BF16)
            c.dma("pool", S["w_in_bf"][:], din["w_in"].ap().rearrange("(k p) n -> p k n", p=128), writes=[S["w_in_bf"].b])
            S["xt"] = [S1.sb(f"xt{i}", [128, DM], F32) for i in range(4)]
            S["st"] = [S1.sb(f"st{i}", [128, 4], F32) for i in range(4)]
            S["junk"] = S1.sb("junk", [128, DM], BF16)
            S["hT"] = [S1.sb(f"hT{i}", [128, 8, 512], BF16) for i in range(1)] * 2
            S["tmpf"] = None
            S["sq"] = S1.sb("sq", [128, 512], F32)
            S["sq2"] = S1.sb("sq2", [128, 512], F32)
            S["rs"] = [S1.sb(f"rs{i}", [128, 24], F32) for i in range(2)]
            S["qn"] = [S1.sb(f"qn{i}", [128, 512], F32) for i in range(1)] * 2
            S["kn"] = [S1.sb(f"kn{i}", [128, 512], F32) for i in range(2)]
            S["vo"] = [S1.sb(f"vo{i}", [128, 512], F32) for i in range(2)]
            S["uT"] = [S1.sb(f"uT{i}", [128, 4, 16 + 512], F32) for i in range(1)] * 2
            carry = S1.sb("carry", [128, 4, 16], F32)
            S["pa"] = S1.sb("pa", [128, 528], F32)
            S["pb"] = S1.sb("pb", [128, 528], F32)
            S["s4"] = S1.sb("s4", [128, 512], F32)
            S["pooled"] = [S1.sb(f"pooled{i}", [128, 512], BF16) for i in range(2)]
            S["t15"] = S1.sb("t15", [128, 15], F32)
            utm = S["qn"][0]
            memset(k, "pool", carry[:], 0.0, [carry.b])

            def store_k(T0, kn):
                c.dma("sp", k.dout["kwp"][s, T0 * 128:(T0 + 1) * 128, :], kn[:], reads=[kn.b])

            def store_v(T0, vo):
                c.dma("sp", k.dout["vwp"][s, T0 * 128:(T0 + 1) * 128, :], vo[:], reads=[vo.b])
                tt_src = vo[:].rearrange("p (a q e) -> p a q e", q=2, e=HD)
                cp(k, "pool", Vx[:, T0, :, 0:64], tt_src[:, :, 0, :], [vo.b], [Vx.b])
                cp(k, "pool", Vx[:, T0, :, 128:192], tt_src[:, :, 1, :], [vo.b], [Vx.b])

            def store_uT(g, uc, bk, uT):
                cp(k, "act" if uc % 2 == 0 else "dve", uT[:, uc, 16:528], bk[:, :], [bk.b], [uT.b])

            R["store_k"], R["store_v"], R["store_uT"] = store_k, store_v, store_uT

            for g in range(4):
                uT = S["uT"][0]
                cp(k, "pool", uT[:, :, 0:16], carry[:], [carry.b], [uT.b])
                hT = mixer_in_group(k, S, g, 4, lambda T0: din["xp"][s, T0 * 128:(T0 + 1) * 128, :], s, False, R)
                cp(k, "pool", carry[:], uT[:, :, 512:528], [uT.b], [carry.b])

                def done(uc, g=g, uT=uT):
                    w = (2, 4, 8, 16)[uc]
                    s4 = S["s4"]
                    pooled = S["pooled"][uc % 2]
                    stt(k, pooled[:], s4[:], 1.0 / w, uT[:, uc, 16:528], ALU.mult, ALU.subtract, [s4.b, uT.b], [pooled.b])
                    if g == 0:
                        t15 = S["t15"]
                        tt(k, "dve", t15[:], s4[:, 0:15], k.invc[:, uc * 15:(uc + 1) * 15], ALU.mult,
                           [s4.b, k.invc.b], [t15.b])
                        tt(k, "dve", pooled[:, 0:15], t15[:], uT[:, uc, 16:31], ALU.subtract, [t15.b, uT.b], [pooled.b])
                    bk = k.bank()
                    mms(k, [(bk[:, :], k.wpool[:, uc, :], pooled[:], True, True)], [k.wpool.b, pooled.b], [bk.b])
                    act(k, R["poolT"][:, uc, g * 512:(g + 1) * 512], bk[:, :], AF.Copy, [bk.b, k.pscT.b], [R["poolT"].b],
                        scale=k.pscT[:, uc:uc + 1])

                pool_adds(k, S, uT, 512, 16, done)
                if g == 3:
                    bk = k.bank()
                    W = S["w_in_bf"]
                    mms(k, [(bk[:, :], hT[:, kc, 384:512], W[:, kc, 1536:2048], kc == 0, kc == 7) for kc in range(8)],
                        [hT.b, W.b], [bk.b])
                    cp(k, "dve", utm[:], bk[:, :], [bk.b], [utm.b])
                    c.dma("sp", k.dout["pp"][s, :, :], utm[128 - PCTX:128, :], reads=[utm.b])
        if k.stage == "p1":
            return

        with Scope(k) as S2:
            G = S2.sb("G", [128, NH, GW], BF16)
            for h in range(NH):
                c.dma("sp", G[:, h, :], AP(k.wscr, 127 + h * WROW, [[8 * WROW - 1, 128], [1, GW]]),
                      reads=[k.wscr.b], writes=[G.b])
            pE = [S2.sb(f"pE{i}", [128, 512], BF16) for i in range(3)]
            pT = [S2.sb(f"pT{i}", [128, 512], BF16) for i in range(3)]
            rden = S2.sb("rden", [128, 512], F32)
            osb = S2.sb("osb", [128, 512], F32)
            QT, KT, attnT = R["QT"], R["KT"], R["attnT"]
            n = 0
            for pair in range(4):
                for par in range(2):
                    h = 2 * pair + par
                    prt = slice(par * 64, par * 64 + 64)
                    for qb in range(4):
                        bo = k.bank(pin=True)
                        nkt = 4 * qb + 4
                        for kt in range(nkt):
                            i = kt - 4 * qb
                            q0 = max(0, i) * 128
                            xoff = 512 * qb - 128 * kt + 384
                            bs = k.bank()
                            mms(k, [(bs[:, q0:512], KT[prt, pair, kt * 128:(kt + 1) * 128],
                                     QT[prt, pair, qb * 512 + q0:(qb + 1) * 512], True, True)], [KT.b, QT.b], [bs.b])
                            e = pE[n % 3]
                            p = pT[n % 3]
                            n += 1
                            act(k, e[:, q0:512], bs[:, q0:512], AF.Exp, [bs.b], [e.b])
                            tt(k, "dve", p[:, q0:512], e[:, q0:512], G[:, h, xoff + q0:xoff + 512], ALU.mult,
                               [e.b, G.b], [p.b])
                            if par == 0:
                                mms(k, [(bo[0:65, q0:512], Vx[:, kt, pair, 0:65], p[:, q0:512], kt == 0, kt == nkt - 1)],
                                    [Vx.b, p.b], [bo.b])
                            else:
                                mms(k, [(bo[:, q0:512], Vx[:, kt, pair, 64:192], p[:, q0:512], kt == 0, kt == nkt - 1)],
                                    [Vx.b, p.b], [bo.b])
                        dr = 64 if par == 0 else 0
                        recip(k, rden[dr:dr + 1, :], bo[dr:dr + 1, :], [bo.b], [rden.b])
                        bb = k.bank()
                        mms(k, [(bb[:, :], k.ones[dr:dr + 1, :], rden[dr:dr + 1, :], True, True)], [k.ones.b, rden.b], [bb.b])
                        cp(k, "act", osb[prt, :], bo[prt, :], [bo.b], [osb.b])
                        tt(k, "dve", attnT[prt, pair, qb * 512:(qb + 1) * 512], osb[prt, :], bb[prt, :], ALU.mult,
                           [osb.b, bb.b], [attnT.b])
                        k.unpin(bo)
        if k.stage == "p12":
            return
      if k.stage in ("p1", "p12"):
          return

      with Scope(k) as S3:
          S = ffn_alloc(k, S3)
          for g in range(4):
              tail_group(k, S, g, 4, lambda T0: din["xp"][s, T0 * 128:(T0 + 1) * 128, :],
                         lambda T0: k.dout["yp"][s, T0 * 128:(T0 + 1) * 128, :], s, False,
                         lambda cc, T0: (R["attnT"] if cc < 4 else R["poolT"]), R, s)


def ffn_alloc(k, S3):
    c = k.c
    S = {}
    S["w_out_bf"] = S3.sb("w_out_sb", [128, 8, DM], BF16)
    c.dma("sp", S["w_out_bf"][:], k.w_out_bf.t.ap().rearrange("(k p) n -> p k n", p=128), reads=[k.w_out_bf.b],
          writes=[S["w_out_bf"].b])
    S["wu"] = [S3.sb(f"wu{i}", [128, 8, 512], BF16) for i in range(2)]
    S["wd"] = [S3.sb(f"wd{i}", [128, 4, 512], BF16) for i in range(2)]
    S["xt"] = [S3.sb(f"xt3_{i}", [128, DM], F32) for i in range(2)]
    S["x1"] = [S3.sb(f"x1_{i}", [128, DM], F32) for i in range(4)]
    S["xn"] = [S3.sb(f"xn2_{i}", [128, DM], F32) for i in range(4)]
    S["st"] = [S3.sb(f"st3_{i}", [128, 4], F32) for i in range(4)]
    S["junk"] = S3.sb("junk3", [128, DM], BF16)
    S["hT2"] = S3.sb("hT2", [128, 8, 512], BF16)
    S["fT"] = S3.sb("fT", [128, 32, 512], BF16)
    S["sqf"] = [S3.sb(f"sqf{i}", [128, 512], F32) for i in range(2)]
    S["yT"] = [S3.sb(f"yT{i}", [128, 512], F32) for i in range(4)]
    S["tmpf"] = S3.sb("tmpf3", [128, 512], F32)
    return S


def tail_group(k, S, g, ntiles, xsrc, ydst, col, sample, mixsrc, R, G):
    c = k.c
    ncols = ntiles * 128
    Wo = S["w_out_bf"]
    x1s, xns = [], []
    for j in range(ntiles):
        T0 = g * 4 + j
        tok = slice(T0 * 128, (T0 + 1) * 128)
        xt = S["xt"][T0 % 2]
        c.dma("sp", xt[:], xsrc(T0), writes=[xt.b])
        b0, b1 = k.bank(), k.bank()
        lst = []
        rd = [Wo.b]
        for cc in range(8):
            m = mixsrc(cc, T0)
            if m.b not in rd:
                rd.append(m.b)
            lst.append((b0[:, :], m[:, cc % 4, tok], Wo[:, cc, 0:512], cc == 0, cc == 7))
            lst.append((b1[:, :], m[:, cc % 4, tok], Wo[:, cc, 512:1024], cc == 0, cc == 7))
        mms(k, lst, rd, [b0.b, b1.b])
        x1 = S["x1"][j]
        xn = S["xn"][j]
        for half, bb in enumerate((b0, b1)):
            hs = slice(half * 512, (half + 1) * 512)
            tt(k, "dve", xn[:, hs], bb[:, :], k.gate1[:, G, hs], ALU.mult, [bb.b, k.gate1.b], [xn.b])
        tt(k, "pool", x1[:], xn[:], xt[:], ALU.add, [xn.b, xt.b], [x1.b])
        st = S["st"][j]
        rms_stats(k, x1, S["junk"], st, 0)
        ts(k, "dve", xn[:], x1[:], st[:, 2:3], None, ALU.mult, None, [x1.b, st.b], [xn.b])
        x1s.append(x1)
        xns.append(xn)
    hT2 = S["hT2"]
    for cc in range(8):
        bk = k.bank()
        trs(k, [(bk[:, j * 128:(j + 1) * 128], xns[j][:, cc * 128:(cc + 1) * 128]) for j in range(ntiles)], k.ident,
            [x.b for x in xns] + [k.ident.b], [bk.b])
        evac_hT(k, bk, hT2, cc, ncols, 1, 2, col, sample, tmp=S["tmpf"])
    fT = S["fT"]
    for ffg in range(8):
        wu = S["wu"][ffg % 2]
        c.dma("sp", wu[:], k.w_up_bf.t.ap()[:, ffg * 512:(ffg + 1) * 512].rearrange("(k p) n -> p k n", p=128),
              reads=[k.w_up_bf.b], writes=[wu.b])
        for fj in range(4):
            fc = ffg * 4 + fj
            bk = k.bank()
            mms(k, [(bk[:, 0:ncols], wu[:, kc, fj * 128:(fj + 1) * 128], hT2[:, kc, 0:ncols], kc == 0, kc == 7)
                    for kc in range(8)], [wu.b, hT2.b], [bk.b])
            sq = S["sqf"][fc % 2]
            act(k, sq[:, 0:ncols], bk[:, 0:ncols], AF.Square, [bk.b], [sq.b])
            stt(k, fT[:, fc, 0:ncols], bk[:, 0:ncols], 0.0, sq[:, 0:ncols], ALU.is_gt, ALU.mult, [bk.b, sq.b], [fT.b])
    for half in range(2):
        accs = [k.bank(pin=True) for _ in range(4)]
        for ffg in range(8):
            wd = S["wd"][(half * 8 + ffg) % 2]
            c.dma("sp", wd[:], k.w_down_bf.t.ap()[ffg * 512:(ffg + 1) * 512, half * 512:(half + 1) * 512]
                  .rearrange("(j p) n -> p j n", p=128), reads=[k.w_down_bf.b], writes=[wd.b])
            lst = []
            for dc in range(4):
                for fj in range(4):
                    fc = ffg * 4 + fj
                    lst.append((accs[dc][:, 0:ncols], wd[:, fj, dc * 128:(dc + 1) * 128], fT[:, fc, 0:ncols],
                                ffg == 0 and fj == 0, ffg == 7 and fj == 3))
            mms(k, lst, [wd.b, fT.b], [a.b for a in accs])
        for dc in range(4):
            dcg = half * 4 + dc
            yT = S["yT"][dc]
            if not sample:
                act(k, yT[:, 0:ncols], accs[dc][:, 0:ncols], AF.Copy, [accs[dc].b, k.adaT.b], [yT.b],
                    scale=k.adaT[:, 4, dcg, col:col + 1])
            else:
                off = ((4 * 8 + dcg) * 18 + 2)
                tt(k, "dve", yT[:, 0:128].rearrange("p (b t) -> p b t", t=TSEQ),
                   accs[dc][:, 0:128].rearrange("p (b t) -> p b t", t=TSEQ),
                   AP(k.adaT, off, [[5 * 8 * 18, 128], [1, NSS], [0, TSEQ]]), ALU.mult, [accs[dc].b, k.adaT.b], [yT.b])
        for a in accs:
            k.unpin(a)
        for j in range(ntiles):
            bk = k.bank()
            trs(k, [(bk[:, dc * 128:(dc + 1) * 128], S["yT"][dc][:, j * 128:(j + 1) * 128]) for dc in range(4)], k.ident,
                [S["yT"][dc].b for dc in range(4)] + [k.ident.b], [bk.b])
            hs = slice(half * 512, (half + 1) * 512)
            tt(k, "dve", x1s[j][:, hs], bk[:, :], x1s[j][:, hs], ALU.add, [bk.b, x1s[j].b], [x1s[j].b])
            if half == 1:
                c.dma("sp", ydst(g * 4 + j), x1s[j][:], reads=[x1s[j].b])


def sample_group(k):
    c, nc, din = k.c, k.nc, k.din
    with Scope(k) as Q:
        R = {}
        R["QT"] = Q.sb("QTs", [128, 4, 128], BF16)
        R["KT"] = Q.sb("KTs", [128, 4, 128], BF16)
        R["poolT"] = Q.sb("poolTs", [128, 4, 128], BF16)
        R["attnT"] = Q.sb("attnTs", [128, 4, 128], BF16)
        uTs = Q.sb("uTs", [128, 4, NSS, 24], F32)
        vnew = Q.sb("vnew", [8, NSS, AW], BF16)
        utm = Q.sb("utm_s", [128, 512], F32)

        with Scope(k) as S1:
            S = {}
            S["w_in_bf"] = S1.sb("w_in_bf_s", [128, 8, 2048], BF16)
            c.dma("pool", S["w_in_bf"][:], din["w_in"].ap().rearrange("(k p) n -> p k n", p=128), writes=[S["w_in_bf"].b])
            S["xt"] = [S1.sb(f"xts{i}", [128, DM], F32) for i in range(4)]
            S["st"] = [S1.sb(f"sts{i}", [128, 4], F32) for i in range(4)]
            S["junk"] = S1.sb("junks", [128, DM], BF16)
            S["hT"] = [S1.sb(f"hTs{i}", [128, 8, 512], BF16) for i in range(1)] * 2
            S["tmpf"] = S1.sb("tmpfs", [128, 512], F32)
            S["sq"] = S1.sb("sqs", [128, 512], F32)
            S["sq2"] = S1.sb("sq2s", [128, 512], F32)
            S["rs"] = [S1.sb(f"rss{i}", [128, 24], F32) for i in range(2)]
            S["qn"] = [S1.sb(f"qns{i}", [128, 512], F32) for i in range(2)]
            S["kn"] = [S1.sb(f"kns{i}", [128, 512], F32) for i in range(2)]
            S["vo"] = [S1.sb(f"vos{i}", [128, 512], F32) for i in range(2)]
            S["uT"] = [None, None]
            spt = [S1.sb(f"spt{i}", [120, 512], F32) for i in range(2)]

            memset(k, "pool", uTs[:], 0.0, [uTs.b])
            for i in range(2):
                c.dma("sp", spt[i][:], din["spool"].ap()[i * 8:(i + 1) * 8, :, :].rearrange("b r c -> (b r) c"),
                      writes=[spt[i].b])
                bk = k.bank()
                trs(k, [(bk[:, uc * 120:(uc + 1) * 120], spt[i][:, uc * 128:(uc + 1) * 128]) for uc in range(4)], k.ident,
                    [spt[i].b, k.ident.b], [bk.b])
                cp(k, "dve", uTs[:, :, i * 8:(i + 1) * 8, 1:16],
                   bk[:, 0:480].rearrange("p (u b r) -> p u b r", u=4, b=8), [bk.b], [uTs.b])

            def store_k(T0, kn):
                for b in range(NSS):
                    c.dma("sp", k.dout["kws"][b, WB - TSEQ:WB, :], kn[b * 8:(b + 1) * 8, :], reads=[kn.b])

            def store_v(T0, vo):
                for b in range(NSS):
                    c.dma("sp", k.dout["vws"][b, WB - TSEQ:WB, :], vo[b * 8:(b + 1) * 8, :], reads=[vo.b],
                          writes=[k.dout["vws"].b])
                c.dma("pool", vnew[:], k.dout["vws"].t.ap()[:, WB - TSEQ:WB, :].rearrange("b t c -> t b c"),
                      reads=[k.dout["vws"].b], writes=[vnew.b])

            def store_uT(g, uc, bk, uT):
                cp(k, "act" if uc % 2 == 0 else "dve", uTs[:, uc, :, 16:24],
                   bk[:, 0:128].rearrange("p (b t) -> p b t", t=TSEQ), [bk.b], [uTs.b])

            R["store_k"], R["store_v"], R["store_uT"] = store_k, store_v, store_uT
            hT = mixer_in_group(k, S, 0, 1, lambda T0: din["xs"].ap(), 0, True, R)
            bk = k.bank()
            W = S["w_in_bf"]
            mms(k, [(bk[:, :], hT[:, kc, 0:128], W[:, kc, 1536:2048], kc == 0, kc == 7) for kc in range(8)],
                [hT.b, W.b], [bk.b])
            cp(k, "dve", utm[:], bk[:, :], [bk.b], [utm.b])
            for b in range(NSS):
                c.dma("sp", k.dout["pso"][b, PCTX - TSEQ:PCTX, :], utm[b * 8:(b + 1) * 8, :], reads=[utm.b])

            pa = S1.sb("pas", [128, NSS, 24], F32)
            pb = S1.sb("pbs", [128, NSS, 24], F32)
            s4 = S1.sb("s4s", [128, 4, NSS, 8], F32)
            pooled = S1.sb("pooleds", [128, 4, NSS * 8], BF16)
            n = 24
            u = uTs
            tt(k, "pool", s4[:, 0, :, :], u[:, 0, :, 16:n], u[:, 0, :, 15:n - 1], ALU.add, [u.b], [s4.b])
            for uc in (1, 2, 3):
                tt(k, "pool", pa[:, :, 1:n], u[:, uc, :, 1:n], u[:, uc, :, 0:n - 1], ALU.add, [u.b], [pa.b])
                if uc == 1:
                    tt(k, "pool", s4[:, 1, :, :], pa[:, :, 16:n], pa[:, :, 14:n - 2], ALU.add, [pa.b], [s4.b])
                    continue
                tt(k, "pool", pb[:, :, 3:n], pa[:, :, 3:n], pa[:, :, 1:n - 2], ALU.add, [pa.b], [pb.b])
                if uc == 2:
                    tt(k, "pool", s4[:, 2, :, :], pb[:, :, 16:n], pb[:, :, 12:n - 4], ALU.add, [pb.b], [s4.b])
                    continue
                tt(k, "pool", pa[:, :, 7:n], pb[:, :, 7:n], pb[:, :, 3:n - 4], ALU.add, [pb.b], [pa.b])
                tt(k, "pool", s4[:, 3, :, :], pa[:, :, 16:n], pa[:, :, 8:n - 8], ALU.add, [pa.b], [s4.b])
            for uc, w in enumerate((2, 4, 8, 16)):
                stt(k, pooled[:, uc, :].rearrange("p (b t) -> p b t", t=TSEQ), s4[:, uc, :, :], 1.0 / w, u[:, uc, :, 16:24],
                    ALU.mult, ALU.subtract, [s4.b, u.b], [pooled.b])
            for uc in range(4):
                bk = k.bank()
                mms(k, [(bk[:, 0:128], k.wpool[:, uc, :], pooled[:, uc, :], True, True)], [k.wpool.b, pooled.b], [bk.b])
                act(k, R["poolT"][:, uc, :], bk[:, 0:128], AF.Copy, [bk.b, k.pscT.b], [R["poolT"].b],
                    scale=k.pscT[:, uc:uc + 1])
        if k.stage == "s1":
            return

        with Scope(k) as S2:
            sample_attention(k, S2, R, vnew)

        with Scope(k) as S3:
            S = ffn_alloc(k, S3)
            tail_group(k, S, 0, 1, lambda T0: din["xs"].ap(), lambda T0: k.dout["ys"].t.ap(), 0, True,
                       lambda cc, T0: (R["attnT"] if cc < 4 else R["poolT"]), R, 2)


def sample_attention(k, S2, R, vnew):
    c, din = k.c, k.din
    QT, KT, attnT = R["QT"], R["KT"], R["attnT"]
    qbd = S2.sb("qbd", [128, 4, NSS, 16], BF16)
    parm = S2.sb("parm", [128, 2], F32)
    bmask = S2.sb("bmask", [64, NH], F32)
    ones_bf = S2.sb("ones_bf", [128, 1], BF16)
    memset(k, "dve", parm[:], 0.0, [parm.b])
    memset(k, "dve", parm[0:64, 0:1], 1.0, [parm.b])
    memset(k, "dve", parm[64:128, 1:2], 1.0, [parm.b])
    memset(k, "dve", ones_bf[:], 1.0, [ones_bf.b])
    tt_id = k.ident[0:64, 0:64].rearrange("p (h t) -> p h t", t=TSEQ)
    reduce_add(k, bmask[:], tt_id, [k.ident.b], [bmask.b])
    tt(k, "dve", qbd[:].rearrange("p a b (q t) -> p (a b) q t", q=2),
       AP(QT, 0, [[4 * 128, 128], [8, 64], [0, 2], [1, 8]]),
       AP(parm, 0, [[2, 128], [0, 64], [1, 2], [0, 8]]), ALU.mult, [QT.b, parm.b], [qbd.b])

    NB = 2
    kc3 = [S2.sb(f"kc3_{i}", [128, 8, AW], F32) for i in range(NB)]
    kc2 = [S2.sb(f"kc2_{i}", [128, 4, AW], F32) for i in range(NB)]
    kc1 = [S2.sb(f"kc1_{i}", [128, AW], F32) for i in range(NB)]
    vc3 = [S2.sb(f"vc3_{i}", [128, 8, AW], BF16) for i in range(NB)]
    vc2 = [S2.sb(f"vc2_{i}", [128, 4, AW], BF16) for i in range(NB)]
    vc1 = [S2.sb(f"vc1_{i}", [128, AW], BF16) for i in range(NB)]
    ktb = [S2.sb(f"ktb{i}", [128, 4, 128], BF16) for i in range(3)]
    pE = [S2.sb(f"pEs{i}", [128, 64], F32) for i in range(3)]
    pT = [S2.sb(f"pTs{i}", [128, 64], BF16) for i in range(3)]
    od = S2.sb("od", [64, 8, HD], F32)
    o2 = S2.sb("o2", [64, 128], F32)
    dn = S2.sb("dn", [64, 2], F32)
    n = 0
    for b in range(NSS):
        i = b % NB
        ck, cv = din["ck"], din["cv"]
        c.dma("sp", kc3[i][:], ck.ap()[b].rearrange("(i j) c -> i j c", j=16)[:, 0:8, :], writes=[kc3[i].b])
        c.dma("sp", kc2[i][:], ck.ap()[b, 1536:2048, :].rearrange("(i j) c -> i j c", j=4), writes=[kc2[i].b])
        c.dma("sp", kc1[i][:], ck.ap()[b, 1920:2048, :], writes=[kc1[i].b])
        c.dma("pool", vc3[i][:], cv.ap()[b].rearrange("(i j) c -> i j c", j=16)[:, 0:8, :], writes=[vc3[i].b])
        c.dma("pool", vc2[i][:], cv.ap()[b, 1536:2048, :].rearrange("(i j) c -> i j c", j=4), writes=[vc2[i].b])
        c.dma("pool", vc1[i][:], cv.ap()[b, 1920:2048, :], writes=[vc1[i].b])
        tiles = [(kc1[i], kc1[i][:, :], vc1[i], vc1[i][:, :], 0)]
        for rho in range(4):
            tiles.append((kc2[i], kc2[i][:, rho, :], vc2[i], vc2[i][:, rho, :], 1 + rho))
        for t0 in range(8):
            tiles.append((kc3[i], kc3[i][:, t0, :], vc3[i], vc3[i][:, t0, :], 5 + t0))
        bo = k.bank(pin=True)
        bd = k.bank(pin=True)
        ntl = len(tiles) + 1
        for ti, (kt_t, kt_ap, vt_t, vt_ap, tau) in enumerate(tiles):
            bt = k.bank()
            trs(k, [(bt[:, p * 128:(p + 1) * 128], kt_ap[:, p * 128:(p + 1) * 128]) for p in range(4)], k.ident,
                [kt_t.b, k.ident.b], [bt.b])
            kb = ktb[n % 3]
            cp(k, "act", kb[:], bt[:, :].rearrange("p (a t) -> p a t", t=128), [bt.b], [kb.b])
            bs = k.bank()
            mms(k, [(bs[:, p * 16:(p + 1) * 16], kb[:, p, :], qbd[:, p, b, :], True, True) for p in range(4)],
                [kb.b, qbd.b], [bs.b])
            e = pE[n % 3]
            p_ = pT[n % 3]
            n += 1
            act(k, e[:], bs[:, 0:64], AF.Exp, [bs.b], [e.b])
            tt(k, "dve", p_[:].rearrange("p (h t) -> p h t", t=TSEQ), e[:].rearrange("p (h t) -> p h t", t=TSEQ),
               AP(k.WS, tau * 64, [[896, 128], [1, NH], [8, TSEQ]]), ALU.mult, [e.b, k.WS.b], [p_.b])
            mms(k, [(bo[0:64, :], p_[:, :], vt_ap, ti == 0, False),
                    (bd[0:64, 0:1], p_[:, :], ones_bf[:, 0:1], ti == 0, False)], [p_.b, vt_t.b, ones_bf.b], [bo.b, bd.b])
        bs = k.bank()
        mms(k, [(bs[0:8, p * 16:(p + 1) * 16], KT[:, p, b * 8:(b + 1) * 8], qbd[:, p, b, :], True, True) for p in range(4)],
            [KT.b, qbd.b], [bs.b])
        e = pE[n % 3]
        p_ = pT[n % 3]
        n += 1
        act(k, e[0:8, :], bs[0:8, 0:64], AF.Exp, [bs.b], [e.b])
        tt(k, "dve", p_[0:8, :].rearrange("p (h t) -> p h t", t=TSEQ), e[0:8, :].rearrange("p (h t) -> p h t", t=TSEQ),
           AP(k.WS, 13 * 64, [[896, 8], [1, NH], [8, TSEQ]]), ALU.mult, [e.b, k.WS.b], [p_.b])
        mms(k, [(bo[0:64, :], p_[0:8, :], vnew[0:8, b, :], False, True),
                (bd[0:64, 0:1], p_[0:8, :], ones_bf[0:8, 0:1], False, True)], [p_.b, vnew.b, ones_bf.b], [bo.b, bd.b])
        tt(k, "dve", od[:], bo[0:64, :].rearrange("p (h e) -> p h e", e=HD),
           AP(bmask, 0, [[NH, 64], [1, NH], [0, HD]]), ALU.mult, [bo.b, bmask.b], [od.b])
        reduce_add(k, o2[:, 0:64], od[:].rearrange("p h e -> p e h"), [od.b], [o2.b])
        recip(k, dn[:, 0:1], bd[0:64, 0:1], [bd.b], [dn.b])
        ts(k, "dve", o2[:, 0:64], o2[:, 0:64], dn[:, 0:1], None, ALU.mult, None, [o2.b, dn.b], [o2.b])
        cp(k, "dve", o2[:, 64:128], o2[:, 0:64], [o2.b], [o2.b])
        k.unpin(bo)
        k.unpin(bd)
        bt = k.bank()
        trs(k, [(bt[:, 0:64], o2[:, :])], k.ident, [o2.b, k.ident.b], [bt.b])
        for par in range(2):
            prt = slice(par * 64, par * 64 + 64)
            src = bt[prt, 0:64].rearrange("p (a q t) -> p a q t", q=2, t=TSEQ)[:, :, par, :]
            cp(k, "act", attnT[prt, :, b * 8:(b + 1) * 8], src, [bt.b], [attnT.b])


_PROG = {}


def get_program(stage="full"):
    if stage not in _PROG:
        _PROG[stage] = build_program(stage)
    return _PROG[stage]


def make_core_inputs(inp, core, cst, nps=NPS, nss=NSS):
    f = lambda a: np.ascontiguousarray(np.asarray(a, dtype=np.float32))
    ps = slice(core * nps, (core + 1) * nps)
    ss = slice(core * nss, (core + 1) * nss)
    cc = np.concatenate([inp["c_prompt"][ps], inp["c_sample"][ss]], axis=0)
    m = {
        "xp": f(inp["x_prompt"][ps]),
        "xs": f(inp["x_sample"][ss].reshape(nss * TSEQ, DM)),
        "cT": f(cc.T),
        "ck": f(inp["cache_k"][0, ss].reshape(nss, WB, AW)),
        "cv": f(inp["cache_v"][0, ss].reshape(nss, WB, AW)),
        "spool": f(inp["state_pool"][0, ss]),
        "w_ada": f(inp["w_ada"][0]),
        "b_adaT": f(inp["b_ada"][0].reshape(48, 128).T),
        "b_row": f(inp["b_ada"][0].reshape(1, -1)),
        "g1T": f(inp["norm1_g"][0].reshape(8, 128).T),
        "g2T": f(inp["norm2_g"][0].reshape(8, 128).T),
        "w_in": f(inp["w_in"][0]),
        "gq2": f(np.tile(inp["q_norm_g"][0], 2).reshape(128, 1)),
        "gk_row": f(inp["k_norm_g"][0].reshape(1, HD)),
        "rel_bias": f(inp["rel_bias"]),
        "w_pool": f(inp["w_pool"][0]),
        "pscT": f(inp["pool_scale"][0].reshape(4, 128).T),
        "w_out": f(inp["w_out"][0]),
        "w_up": f(inp["w_up"][0]),
        "w_down": f(inp["w_down"][0]),
    }
    m.update(cst)
    return m


def run_cores(inp, cores, stage="full"):
    cst = make_consts()
    nc = get_program(stage)
    in_maps = [make_core_inputs(inp, cid, cst) for cid in cores]
    res = run_bass_kernel_spmd(nc, in_maps, core_ids=list(range(len(cores))))
    return res.results


def kernel(**inputs):
    inp = {k_: np.asarray(v) for k_, v in inputs.items()}
    res = run_cores(inp, list(range(NCORES)), "full")
    B = NCORES * NPS
    DB = NCORES * NSS
    yp = np.concatenate([np.asarray(r["yp"]) for r in res], axis=0).reshape(B, SEQ, DM)
    ys = np.concatenate([np.asarray(r["ys"]).reshape(NSS, TSEQ, DM) for r in res], axis=0)
    kwp = np.concatenate([np.asarray(r["kwp"]) for r in res], axis=0).reshape(1, B, SEQ, NH, HD)
    vwp = np.concatenate([np.asarray(r["vwp"]) for r in res], axis=0).reshape(1, B, SEQ, NH, HD)
    pp = np.concatenate([np.asarray(r["pp"]) for r in res], axis=0).reshape(1, B, PCTX, AW)
    kws = np.concatenate([np.asarray(r["kws"]) for r in res], axis=0).reshape(1, DB, WB, NH, HD)
    vws = np.concatenate([np.asarray(r["vws"]) for r in res], axis=0).reshape(1, DB, WB, NH, HD)
    pso = np.concatenate([np.asarray(r["pso"]) for r in res], axis=0).reshape(1, DB, PCTX, AW)
    return tuple(np.ascontiguousarray(a, dtype=np.float32) for a in (yp, ys, kwp, vwp, pp, kws, vws, pso))
```

```python
import math
from contextlib import ExitStack

import numpy as np
import ml_dtypes
import concourse.bass as bass
import concourse.mybir as mybir
from concourse.bass_utils import run_bass_kernel_spmd

F32 = mybir.dt.float32
BF16 = mybir.dt.bfloat16
AF = mybir.ActivationFunctionType
ALU = mybir.AluOpType
AX = mybir.AxisListType

NCORES = 8
NPS = 2
NSS = 16
SEQ = 2048
DM = 1024
AW = 512
NH = 8
HD = 64
FF = 4096
TSEQ = 8
WB = 2048
PCTX = 15
EPS = 1e-6
GW = 2432
WROW = 2560


class Ev:
    __slots__ = ("sem", "val", "eng", "sid")

    def __init__(self, sem, val, eng, sid):
        self.sem, self.val, self.eng, self.sid = sem, val, eng, sid


class Buf:
    def __init__(self, name):
        self.name = name
        self.w = None
        self.rd = {}
        self.dsem = None
        self.dsid = None
        self.dcount = 0


class Eng:
    def __init__(self, name):
        self.name = name
        self.ops = []
        self.sem = None
        self.sid = None
        self.count = 0
        self.waited = {}


class Ctx:
    def __init__(self, nc):
        self.nc = nc
        self.engs = {n: Eng(n) for n in ("pe", "act", "dve", "pool", "sp")}
        self.bufs = []
        self.nsem = 0
        self.free_dsems = []
        for e in self.engs.values():
            e.sem, e.sid = self._new_sem("e_" + e.name)

    def _new_sem(self, name):
        self.nsem += 1
        return self.nc.alloc_semaphore(f"{name}_{self.nsem}"), self.nsem

    def buf(self, name):
        b = Buf(name)
        self.bufs.append(b)
        return b

    def retire(self, bufs):
        for b in bufs:
            if b.dsem is not None:
                self.free_dsems.append((b.dsem, b.dsid, b.dcount))
                b.dsem = None
            if b in self.bufs:
                self.bufs.remove(b)

    def _wait(self, E, ev):
        if ev is None:
            return
        if E.waited.get(ev.sid, 0) >= ev.val:
            return
        E.waited[ev.sid] = ev.val
        sem, val = ev.sem, ev.val
        E.ops.append(lambda h: h.wait_ge(sem, val))

    def op(self, ename, fn, reads=(), writes=()):
        E = self.engs[ename]
        for b in reads:
            if b.w is not None and not (ename == "pe" and b.w.eng == "pe"):
                self._wait(E, b.w)
        for b in writes:
            if b.w is not None and not (ename == "pe" and b.w.eng == "pe"):
                self._wait(E, b.w)
            for ev in b.rd.values():
                if ev.eng != ename:
                    self._wait(E, ev)
        if E.count >= 30000:
            E.sem, E.sid = self._new_sem("e_" + E.name)
            E.count = 0
        E.count += 1
        sem, val = E.sem, E.count
        ev = Ev(sem, val, ename, E.sid)

        def run(h):
            inst = fn(h)
            inst.then_inc(sem, 1)

        E.ops.append(run)
        for b in reads:
            b.rd[ename] = ev
        for b in writes:
            b.w = ev
            b.rd = {}
        return ev

    def dma(self, q, out_ap, in_ap, reads=(), writes=(), sembuf=None):
        E = self.engs[q]
        for b in reads:
            self._wait(E, b.w)
        for b in writes:
            self._wait(E, b.w)
            for ev in b.rd.values():
                self._wait(E, ev)
        sb = sembuf or (writes[0] if writes else reads[0])
        if sb.dsem is None:
            if self.free_dsems:
                sb.dsem, sb.dsid, sb.dcount = self.free_dsems.pop()
            else:
                sb.dsem, sb.dsid = self._new_sem("d")
        sb.dcount += 16
        sem = sb.dsem
        ev = Ev(sem, sb.dcount, "dma", sb.dsid)
        E.ops.append(lambda h: h.dma_start(out=out_ap, in_=in_ap).then_inc(sem, 16))
        for b in reads:
            b.rd[("dma", sb.dsid)] = ev
        for b in writes:
            b.w = ev
            b.rd = {}
        return ev

    def barrier(self, engines=None):
        evs = []
        for e in self.engs.values():
            if e.count > 0:
                evs.append(Ev(e.sem, e.count, e.name, e.sid))
        for b in self.bufs:
            if b.dsem is not None and b.dcount > 0:
                evs.append(Ev(b.dsem, b.dcount, "dma", b.dsid))
        for s, sid, cnt in self.free_dsems:
            if cnt > 0:
                evs.append(Ev(s, cnt, "dma", sid))
        for en, E in self.engs.items():
            if engines is not None and en not in engines:
                continue
            for ev in evs:
                if ev.sid == E.sid:
                    continue
                self._wait(E, ev)

    def emit(self):
        nc = self.nc
        self.barrier(engines=("sp",))
        with nc.Block() as block:
            @block.tensor
            def _(h):
                for f in self.engs["pe"].ops:
                    f(h)

            @block.scalar
            def _(h):
                for f in self.engs["act"].ops:
                    f(h)

            @block.vector
            def _(h):
                for f in self.engs["dve"].ops:
                    f(h)

            @block.gpsimd
            def _(h):
                for f in self.engs["pool"].ops:
                    f(h)

            @block.sync
            def _(h):
                for f in self.engs["sp"].ops:
                    f(h)


class T:
    def __init__(self, t, b):
        self.t, self.b = t, b

    def __getitem__(self, k):
        return self.t[k]


class K:
    pass


def AP(t, off, dims):
    th = t.t if isinstance(t, T) else t
    return bass.AP(tensor=th, offset=off, ap=[list(d) for d in dims])


def act(k, out, in_, func, reads, writes, **kw):
    return k.c.op("act", lambda h: h.activation(out=out, in_=in_, func=func, **kw), reads, writes)


def tt(k, eng, out, in0, in1, op, reads, writes):
    return k.c.op(eng, lambda h: h.tensor_tensor(out=out, in0=in0, in1=in1, op=op), reads, writes)


def ts(k, eng, out, in0, s1, s2, op0, op1, reads, writes):
    if op1 is None:
        return k.c.op(eng, lambda h: h.tensor_scalar(out=out, in0=in0, scalar1=s1, scalar2=None, op0=op0), reads, writes)
    return k.c.op(eng, lambda h: h.tensor_scalar(out=out, in0=in0, scalar1=s1, scalar2=s2, op0=op0, op1=op1), reads, writes)


def stt(k, out, in0, scalar, in1, op0, op1, reads, writes):
    return k.c.op("dve", lambda h: h.scalar_tensor_tensor(out=out, in0=in0, scalar=scalar, in1=in1, op0=op0, op1=op1),
                  reads, writes)


def cp(k, eng, out, in_, reads, writes):
    if eng == "act":
        return k.c.op("act", lambda h: h.activation(out=out, in_=in_, func=AF.Copy), reads, writes)
    return k.c.op(eng, lambda h: h.tensor_copy(out=out, in_=in_), reads, writes)


def memset(k, eng, ap, val, writes):
    return k.c.op(eng, lambda h: h.memset(ap, val), (), writes)


def recip(k, out, in_, reads, writes):
    return k.c.op("dve", lambda h: h.reciprocal(out=out, in_=in_), reads, writes)


def reduce_add(k, out, in_, reads, writes):
    return k.c.op("dve", lambda h: h.tensor_reduce(out=out, in_=in_, axis=AX.X, op=ALU.add), reads, writes)


def mms(k, lst, reads, writes):
    lst = list(lst)

    def f(h):
        i = None
        for (o, l, r, st, sp) in lst:
            i = h.matmul(o, lhsT=l, rhs=r, start=st, stop=sp)
        return i

    return k.c.op("pe", f, reads, writes)


def trs(k, lst, ident, reads, writes):
    lst = list(lst)

    def f(h):
        i = None
        for (o, a) in lst:
            n = a.shape[0]
            i = h.transpose(out=o, in_=a, identity=ident[0:n, 0:n])
        return i

    return k.c.op("pe", f, reads, writes)


class Scope:
    def __init__(self, k):
        self.k = k
        self.es = ExitStack()
        self.bufs = []

    def __enter__(self):
        self.es.__enter__()
        return self

    def sb(self, name, shape, dt):
        t = self.es.enter_context(self.k.nc.sbuf_tensor(name + f"_{self.k.uid()}", list(shape), dt))
        b = self.k.c.buf(name)
        self.bufs.append(b)
        return T(t, b)

    def __exit__(self, *a):
        self.k.c.barrier()
        self.k.c.retire(self.bufs)
        return self.es.__exit__(*a)


def _bucket(d):
    d = np.asarray(d, dtype=np.int64)
    exact = 16
    df = np.maximum(d.astype(np.float32), np.float32(1.0))
    large = exact + (np.log(df / np.float32(exact)) / np.float32(math.log(2048 / exact))
                     * np.float32(32 - exact)).astype(np.int32)
    large = np.minimum(large, 31)
    return np.where(d < exact, d, large).astype(np.int64)


def _mult(d):
    d = np.asarray(d, dtype=np.int64)
    m = ((d >= 0) & (d <= 128)).astype(np.float32)
    m += ((d >= 0) & (d <= 512) & (d % 4 == 0)).astype(np.float32)
    m += ((d >= 0) & (d <= 2048) & (d % 16 == 0)).astype(np.float32)
    return m


def make_consts():
    cst = {}
    cst["ident"] = np.eye(128, dtype=np.float32)
    d = np.arange(WROW) - 511
    ohm = np.zeros((32, WROW), np.float32)
    dd = np.maximum(d, 0)
    ohm[_bucket(dd), np.arange(WROW)] = _mult(d)
    cst["ohm"] = ohm
    ohs = np.zeros((32, 14, 8, 128), np.float32)
    kr = np.arange(128)
    for t in range(8):
        dA = 128 + t - kr
        v = (dA >= 0) & (dA <= 128)
        ohs[_bucket(np.maximum(dA, 0))[v], 0, t, kr[v]] = 1.0
        for rho in range(4):
            if t % 4 != rho:
                continue
            d2 = 512 + t - rho - 4 * kr
            v = (d2 >= 0) & (d2 <= 512)
            ohs[_bucket(np.maximum(d2, 0))[v], 1 + rho, t, kr[v]] = 1.0
        d3 = 2048 - 16 * kr
        ohs[_bucket(d3), 5 + t, t, kr] = 1.0
        for tp in range(t + 1):
            dn = t - tp
            ohs[_bucket(dn), 13, t, tp] = _mult(dn)
    cst["ohs"] = ohs.reshape(32, 14 * 8 * 128)
    sel = np.zeros((18, 3, 128), np.float32)
    sel[0, 0, :] = 1.0
    sel[1, 1, :] = 1.0
    for b in range(NSS):
        sel[2 + b, 2, b * 8:(b + 1) * 8] = 1.0
    cst["sel"] = sel.reshape(18, 3 * 128)
    invc = np.zeros((128, 4, 15), np.float32)
    for g, w in enumerate((2, 4, 8, 16)):
        for t in range(15):
            invc[:, g, t] = 1.0 / min(w, t + 1)
    cst["invc"] = invc.reshape(128, 60)
    return cst


IN_SPECS = [
    ("xp", [NPS, SEQ, DM]), ("xs", [128, DM]), ("cT", [DM, 18]),
    ("ck", [NSS, WB, AW]), ("cv", [NSS, WB, AW]), ("spool", [NSS, PCTX, AW]),
    ("w_ada", [DM, 6 * DM]), ("b_adaT", [128, 48]), ("b_row", [1, 6 * DM]),
    ("g1T", [128, 8]), ("g2T", [128, 8]), ("w_in", [DM, 2048]), ("gq2", [128, 1]), ("gk_row", [1, HD]),
    ("rel_bias", [32, NH]), ("w_pool", [4, 128, 128]), ("pscT", [128, 4]),
    ("w_out", [DM, DM]), ("w_up", [DM, FF]), ("w_down", [FF, DM]),
    ("ident", [128, 128]), ("ohm", [32, WROW]), ("ohs", [32, 14 * 8 * 128]), ("sel", [18, 3 * 128]),
    ("invc", [128, 60]),
]
OUT_SPECS = [
    ("yp", [NPS, SEQ, DM]), ("ys", [128, DM]), ("kwp", [NPS, SEQ, AW]), ("vwp", [NPS, SEQ, AW]),
    ("pp", [NPS, PCTX, AW]), ("kws", [NSS, WB, AW]), ("vws", [NSS, WB, AW]), ("pso", [NSS, PCTX, AW]),
]


def build_program(stage="full"):
    nc = bass.Bass("TRN2", target_bir_lowering=False)
    k = K()
    k.nc = nc
    k.c = Ctx(nc)
    k._uid = 0

    def uid():
        k._uid += 1
        return k._uid

    k.uid = uid
    k.stage = stage
    c = k.c
    k.din = {}
    for name, shape in IN_SPECS:
        k.din[name] = nc.dram_tensor(name, shape, F32, kind="ExternalInput")
    k.dout = {}
    for name, shape in OUT_SPECS:
        k.dout[name] = T(nc.dram_tensor(name, shape, F32, kind="ExternalOutput"), c.buf("o_" + name))
    k.wscr = T(nc.dram_tensor("wscr", [128 * 8 * WROW], BF16, kind="Internal"), c.buf("wscr"))
    k.w_out_bf = T(nc.dram_tensor("w_out_bf", [DM, DM], BF16, kind="Internal"), c.buf("w_out_bf"))
    k.w_up_bf = T(nc.dram_tensor("w_up_bf", [DM, FF], BF16, kind="Internal"), c.buf("w_up_bf"))
    k.w_down_bf = T(nc.dram_tensor("w_down_bf", [FF, DM], BF16, kind="Internal"), c.buf("w_down_bf"))

    k.ps_t = [nc.alloc_psum_tensor(f"ps{i}", [128, 512], F32) for i in range(8)]
    k.ps_b = [c.buf(f"ps{i}") for i in range(8)]
    k.ps_i = 0

    k.pinned = set()

    def bank(pin=False):
        while k.ps_i in k.pinned:
            k.ps_i = (k.ps_i + 1) % 8
        i = k.ps_i
        k.ps_i = (i + 1) % 8
        if pin:
            k.pinned.add(i)
        t = T(k.ps_t[i], k.ps_b[i])
        t.idx = i
        return t

    def unpin(t):
        k.pinned.discard(t.idx)

    k.bank = bank
    k.unpin = unpin
    k.bg = c.buf("bg")

    with Scope(k) as P:
        k.P = P
        setup(k)
        if stage in ("full", "prompt", "p1", "p12"):
            for s in range(NPS):
                prompt_seq(k, s)
                if stage in ("p1", "p12"):
                    break
        if stage in ("full", "sample", "s1"):
            sample_group(k)
    c.emit()
    return nc


def setup(k):
    c, nc, P, din = k.c, k.nc, k.P, k.din
    full = k.stage in ("full", "sample")
    k.bg_done = False
    if k.stage == "sample":
        issue_bg(k)

    k.ident = P.sb("ident", [128, 128], F32)
    k.ones = P.sb("ones", [128, 128], F32)
    k.epsT = P.sb("epsT", [128, 1], F32)
    k.adaT = P.sb("adaT", [128, 5, 8, 18], F32)
    k.gsT = P.sb("gsT", [128, 2, 8, 18], F32)
    k.gate1 = P.sb("gate1", [128, 3, DM], F32)
    k.gq8 = P.sb("gq8", [128, 3], F32)
    k.gkb = P.sb("gkb", [128, HD], F32)
    k.pscT = P.sb("pscT", [128, 4], F32)
    k.invc = P.sb("invc", [128, 60], F32)
    k.WS = P.sb("WS", [128, 112, 8], F32)
    k.wpool = P.sb("wpool", [128, 4, 128], BF16)

    c.dma("sp", k.ident[:], din["ident"].ap(), writes=[k.ident.b])
    memset(k, "dve", k.ones[:], 1.0, [k.ones.b])
    memset(k, "dve", k.epsT[:], EPS, [k.epsT.b])
    c.dma("sp", k.pscT[:], din["pscT"].ap(), writes=[k.pscT.b])
    c.dma("sp", k.invc[:], din["invc"].ap(), writes=[k.invc.b])
    c.dma("sp", k.gkb[:], AP(din["gk_row"], 0, [[0, 128], [1, HD]]), writes=[k.gkb.b])
    c.dma("pool", k.wpool[:], din["w_pool"].ap().rearrange("g c d -> c g d"), writes=[k.wpool.b])

    if k.stage in ("full", "prompt", "sample"):
        for src, dst in ((din["w_out"], k.w_out_bf), (din["w_up"], k.w_up_bf), (din["w_down"], k.w_down_bf)):
            c.dma("pool", dst.t.ap().rearrange("(p a) n -> p (a n)", p=128),
                  src.ap().rearrange("(p a) n -> p (a n)", p=128), writes=[dst.b])

    with Scope(k) as Sx:
        cTs = Sx.sb("cTs", [128, 8, 18], F32)
        scT = Sx.sb("scT", [128, 8, 18], F32)
        badaT = Sx.sb("badaT", [128, 48], F32)
        brow = Sx.sb("brow", [1, 6 * DM], F32)
        g12 = Sx.sb("g12", [128, 2, 8], F32)
        gq2 = Sx.sb("gq2", [128, 1], F32)
        sel = Sx.sb("sel", [18, 3 * 128], F32)
        atm = Sx.sb("atm", [18, 6 * DM], F32)
        wa = [Sx.sb(f"wa{i}", [128, 8, 1024], F32) for i in range(2)]
        tmp18 = Sx.sb("tmp18", [128, 8, 18], F32)

        c.dma("sp", cTs[:], din["cT"].ap().rearrange("(k p) n -> p k n", p=128), writes=[cTs.b])
        c.dma("sp", badaT[:], din["b_adaT"].ap(), writes=[badaT.b])
        c.dma("sp", brow[:], din["b_row"].ap(), writes=[brow.b])
        c.dma("sp", g12[:, 0, :], din["g1T"].ap(), writes=[g12.b])
        c.dma("sp", g12[:, 1, :], din["g2T"].ap(), writes=[g12.b])
        c.dma("sp", gq2[:], din["gq2"].ap(), writes=[gq2.b])
        c.dma("sp", sel[:], din["sel"].ap(), writes=[sel.b])

        memset(k, "dve", k.gq8[:], 0.0, [k.gq8.b])
        ts(k, "dve", k.gq8[:, 0:1], gq2[:], HD ** -0.5, None, ALU.mult, None, [gq2.b], [k.gq8.b])
        ts(k, "dve", k.gq8[0:64, 1:2], gq2[0:64, :], HD ** -0.5, None, ALU.mult, None, [gq2.b], [k.gq8.b])
        ts(k, "dve", k.gq8[64:128, 2:3], gq2[64:128, :], HD ** -0.5, None, ALU.mult, None, [gq2.b], [k.gq8.b])
        act(k, scT[:], cTs[:], AF.Silu, [cTs.b], [scT.b])

        kind_of_blk = {0: 0, 1: 1, 3: 2, 4: 3, 5: 4}
        for blk in range(6):
            w = wa[blk % 2]
            c.dma("sp", w[:], din["w_ada"].ap()[:, blk * 1024:(blk + 1) * 1024].rearrange("(k p) n -> p k n", p=128),
                  writes=[w.b])
            for half in range(2):
                bk = k.bank()
                lst = [(bk[0:18, :], scT[:, kc, :], w[:, kc, half * 512:(half + 1) * 512], kc == 0, False)
                       for kc in range(8)]
                col0 = blk * 1024 + half * 512
                lst.append((bk[0:18, :], k.ones[0:1, 0:18], brow[0:1, col0:col0 + 512], False, True))
                mms(k, lst, [scT.b, w.b, k.ones.b, brow.b], [bk.b])
                cp(k, "dve" if half == 0 else "act", atm[0:18, col0:col0 + 512], bk[0:18, :], [bk.b], [atm.b])
            if blk != 2:
                kind = kind_of_blk[blk]
                bk = k.bank()
                trs(k, [(bk[:, fcl * 18:(fcl + 1) * 18], atm[0:18, blk * 1024 + fcl * 128:blk * 1024 + (fcl + 1) * 128])
                        for fcl in range(8)], k.ident, [atm.b, k.ident.b], [bk.b])
                cp(k, "dve", k.adaT[:, kind, :, :], bk[:, 0:144].rearrange("p (a b) -> p a b", b=18), [bk.b], [k.adaT.b])
        for j, kind in enumerate((1, 3)):
            ts(k, "dve", tmp18[:], k.adaT[:, kind, :, :], 1.0, None, ALU.add, None, [k.adaT.b], [tmp18.b])
            tt(k, "dve", k.gsT[:, j, :, :], tmp18[:], AP(g12, j * 8, [[16, 128], [1, 8], [0, 18]]), ALU.mult,
               [tmp18.b, g12.b], [k.gsT.b])
        for G in range(3):
            for half in range(2):
                bk = k.bank()
                mms(k, [(bk[:, :], sel[0:18, G * 128:(G + 1) * 128], atm[0:18, 2048 + half * 512:2048 + (half + 1) * 512],
                         True, True)], [sel.b, atm.b], [bk.b])
                cp(k, "act", k.gate1[:, G, half * 512:(half + 1) * 512], bk[:, :], [bk.b], [k.gate1.b])

    with Scope(k) as Sx:
        rb = Sx.sb("rb", [32, NH], F32)
        expb = Sx.sb("expb", [32, NH], F32)
        expr = Sx.sb("expr", [32, NH, 128], F32)
        ohm = Sx.sb("ohm", [32, WROW], F32)
        ohs = Sx.sb("ohs", [32, 112 * 128], F32)
        wrep = Sx.sb("wrep", [128, NH, WROW], BF16)
        c.dma("sp", rb[:], din["rel_bias"].ap(), writes=[rb.b])
        c.dma("sp", ohm[:], din["ohm"].ap(), writes=[ohm.b])
        c.dma("sp", ohs[:], din["ohs"].ap(), writes=[ohs.b])
        act(k, expb[:], rb[:], AF.Exp, [rb.b], [expb.b])
        cp(k, "dve", expr[:], AP(expb, 0, [[NH, 32], [1, NH], [0, 128]]), [expb.b], [expr.b])
        if k.stage != "s1":
            n = 0
            for h in range(NH):
                for blk in range(WROW // 512):
                    bk = k.bank()
                    mms(k, [(bk[:, :], expr[:, h, :], ohm[:, blk * 512:(blk + 1) * 512], True, True)],
                        [expr.b, ohm.b], [bk.b])
                    cp(k, "act" if n % 2 == 0 else "dve", wrep[:, h, blk * 512:(blk + 1) * 512], bk[:, :], [bk.b], [wrep.b])
                    n += 1
            c.dma("sp", k.wscr.t.ap().rearrange("(p f) -> p f", p=128), wrep[:].rearrange("p h w -> p (h w)"),
                  reads=[wrep.b], writes=[k.wscr.b])
        for half in range(2):
            bk = k.bank()
            lst = []
            for i in range(56):
                ii = half * 56 + i
                lst.append((bk[:, i * 8:(i + 1) * 8], ohs[:, ii * 128:(ii + 1) * 128], expb[:, :], True, True))
            mms(k, lst, [ohs.b, expb.b], [bk.b])
            cp(k, "dve", k.WS[:, half * 56:(half + 1) * 56, :], bk[:, 0:448].rearrange("p (a b) -> p a b", b=8),
               [bk.b], [k.WS.b])


def issue_bg(k):
    if k.bg_done or k.stage not in ("full", "sample"):
        return
    k.bg_done = True
    c, din = k.c, k.din
    for b in range(NSS):
        for src, dst in ((din["ck"], k.dout["kws"]), (din["cv"], k.dout["vws"])):
            c.dma("pool", dst.t.ap()[b, 0:WB - TSEQ, :].rearrange("(p r) c -> p (r c)", p=15),
                  src.ap()[b, TSEQ:WB, :].rearrange("(p r) c -> p (r c)", p=15), reads=[k.bg])
    c.dma("pool", k.dout["pso"].t.ap()[:, 0:PCTX - TSEQ, :], din["spool"].ap()[:, TSEQ:PCTX, :], reads=[k.bg])


def rms_stats(k, x, junk, st, col):
    act(k, junk[:], x[:], AF.Square, [x.b], [junk.b, st.b], accum_out=st[:, col:col + 1])
    act(k, st[:, col + 1:col + 2], st[:, col:col + 1], AF.Sqrt, [st.b, k.epsT.b], [st.b], bias=k.epsT[:], scale=1.0 / DM)
    recip(k, st[:, col + 2:col + 3], st[:, col + 1:col + 2], [st.b], [st.b])


def head_norm(k, bk, sq, rs, out, extra_gain=None):
    act(k, sq[:], bk[:, :], AF.Square, [bk.b], [sq.b])
    reduce_add(k, rs[:, 0:8], sq[:].rearrange("p (h e) -> p h e", e=HD), [sq.b], [rs.b])
    act(k, rs[:, 8:16], rs[:, 0:8], AF.Sqrt, [rs.b, k.epsT.b], [rs.b], bias=k.epsT[:], scale=1.0 / HD)
    recip(k, rs[:, 16:24], rs[:, 8:16], [rs.b], [rs.b])
    tt(k, "dve", out[:].rearrange("p (h e) -> p h e", e=HD), bk[:, :].rearrange("p (h e) -> p h e", e=HD),
       AP(rs, 16, [[24, 128], [1, 8], [0, HD]]), ALU.mult, [bk.b, rs.b], [out.b])
    if extra_gain is not None:
        tt(k, "dve", out[:].rearrange("p (h e) -> p h e", e=HD), out[:].rearrange("p (h e) -> p h e", e=HD),
           AP(extra_gain, 0, [[HD, 128], [0, 8], [1, HD]]), ALU.mult, [out.b, extra_gain.b], [out.b])


def evac_hT(k, bk, hT, cc, ncols, kind_scale, kind_shift, col, sample, tmp=None):
    if not sample:
        act(k, hT[:, cc, 0:ncols], bk[:, 0:ncols], AF.Identity, [bk.b, k.gsT.b, k.adaT.b], [hT.b],
            scale=k.gsT[:, kind_scale, cc, col:col + 1], bias=k.adaT[:, kind_shift, cc, col:col + 1])
    else:
        gs_off = ((kind_scale * 8 + cc) * 18 + 2)
        sh_off = ((kind_shift * 8 + cc) * 18 + 2)
        tt(k, "dve", tmp[:, 0:128].rearrange("p (b t) -> p b t", t=TSEQ), bk[:, 0:128].rearrange("p (b t) -> p b t", t=TSEQ),
           AP(k.gsT, gs_off, [[2 * 8 * 18, 128], [1, NSS], [0, TSEQ]]), ALU.mult, [bk.b, k.gsT.b], [tmp.b])
        tt(k, "dve", hT[:, cc, 0:128].rearrange("p (b t) -> p b t", t=TSEQ), tmp[:, 0:128].rearrange("p (b t) -> p b t", t=TSEQ),
           AP(k.adaT, sh_off, [[5 * 8 * 18, 128], [1, NSS], [0, TSEQ]]), ALU.add, [tmp.b, k.adaT.b], [hT.b])


def mixer_in_group(k, S, g, ntiles, xsrc, col, sample, R):
    c = k.c
    W = S["w_in_bf"]
    hT = S["hT"][g % 2]
    ncols = ntiles * 128
    xts = []
    for j in range(ntiles):
        T0 = g * 4 + j
        xt = S["xt"][T0 % 4]
        c.dma("sp", xt[:], xsrc(T0), writes=[xt.b])
        st = S["st"][T0 % 4]
        rms_stats(k, xt, S["junk"], st, 0)
        ts(k, "dve", xt[:], xt[:], st[:, 2:3], None, ALU.mult, None, [xt.b, st.b], [xt.b])
        xts.append(xt)
    for cc in range(8):
        bk = k.bank()
        trs(k, [(bk[:, j * 128:(j + 1) * 128], xts[j][:, cc * 128:(cc + 1) * 128]) for j in range(ntiles)], k.ident,
            [x.b for x in xts] + [k.ident.b], [bk.b])
        evac_hT(k, bk, hT, cc, ncols, 0, 0, col, sample, tmp=S["tmpf"])
    def proj(j):
        T0 = g * 4 + j
        tok = slice(j * 128, (j + 1) * 128)
        bq, bkk, bv = k.bank(), k.bank(), k.bank()
        lst = []
        for kc in range(8):
            lst.append((bq[:, :], hT[:, kc, tok], W[:, kc, 0:512], kc == 0, kc == 7))
            lst.append((bkk[:, :], hT[:, kc, tok], W[:, kc, 512:1024], kc == 0, kc == 7))
        mms(k, lst, [hT.b, W.b], [bq.b, bkk.b])
        mms(k, [(bv[:, :], hT[:, kc, tok], W[:, kc, 1024:1536], kc == 0, kc == 7) for kc in range(8)], [hT.b, W.b], [bv.b])
        qn = S["qn"][T0 % 2]
        head_norm(k, bq, S["sq"], S["rs"][0], qn)
        kn = S["kn"][T0 % 2]
        head_norm(k, bkk, S["sq2"], S["rs"][1], kn, extra_gain=k.gkb)
        R["store_k"](T0, kn)
        vo = S["vo"][T0 % 2]
        cp(k, "act", vo[:], bv[:, :], [bv.b], [vo.b])
        R["store_v"](T0, vo)

    def trsp(j):
        T0 = g * 4 + j
        qn = S["qn"][T0 % 2]
        kn = S["kn"][T0 % 2]
        bt = k.bank()
        trs(k, [(bt[:, p * 128:(p + 1) * 128], qn[:, p * 128:(p + 1) * 128]) for p in range(4)], k.ident,
            [qn.b, k.ident.b], [bt.b])
        if R.get("qmask"):
            for par in range(2):
                act(k, R["QT"][:, par, :, T0 * 128:(T0 + 1) * 128], bt[:, :].rearrange("p (a t) -> p a t", t=128), AF.Copy,
                    [bt.b, k.gq8.b], [R["QT"].b], scale=k.gq8[:, 1 + par:2 + par])
        else:
            act(k, R["QT"][:, :, T0 * 128:(T0 + 1) * 128], bt[:, :].rearrange("p (a t) -> p a t", t=128), AF.Copy,
                [bt.b, k.gq8.b], [R["QT"].b], scale=k.gq8[:, 0:1])
        bt2 = k.bank()
        trs(k, [(bt2[:, p * 128:(p + 1) * 128], kn[:, p * 128:(p + 1) * 128]) for p in range(4)], k.ident,
            [kn.b, k.ident.b], [bt2.b])
        cp(k, "dve", R["KT"][:, :, T0 * 128:(T0 + 1) * 128], bt2[:, :].rearrange("p (a t) -> p a t", t=128),
           [bt2.b], [R["KT"].b])

    proj(0)
    for j in range(ntiles):
        if j + 1 < ntiles:
            proj(j + 1)
        trsp(j)
    uT = S["uT"][g % 2]
    for uc in range(4):
        bk = k.bank()
        mms(k, [(bk[:, 0:ncols], W[:, kc, 1536 + uc * 128:1536 + (uc + 1) * 128], hT[:, kc, 0:ncols], kc == 0, kc == 7)
                for kc in range(8)], [hT.b, W.b], [bk.b])
        R["store_uT"](g, uc, bk, uT)
    return hT


def pool_adds(k, S, uT, ncols, lead, done):
    pa, pb, s4 = S["pa"], S["pb"], S["s4"]
    n = lead + ncols
    u = uT
    for uc in range(4):
        if uc == 0:
            tt(k, "pool", s4[:, 0:ncols], u[:, 0, lead:n], u[:, 0, lead - 1:n - 1], ALU.add, [u.b], [s4.b])
            done(uc)
            continue
        tt(k, "pool", pa[:, 1:n], u[:, uc, 1:n], u[:, uc, 0:n - 1], ALU.add, [u.b], [pa.b])
        if uc == 1:
            tt(k, "pool", s4[:, 0:ncols], pa[:, lead:n], pa[:, lead - 2:n - 2], ALU.add, [pa.b], [s4.b])
            done(uc)
            continue
        tt(k, "pool", pb[:, 3:n], pa[:, 3:n], pa[:, 1:n - 2], ALU.add, [pa.b], [pb.b])
        if uc == 2:
            tt(k, "pool", s4[:, 0:ncols], pb[:, lead:n], pb[:, lead - 4:n - 4], ALU.add, [pb.b], [s4.b])
            done(uc)
            continue
        tt(k, "pool", pa[:, 7:n], pb[:, 7:n], pb[:, 3:n - 4], ALU.add, [pb.b], [pa.b])
        tt(k, "pool", s4[:, 0:ncols], pa[:, lead:n], pa[:, lead - 8:n - 8], ALU.add, [pa.b], [s4.b])
        done(uc)


def prompt_seq(k, s):
    c, nc, din = k.c, k.nc, k.din
    with Scope(k) as Q0:
      R = {}
      R["poolT"] = Q0.sb("poolT", [128, 4, SEQ], BF16)
      arena = Q0.sb("arena", [128, 8, SEQ], BF16)
      R["attnT"] = arena
      R["qmask"] = True
      with Scope(k) as Q:
        R["QT"] = Q.sb("QT", [128, 2, 4, SEQ], BF16)
        R["KT"] = Q.sb("KT", [128, 4, SEQ], BF16)
        R["Vx"] = Q.sb("Vx", [128, 16, 4, 192], BF16)
        Vx = R["Vx"]
        memset(k, "pool", Vx[:], 0.0, [Vx.b])
        memset(k, "pool", Vx[:, :, :, 64:65], 1.0, [Vx.b])

        with Scope(k) as S1:
            S = {}
            S["w_in_bf"] = arena
            c.dma("pool", S["w_in_bf"][:], din["w_in"].ap().rearrange("(k p) n -> p k n", p=128), writes=[S["w_in_bf"].b])
            S["xt"] = [S1.sb(f"xt{i}", [128, DM], F32) for i in range(4)]
            S["st"] = [S1.sb(f"st{i}", [128, 4], F32) for i in range(4)]
            S["junk"] = S1.sb("junk", [128, DM], BF16)
            S["hT"] = [S1.sb(f"hT{i}", [128, 8, 512], BF16) for i in range(1)] * 2
            S["tmpf"] = None
            S["sq"] = S1.sb("sq", [128, 512], F32)
            S["sq2"] = S1.sb("sq2", [128, 512], F32)
            S["rs"] = [S1.sb(f"rs{i}", [128, 24], F32) for i in range(2)]
            S["qn"] = [S1.sb(f"qn{i}", [128, 512], F32) for i in range(2)]
            S["kn"] = [S1.sb(f"kn{i}", [128, 512], F32) for i in range(2)]
            S["vo"] = [S1.sb(f"vo{i}", [128, 512], F32) for i in range(2)]
            S["uT"] = [S1.sb(f"uT{i}", [128, 4, 16 + 512], F32) for i in range(1)] * 2
            carry = S1.sb("carry", [128, 4, 16], F32)
            S["pa"] = S1.sb("pa", [128, 528], F32)
            S["pb"] = S1.sb("pb", [128, 528], F32)
            S["s4"] = S1.sb("s4", [128, 512], F32)
            S["pooled"] = [S1.sb(f"pooled{i}", [128, 512], BF16) for i in range(2)]
            S["t15"] = S1.sb("t15", [128, 15], F32)
            utm = S["qn"][0]
            memset(k, "pool", carry[:], 0.0, [carry.b])

            def store_k(T0, kn):
                c.dma("sp", k.dout["kwp"][s, T0 * 128:(T0 + 1) * 128, :], kn[:], reads=[kn.b])

            def store_v(T0, vo):
                c.dma("sp", k.dout["vwp"][s, T0 * 128:(T0 + 1) * 128, :], vo[:], reads=[vo.b])
                tt_src = vo[:].rearrange("p (a q e) -> p a q e", q=2, e=HD)
                cp(k, "pool", Vx[:, T0, :, 0:64], tt_src[:, :, 0, :], [vo.b], [Vx.b])
                cp(k, "pool", Vx[:, T0, :, 128:192], tt_src[:, :, 1, :], [vo.b], [Vx.b])

            def store_uT(g, uc, bk, uT):
                cp(k, "act" if uc % 2 == 0 else "dve", uT[:, uc, 16:528], bk[:, :], [bk.b], [uT.b])

            R["store_k"], R["store_v"], R["store_uT"] = store_k, store_v, store_uT

            for g in range(4):
                uT = S["uT"][0]
                cp(k, "pool", uT[:, :, 0:16], carry[:], [carry.b], [uT.b])
                hT = mixer_in_group(k, S, g, 4, lambda T0: din["xp"][s, T0 * 128:(T0 + 1) * 128, :], s, False, R)
                cp(k, "pool", carry[:], uT[:, :, 512:528], [uT.b], [carry.b])

                def done(uc, g=g, uT=uT):
                    w = (2, 4, 8, 16)[uc]
                    s4 = S["s4"]
                    pooled = S["pooled"][uc % 2]
                    stt(k, pooled[:], s4[:], 1.0 / w, uT[:, uc, 16:528], ALU.mult, ALU.subtract, [s4.b, uT.b], [pooled.b])
                    if g == 0:
                        t15 = S["t15"]
                        tt(k, "dve", t15[:], s4[:, 0:15], k.invc[:, uc * 15:(uc + 1) * 15], ALU.mult,
                           [s4.b, k.invc.b], [t15.b])
                        tt(k, "dve", pooled[:, 0:15], t15[:], uT[:, uc, 16:31], ALU.subtract, [t15.b, uT.b], [pooled.b])
                    bk = k.bank()
                    mms(k, [(bk[:, :], k.wpool[:, uc, :], pooled[:], True, True)], [k.wpool.b, pooled.b], [bk.b])
                    act(k, R["poolT"][:, uc, g * 512:(g + 1) * 512], bk[:, :], AF.Copy, [bk.b, k.pscT.b], [R["poolT"].b],
                        scale=k.pscT[:, uc:uc + 1])

                pool_adds(k, S, uT, 512, 16, done)
                if g == 3:
                    bk = k.bank()
                    W = S["w_in_bf"]
                    mms(k, [(bk[:, :], hT[:, kc, 384:512], W[:, kc, 1536:2048], kc == 0, kc == 7) for kc in range(8)],
                        [hT.b, W.b], [bk.b])
                    cp(k, "dve", utm[:], bk[:, :], [bk.b], [utm.b])
                    c.dma("sp", k.dout["pp"][s, :, :], utm[128 - PCTX:128, :], reads=[utm.b])
        if k.stage == "p1":
            return

        with Scope(k) as S2:
            issue_bg(k)
            G = S2.sb("G", [128, NH, GW], BF16)
            for h in range(NH):
                c.dma("sp", G[:, h, :], AP(k.wscr, 127 + h * WROW, [[8 * WROW - 1, 128], [1, GW]]),
                      reads=[k.wscr.b], writes=[G.b])
            NPB = 5
            pE = [S2.sb(f"pE{i}", [128, 512], BF16) for i in range(NPB)]
            pT = [S2.sb(f"pT{i}", [128, 512], BF16) for i in range(NPB)]
            rden = [S2.sb(f"rden{i}", [128, 1024], F32) for i in range(2)]
            rhl = [S2.sb(f"rhl{i}", [128, 1024], BF16) for i in range(2)]
            osb = [S2.sb(f"osb{i}", [128, 512], F32) for i in range(2)]
            ones_bf = S2.sb("ones_bf2", [128, 128], BF16)
            memset(k, "dve", ones_bf[:], 1.0, [ones_bf.b])
            QT, KT, attnT = R["QT"], R["KT"], R["attnT"]
            steps = []
            for pair in range(4):
                for par in range(2):
                    for qb in range(4):
                        nkt = 4 * qb + 4
                        for kt in range(nkt):
                            steps.append((pair, par, qb, kt, nkt))
            LAG = 2
            bos = {}
            deferred = []
            gcnt = [0]

            def front(idx):
                pair, par, qb, kt, nkt = steps[idx]
                h = 2 * pair + par
                prt = slice(par * 64, par * 64 + 64)
                if kt == 0:
                    bos[(pair, par, qb)] = k.bank(pin=True)
                i = kt - 4 * qb
                q0 = max(0, i) * 128
                xoff = 512 * qb - 128 * kt + 384
                bs = k.bank()
                mms(k, [(bs[:, q0:512], KT[:, pair, kt * 128:(kt + 1) * 128],
                         QT[:, par, pair, qb * 512 + q0:(qb + 1) * 512], True, True)], [KT.b, QT.b], [bs.b])
                e = pE[idx % NPB]
                p = pT[idx % NPB]
                act(k, e[:, q0:512], bs[:, q0:512], AF.Exp, [bs.b], [e.b])
                tt(k, "dve", p[:, q0:512], e[:, q0:512], G[:, h, xoff + q0:xoff + 512], ALU.mult, [e.b, G.b], [p.b])

            def back(idx):
                pair, par, qb, kt, nkt = steps[idx]
                prt = slice(par * 64, par * 64 + 64)
                bo = bos[(pair, par, qb)]
                p = pT[idx % NPB]
                i = kt - 4 * qb
                q0 = max(0, i) * 128
                vcol = 0 if par == 0 else 64
                mms(k, [(bo[:, q0:512], Vx[:, kt, pair, vcol:vcol + 128], p[:, q0:512], kt == 0, kt == nkt - 1)],
                    [Vx.b, p.b], [bo.b])
                if kt == nkt - 1:
                    dr = 64 if par == 0 else 0
                    gcnt[0] += 1
                    rd = rden[gcnt[0] % 2]
                    ob = osb[gcnt[0] % 2]
                    rh = rhl[gcnt[0] % 2]
                    drs = slice(dr, dr + 1)
                    act(k, rd[drs, 0:512], bo[drs, :], AF.Ln, [bo.b], [rd.b])
                    act(k, rd[drs, 512:1024], rd[drs, 0:512], AF.Exp, [rd.b], [rd.b], scale=-1.0)
                    cp(k, "dve", rh[drs, 0:512], rd[drs, 512:1024], [rd.b], [rh.b])
                    tt(k, "dve", rh[drs, 512:1024], rd[drs, 512:1024], rh[drs, 0:512], ALU.subtract, [rd.b, rh.b], [rh.b])
                    cp(k, "act", ob[prt, :], bo[prt, :], [bo.b], [ob.b])

                    def epi(pair=pair, par=par, qb=qb, prt=prt, drs=drs, rh=rh, ob=ob, bo=bo):
                        bb = k.bank()
                        mms(k, [(bb[:, :], ones_bf[drs, :], rh[drs, 0:512], True, False),
                                (bb[:, :], ones_bf[drs, :], rh[drs, 512:1024], False, True)], [ones_bf.b, rh.b], [bb.b])
                        tt(k, "dve", attnT[prt, pair, qb * 512:(qb + 1) * 512], ob[prt, :], bb[prt, :], ALU.mult,
                           [ob.b, bb.b], [attnT.b])
                        k.unpin(bo)

                    deferred.append((idx + LAG + 3, epi))

            nst = len(steps)
            for idx in range(nst + LAG + 8):
                if idx < nst:
                    front(idx)
                if LAG <= idx < nst + LAG:
                    back(idx - LAG)
                for d in [d for d in deferred if d[0] <= idx]:
                    d[1]()
                    deferred.remove(d)
            assert not deferred
        if k.stage == "p12":
            return
      if k.stage in ("p1", "p12"):
          return

      with Scope(k) as S3:
          S = ffn_alloc(k, S3)
          for g in range(4):
              tail_group(k, S, g, 4, lambda T0: din["xp"][s, T0 * 128:(T0 + 1) * 128, :],
                         lambda T0: k.dout["yp"][s, T0 * 128:(T0 + 1) * 128, :], s, False,
                         lambda cc, T0: (R["attnT"] if cc < 4 else R["poolT"]), R, s)


def ffn_alloc(k, S3):
    c = k.c
    S = {}
    S["w_out_bf"] = S3.sb("w_out_sb", [128, 8, DM], BF16)
    c.dma("sp", S["w_out_bf"][:], k.w_out_bf.t.ap().rearrange("(k p) n -> p k n", p=128), reads=[k.w_out_bf.b],
          writes=[S["w_out_bf"].b])
    S["wu"] = [S3.sb(f"wu{i}", [128, 8, 512], BF16) for i in range(2)]
    S["wd"] = [S3.sb(f"wd{i}", [128, 4, 512], BF16) for i in range(2)]
    S["xt"] = [S3.sb(f"xt3_{i}", [128, DM], F32) for i in range(2)]
    S["x1"] = [S3.sb(f"x1_{i}", [128, DM], F32) for i in range(4)]
    S["xn"] = [S3.sb(f"xn2_{i}", [128, DM], F32) for i in range(4)]
    S["st"] = [S3.sb(f"st3_{i}", [128, 4], F32) for i in range(4)]
    S["junk"] = S3.sb("junk3", [128, DM], BF16)
    S["hT2"] = S3.sb("hT2", [128, 8, 512], BF16)
    S["fT"] = S3.sb("fT", [128, 32, 512], BF16)
    S["sqf"] = [S3.sb(f"sqf{i}", [128, 512], F32) for i in range(2)]
    S["yT"] = [S3.sb(f"yT{i}", [128, 512], F32) for i in range(4)]
    S["tmpf"] = S3.sb("tmpf3", [128, 512], F32)
    return S


def tail_group(k, S, g, ntiles, xsrc, ydst, col, sample, mixsrc, R, G):
    c = k.c
    ncols = ntiles * 128
    Wo = S["w_out_bf"]
    x1s, xns = [], []
    for j in range(ntiles):
        T0 = g * 4 + j
        tok = slice(T0 * 128, (T0 + 1) * 128)
        xt = S["xt"][T0 % 2]
        c.dma("sp", xt[:], xsrc(T0), writes=[xt.b])
        b0, b1 = k.bank(), k.bank()
        lst = []
        rd = [Wo.b]
        for cc in range(8):
            m = mixsrc(cc, T0)
            if m.b not in rd:
                rd.append(m.b)
            lst.append((b0[:, :], m[:, cc % 4, tok], Wo[:, cc, 0:512], cc == 0, cc == 7))
            lst.append((b1[:, :], m[:, cc % 4, tok], Wo[:, cc, 512:1024], cc == 0, cc == 7))
        mms(k, lst, rd, [b0.b, b1.b])
        x1 = S["x1"][j]
        xn = S["xn"][j]
        for half, bb in enumerate((b0, b1)):
            hs = slice(half * 512, (half + 1) * 512)
            tt(k, "dve", xn[:, hs], bb[:, :], k.gate1[:, G, hs], ALU.mult, [bb.b, k.gate1.b], [xn.b])
        tt(k, "pool", x1[:], xn[:], xt[:], ALU.add, [xn.b, xt.b], [x1.b])
        st = S["st"][j]
        rms_stats(k, x1, S["junk"], st, 0)
        ts(k, "dve", xn[:], x1[:], st[:, 2:3], None, ALU.mult, None, [x1.b, st.b], [xn.b])
        x1s.append(x1)
        xns.append(xn)
    hT2 = S["hT2"]
    for cc in range(8):
        bk = k.bank()
        trs(k, [(bk[:, j * 128:(j + 1) * 128], xns[j][:, cc * 128:(cc + 1) * 128]) for j in range(ntiles)], k.ident,
            [x.b for x in xns] + [k.ident.b], [bk.b])
        evac_hT(k, bk, hT2, cc, ncols, 1, 2, col, sample, tmp=S["tmpf"])
    fT = S["fT"]
    for ffg in range(8):
        wu = S["wu"][ffg % 2]
        c.dma("sp", wu[:], k.w_up_bf.t.ap()[:, ffg * 512:(ffg + 1) * 512].rearrange("(k p) n -> p k n", p=128),
              reads=[k.w_up_bf.b], writes=[wu.b])
        for fj in range(4):
            fc = ffg * 4 + fj
            bk = k.bank()
            mms(k, [(bk[:, 0:ncols], wu[:, kc, fj * 128:(fj + 1) * 128], hT2[:, kc, 0:ncols], kc == 0, kc == 7)
                    for kc in range(8)], [wu.b, hT2.b], [bk.b])
            sq = S["sqf"][fc % 2]
            act(k, sq[:, 0:ncols], bk[:, 0:ncols], AF.Square, [bk.b], [sq.b])
            stt(k, fT[:, fc, 0:ncols], bk[:, 0:ncols], 0.0, sq[:, 0:ncols], ALU.is_gt, ALU.mult, [bk.b, sq.b], [fT.b])
    for half in range(2):
        accs = [k.bank(pin=True) for _ in range(4)]
        for ffg in range(8):
            wd = S["wd"][(half * 8 + ffg) % 2]
            c.dma("sp", wd[:], k.w_down_bf.t.ap()[ffg * 512:(ffg + 1) * 512, half * 512:(half + 1) * 512]
                  .rearrange("(j p) n -> p j n", p=128), reads=[k.w_down_bf.b], writes=[wd.b])
            lst = []
            for dc in range(4):
                for fj in range(4):
                    fc = ffg * 4 + fj
                    lst.append((accs[dc][:, 0:ncols], wd[:, fj, dc * 128:(dc + 1) * 128], fT[:, fc, 0:ncols],
                                ffg == 0 and fj == 0, ffg == 7 and fj == 3))
            mms(k, lst, [wd.b, fT.b], [a.b for a in accs])
        for dc in range(4):
            dcg = half * 4 + dc
            yT = S["yT"][dc]
            if not sample:
                act(k, yT[:, 0:ncols], accs[dc][:, 0:ncols], AF.Copy, [accs[dc].b, k.adaT.b], [yT.b],
                    scale=k.adaT[:, 4, dcg, col:col + 1])
            else:
                off = ((4 * 8 + dcg) * 18 + 2)
                tt(k, "dve", yT[:, 0:128].rearrange("p (b t) -> p b t", t=TSEQ),
                   accs[dc][:, 0:128].rearrange("p (b t) -> p b t", t=TSEQ),
                   AP(k.adaT, off, [[5 * 8 * 18, 128], [1, NSS], [0, TSEQ]]), ALU.mult, [accs[dc].b, k.adaT.b], [yT.b])
        for a in accs:
            k.unpin(a)
        for j in range(ntiles):
            bk = k.bank()
            trs(k, [(bk[:, dc * 128:(dc + 1) * 128], S["yT"][dc][:, j * 128:(j + 1) * 128]) for dc in range(4)], k.ident,
                [S["yT"][dc].b for dc in range(4)] + [k.ident.b], [bk.b])
            hs = slice(half * 512, (half + 1) * 512)
            tt(k, "dve", x1s[j][:, hs], bk[:, :], x1s[j][:, hs], ALU.add, [bk.b, x1s[j].b], [x1s[j].b])
            if half == 1:
                c.dma("sp", ydst(g * 4 + j), x1s[j][:], reads=[x1s[j].b])


def sample_group(k):
    c, nc, din = k.c, k.nc, k.din
    with Scope(k) as Q:
        R = {}
        R["QT"] = Q.sb("QTs", [128, 4, 128], BF16)
        R["KT"] = Q.sb("KTs", [128, 4, 128], BF16)
        R["poolT"] = Q.sb("poolTs", [128, 4, 128], BF16)
        R["attnT"] = Q.sb("attnTs", [128, 4, 128], BF16)
        uTs = Q.sb("uTs", [128, 4, NSS, 24], F32)
        vnew = Q.sb("vnew", [8, NSS, AW], BF16)
        utm = Q.sb("utm_s", [128, 512], F32)

        with Scope(k) as S1:
            S = {}
            S["w_in_bf"] = S1.sb("w_in_bf_s", [128, 8, 2048], BF16)
            c.dma("pool", S["w_in_bf"][:], din["w_in"].ap().rearrange("(k p) n -> p k n", p=128), writes=[S["w_in_bf"].b])
            S["xt"] = [S1.sb(f"xts{i}", [128, DM], F32) for i in range(4)]
            S["st"] = [S1.sb(f"sts{i}", [128, 4], F32) for i in range(4)]
            S["junk"] = S1.sb("junks", [128, DM], BF16)
            S["hT"] = [S1.sb(f"hTs{i}", [128, 8, 512], BF16) for i in range(1)] * 2
            S["tmpf"] = S1.sb("tmpfs", [128, 512], F32)
            S["sq"] = S1.sb("sqs", [128, 512], F32)
            S["sq2"] = S1.sb("sq2s", [128, 512], F32)
            S["rs"] = [S1.sb(f"rss{i}", [128, 24], F32) for i in range(2)]
            S["qn"] = [S1.sb(f"qns{i}", [128, 512], F32) for i in range(2)]
            S["kn"] = [S1.sb(f"kns{i}", [128, 512], F32) for i in range(2)]
            S["vo"] = [S1.sb(f"vos{i}", [128, 512], F32) for i in range(2)]
            S["uT"] = [None, None]
            spt = [S1.sb(f"spt{i}", [120, 512], F32) for i in range(2)]

            memset(k, "pool", uTs[:], 0.0, [uTs.b])
            for i in range(2):
                c.dma("sp", spt[i][:], din["spool"].ap()[i * 8:(i + 1) * 8, :, :].rearrange("b r c -> (b r) c"),
                      writes=[spt[i].b])
                bk = k.bank()
                trs(k, [(bk[:, uc * 120:(uc + 1) * 120], spt[i][:, uc * 128:(uc + 1) * 128]) for uc in range(4)], k.ident,
                    [spt[i].b, k.ident.b], [bk.b])
                cp(k, "dve", uTs[:, :, i * 8:(i + 1) * 8, 1:16],
                   bk[:, 0:480].rearrange("p (u b r) -> p u b r", u=4, b=8), [bk.b], [uTs.b])

            def store_k(T0, kn):
                for b in range(NSS):
                    c.dma("sp", k.dout["kws"][b, WB - TSEQ:WB, :], kn[b * 8:(b + 1) * 8, :], reads=[kn.b])

            def store_v(T0, vo):
                for b in range(NSS):
                    c.dma("sp", k.dout["vws"][b, WB - TSEQ:WB, :], vo[b * 8:(b + 1) * 8, :], reads=[vo.b],
                          writes=[k.dout["vws"].b])
                c.dma("pool", vnew[:], k.dout["vws"].t.ap()[:, WB - TSEQ:WB, :].rearrange("b t c -> t b c"),
                      reads=[k.dout["vws"].b], writes=[vnew.b])

            def store_uT(g, uc, bk, uT):
                cp(k, "act" if uc % 2 == 0 else "dve", uTs[:, uc, :, 16:24],
                   bk[:, 0:128].rearrange("p (b t) -> p b t", t=TSEQ), [bk.b], [uTs.b])

            R["store_k"], R["store_v"], R["store_uT"] = store_k, store_v, store_uT
            hT = mixer_in_group(k, S, 0, 1, lambda T0: din["xs"].ap(), 0, True, R)
            bk = k.bank()
            W = S["w_in_bf"]
            mms(k, [(bk[:, :], hT[:, kc, 0:128], W[:, kc, 1536:2048], kc == 0, kc == 7) for kc in range(8)],
                [hT.b, W.b], [bk.b])
            cp(k, "dve", utm[:], bk[:, :], [bk.b], [utm.b])
            for b in range(NSS):
                c.dma("sp", k.dout["pso"][b, PCTX - TSEQ:PCTX, :], utm[b * 8:(b + 1) * 8, :], reads=[utm.b])

            pa = S1.sb("pas", [128, NSS, 24], F32)
            pb = S1.sb("pbs", [128, NSS, 24], F32)
            s4 = S1.sb("s4s", [128, 4, NSS, 8], F32)
            pooled = S1.sb("pooleds", [128, 4, NSS * 8], BF16)
            n = 24
            u = uTs
            tt(k, "pool", s4[:, 0, :, :], u[:, 0, :, 16:n], u[:, 0, :, 15:n - 1], ALU.add, [u.b], [s4.b])
            for uc in (1, 2, 3):
                tt(k, "pool", pa[:, :, 1:n], u[:, uc, :, 1:n], u[:, uc, :, 0:n - 1], ALU.add, [u.b], [pa.b])
                if uc == 1:
                    tt(k, "pool", s4[:, 1, :, :], pa[:, :, 16:n], pa[:, :, 14:n - 2], ALU.add, [pa.b], [s4.b])
                    continue
                tt(k, "pool", pb[:, :, 3:n], pa[:, :, 3:n], pa[:, :, 1:n - 2], ALU.add, [pa.b], [pb.b])
                if uc == 2:
                    tt(k, "pool", s4[:, 2, :, :], pb[:, :, 16:n], pb[:, :, 12:n - 4], ALU.add, [pb.b], [s4.b])
                    continue
                tt(k, "pool", pa[:, :, 7:n], pb[:, :, 7:n], pb[:, :, 3:n - 4], ALU.add, [pb.b], [pa.b])
                tt(k, "pool", s4[:, 3, :, :], pa[:, :, 16:n], pa[:, :, 8:n - 8], ALU.add, [pa.b], [s4.b])
            for uc, w in enumerate((2, 4, 8, 16)):
                stt(k, pooled[:, uc, :].rearrange("p (b t) -> p b t", t=TSEQ), s4[:, uc, :, :], 1.0 / w, u[:, uc, :, 16:24],
                    ALU.mult, ALU.subtract, [s4.b, u.b], [pooled.b])
            for uc in range(4):
                bk = k.bank()
                mms(k, [(bk[:, 0:128], k.wpool[:, uc, :], pooled[:, uc, :], True, True)], [k.wpool.b, pooled.b], [bk.b])
                act(k, R["poolT"][:, uc, :], bk[:, 0:128], AF.Copy, [bk.b, k.pscT.b], [R["poolT"].b],
                    scale=k.pscT[:, uc:uc + 1])
        if k.stage == "s1":
            return

        with Scope(k) as S2:
            sample_attention(k, S2, R, vnew)

        with Scope(k) as S3:
            S = ffn_alloc(k, S3)
            tail_group(k, S, 0, 1, lambda T0: din["xs"].ap(), lambda T0: k.dout["ys"].t.ap(), 0, True,
                       lambda cc, T0: (R["attnT"] if cc < 4 else R["poolT"]), R, 2)


def sample_attention(k, S2, R, vnew):
    c, din = k.c, k.din
    QT, KT, attnT = R["QT"], R["KT"], R["attnT"]
    qbd = S2.sb("qbd", [128, 4, NSS, 16], BF16)
    parm = S2.sb("parm", [128, 2], F32)
    bmask = S2.sb("bmask", [64, NH], F32)
    ones_bf = S2.sb("ones_bf", [128, 1], BF16)
    memset(k, "dve", parm[:], 0.0, [parm.b])
    memset(k, "dve", parm[0:64, 0:1], 1.0, [parm.b])
    memset(k, "dve", parm[64:128, 1:2], 1.0, [parm.b])
    memset(k, "dve", ones_bf[:], 1.0, [ones_bf.b])
    tt_id = k.ident[0:64, 0:64].rearrange("p (h t) -> p h t", t=TSEQ)
    reduce_add(k, bmask[:], tt_id, [k.ident.b], [bmask.b])
    tt(k, "dve", qbd[:].rearrange("p a b (q t) -> p (a b) q t", q=2),
       AP(QT, 0, [[4 * 128, 128], [8, 64], [0, 2], [1, 8]]),
       AP(parm, 0, [[2, 128], [0, 64], [1, 2], [0, 8]]), ALU.mult, [QT.b, parm.b], [qbd.b])

    NB = 3
    kc3 = [S2.sb(f"kc3_{i}", [128, 8, AW], F32) for i in range(NB)]
    kc2 = [S2.sb(f"kc2_{i}", [128, 4, AW], F32) for i in range(NB)]
    kc1 = [S2.sb(f"kc1_{i}", [128, AW], F32) for i in range(NB)]
    vc3 = [S2.sb(f"vc3_{i}", [128, 8, AW], BF16) for i in range(NB)]
    vc2 = [S2.sb(f"vc2_{i}", [128, 4, AW], BF16) for i in range(NB)]
    vc1 = [S2.sb(f"vc1_{i}", [128, AW], BF16) for i in range(NB)]
    NK = 4
    ktb = [S2.sb(f"ktb{i}", [128, 4, 128], BF16) for i in range(NK)]
    pE = [S2.sb(f"pEs{i}", [128, 64], F32) for i in range(NK)]
    pT = [S2.sb(f"pTs{i}", [128, 64], BF16) for i in range(NK)]
    od = [S2.sb(f"od{i}", [64, 8, HD], F32) for i in range(2)]
    o2 = [S2.sb(f"o2{i}", [64, 128], F32) for i in range(2)]
    dn = [S2.sb(f"dn{i}", [64, 2], F32) for i in range(2)]
    ck, cv = din["ck"], din["cv"]

    def load_seq(b):
        i = b % NB
        c.dma("sp", kc3[i][:], ck.ap()[b].rearrange("(i j) c -> i j c", j=16)[:, 0:8, :], writes=[kc3[i].b])
        c.dma("sp", kc2[i][:], ck.ap()[b, 1536:2048, :].rearrange("(i j) c -> i j c", j=4), writes=[kc2[i].b])
        c.dma("sp", kc1[i][:], ck.ap()[b, 1920:2048, :], writes=[kc1[i].b])
        c.dma("pool", vc3[i][:], cv.ap()[b].rearrange("(i j) c -> i j c", j=16)[:, 0:8, :], writes=[vc3[i].b])
        c.dma("pool", vc2[i][:], cv.ap()[b, 1536:2048, :].rearrange("(i j) c -> i j c", j=4), writes=[vc2[i].b])
        c.dma("pool", vc1[i][:], cv.ap()[b, 1920:2048, :], writes=[vc1[i].b])

    def tile_of(b, ti):
        i = b % NB
        if ti == 0:
            return kc1[i], kc1[i][:, :], vc1[i], vc1[i][:, :], 0
        if ti <= 4:
            rho = ti - 1
            return kc2[i], kc2[i][:, rho, :], vc2[i], vc2[i][:, rho, :], 1 + rho
        t0 = ti - 5
        return kc3[i], kc3[i][:, t0, :], vc3[i], vc3[i][:, t0, :], 5 + t0

    NT = 14
    steps = [(b, ti) for b in range(NSS) for ti in range(NT)]
    acc = {}
    deferred = []

    def stA(i):
        b, ti = steps[i]
        if ti == 0 and b + 1 < NSS:
            load_seq(b + 1)
        if ti == NT - 1:
            return
        kt_t, kt_ap, vt_t, vt_ap, tau = tile_of(b, ti)
        bt = k.bank()
        trs(k, [(bt[:, p * 128:(p + 1) * 128], kt_ap[:, p * 128:(p + 1) * 128]) for p in range(4)], k.ident,
            [kt_t.b, k.ident.b], [bt.b])
        kb = ktb[i % NK]
        cp(k, "act", kb[:], bt[:, :].rearrange("p (a t) -> p a t", t=128), [bt.b], [kb.b])

    def stB(i):
        b, ti = steps[i]
        e = pE[i % NK]
        p_ = pT[i % NK]
        bs = k.bank()
        if ti < NT - 1:
            tau = tile_of(b, ti)[4]
            kb = ktb[i % NK]
            mms(k, [(bs[:, p * 16:(p + 1) * 16], kb[:, p, :], qbd[:, p, b, :], True, True) for p in range(4)],
                [kb.b, qbd.b], [bs.b])
            act(k, e[:], bs[:, 0:64], AF.Exp, [bs.b], [e.b])
            tt(k, "dve", p_[:].rearrange("p (h t) -> p h t", t=TSEQ), e[:].rearrange("p (h t) -> p h t", t=TSEQ),
               AP(k.WS, tau * 64, [[896, 128], [1, NH], [8, TSEQ]]), ALU.mult, [e.b, k.WS.b], [p_.b])
        else:
            mms(k, [(bs[0:8, p * 16:(p + 1) * 16], KT[:, p, b * 8:(b + 1) * 8], qbd[:, p, b, :], True, True)
                    for p in range(4)], [KT.b, qbd.b], [bs.b])
            act(k, e[0:8, :], bs[0:8, 0:64], AF.Exp, [bs.b], [e.b])
            tt(k, "dve", p_[0:8, :].rearrange("p (h t) -> p h t", t=TSEQ), e[0:8, :].rearrange("p (h t) -> p h t", t=TSEQ),
               AP(k.WS, 13 * 64, [[896, 8], [1, NH], [8, TSEQ]]), ALU.mult, [e.b, k.WS.b], [p_.b])

    def stC(i):
        b, ti = steps[i]
        p_ = pT[i % NK]
        if ti == 0:
            acc[b] = (k.bank(pin=True), k.bank(pin=True))
        bo, bd = acc[b]
        if ti < NT - 1:
            kt_t, kt_ap, vt_t, vt_ap, tau = tile_of(b, ti)
            mms(k, [(bo[0:64, :], p_[:, :], vt_ap, ti == 0, False),
                    (bd[0:64, 0:1], p_[:, :], ones_bf[:, 0:1], ti == 0, False)], [p_.b, vt_t.b, ones_bf.b], [bo.b, bd.b])
            return
        mms(k, [(bo[0:64, :], p_[0:8, :], vnew[0:8, b, :], False, True),
                (bd[0:64, 0:1], p_[0:8, :], ones_bf[0:8, 0:1], False, True)], [p_.b, vnew.b, ones_bf.b], [bo.b, bd.b])
        od_, o2_, dn_ = od[b % 2], o2[b % 2], dn[b % 2]
        tt(k, "dve", od_[:], bo[0:64, :].rearrange("p (h e) -> p h e", e=HD),
           AP(bmask, 0, [[NH, 64], [1, NH], [0, HD]]), ALU.mult, [bo.b, bmask.b], [od_.b])
        reduce_add(k, o2_[:, 0:64], od_[:].rearrange("p h e -> p e h"), [od_.b], [o2_.b])
        recip(k, dn_[:, 0:1], bd[0:64, 0:1], [bd.b], [dn_.b])
        ts(k, "dve", o2_[:, 0:64], o2_[:, 0:64], dn_[:, 0:1], None, ALU.mult, None, [o2_.b, dn_.b], [o2_.b])
        cp(k, "dve", o2_[:, 64:128], o2_[:, 0:64], [o2_.b], [o2_.b])
        k.unpin(bo)
        k.unpin(bd)

        def epi(b=b, o2_=o2_):
            bt = k.bank()
            trs(k, [(bt[:, 0:64], o2_[:, :])], k.ident, [o2_.b, k.ident.b], [bt.b])
            for par in range(2):
                prt = slice(par * 64, par * 64 + 64)
                src = bt[prt, 0:64].rearrange("p (a q t) -> p a q t", q=2, t=TSEQ)[:, :, par, :]
                cp(k, "act", attnT[prt, :, b * 8:(b + 1) * 8], src, [bt.b], [attnT.b])

        deferred.append((i + 4, epi))

    load_seq(0)
    n = len(steps)
    for i in range(n + 8):
        if i < n:
            stA(i)
        if 1 <= i < n + 1:
            stB(i - 1)
        if 2 <= i < n + 2:
            stC(i - 2)
        for d in [d for d in deferred if d[0] <= i]:
            d[1]()
            deferred.remove(d)
    assert not deferred


_PROG = {}


def get_program(stage="full"):
    if stage not in _PROG:
        _PROG[stage] = build_program(stage)
    return _PROG[stage]


def make_core_inputs(inp, core, cst, nps=NPS, nss=NSS):
    f = lambda a: np.ascontiguousarray(np.asarray(a, dtype=np.float32))
    ps = slice(core * nps, (core + 1) * nps)
    ss = slice(core * nss, (core + 1) * nss)
    cc = np.concatenate([inp["c_prompt"][ps], inp["c_sample"][ss]], axis=0)
    m = {
        "xp": f(inp["x_prompt"][ps]),
        "xs": f(inp["x_sample"][ss].reshape(nss * TSEQ, DM)),
        "cT": f(cc.T),
        "ck": f(inp["cache_k"][0, ss].reshape(nss, WB, AW)),
        "cv": f(inp["cache_v"][0, ss].reshape(nss, WB, AW)),
        "spool": f(inp["state_pool"][0, ss]),
        "w_ada": f(inp["w_ada"][0]),
        "b_adaT": f(inp["b_ada"][0].reshape(48, 128).T),
        "b_row": f(inp["b_ada"][0].reshape(1, -1)),
        "g1T": f(inp["norm1_g"][0].reshape(8, 128).T),
        "g2T": f(inp["norm2_g"][0].reshape(8, 128).T),
        "w_in": f(inp["w_in"][0]),
        "gq2": f(np.tile(inp["q_norm_g"][0], 2).reshape(128, 1)),
        "gk_row": f(inp["k_norm_g"][0].reshape(1, HD)),
        "rel_bias": f(inp["rel_bias"]),
        "w_pool": f(inp["w_pool"][0]),
        "pscT": f(inp["pool_scale"][0].reshape(4, 128).T),
        "w_out": f(inp["w_out"][0]),
        "w_up": f(inp["w_up"][0]),
        "w_down": f(inp["w_down"][0]),
    }
    m.update(cst)
    return m


def run_cores(inp, cores, stage="full"):
    cst = make_consts()
    nc = get_program(stage)
    in_maps = [make_core_inputs(inp, cid, cst) for cid in cores]
    res = run_bass_kernel_spmd(nc, in_maps, core_ids=list(range(len(cores))))
    return res.results


def kernel(**inputs):
    inp = {k_: np.asarray(v) for k_, v in inputs.items()}
    res = run_cores(inp, list(range(NCORES)), "full")
    B = NCORES * NPS
    DB = NCORES * NSS
    yp = np.concatenate([np.asarray(r["yp"]) for r in res], axis=0).reshape(B, SEQ, DM)
    ys = np.concatenate([np.asarray(r["ys"]).reshape(NSS, TSEQ, DM) for r in res], axis=0)
    kwp = np.concatenate([np.asarray(r["kwp"]) for r in res], axis=0).reshape(1, B, SEQ, NH, HD)
    vwp = np.concatenate([np.asarray(r["vwp"]) for r in res], axis=0).reshape(1, B, SEQ, NH, HD)
    pp = np.concatenate([np.asarray(r["pp"]) for r in res], axis=0).reshape(1, B, PCTX, AW)
    kws = np.concatenate([np.asarray(r["kws"]) for r in res], axis=0).reshape(1, DB, WB, NH, HD)
    vws = np.concatenate([np.asarray(r["vws"]) for r in res], axis=0).reshape(1, DB, WB, NH, HD)
    pso = np.concatenate([np.asarray(r["pso"]) for r in res], axis=0).reshape(1, DB, PCTX, AW)
    return tuple(np.ascontiguousarray(a, dtype=np.float32) for a in (yp, ys, kwp, vwp, pp, kws, vws, pso))
```

```python
import math
from contextlib import ExitStack

import numpy as np
import ml_dtypes
import concourse.bass as bass
import concourse.mybir as mybir
from concourse.bass_utils import run_bass_kernel_spmd

F32 = mybir.dt.float32
BF16 = mybir.dt.bfloat16
AF = mybir.ActivationFunctionType
ALU = mybir.AluOpType
AX = mybir.AxisListType

NCORES = 8
NPS = 2
NSS = 16
SEQ = 2048
DM = 1024
AW = 512
NH = 8
HD = 64
FF = 4096
TSEQ = 8
WB = 2048
PCTX = 15
EPS = 1e-6
GW = 2432
WROW = 2560


class Ev:
    __slots__ = ("sem", "val", "eng", "sid")

    def __init__(self, sem, val, eng, sid):
        self.sem, self.val, self.eng, self.sid = sem, val, eng, sid


class Buf:
    def __init__(self, name):
        self.name = name
        self.w = None
        self.rd = {}
        self.dsem = None
        self.dsid = None
        self.dcount = 0


class Eng:
    def __init__(self, name):
        self.name = name
        self.ops = []
        self.sem = None
        self.sid = None
        self.count = 0
        self.waited = {}


class Ctx:
    def __init__(self, nc):
        self.nc = nc
        self.engs = {n: Eng(n) for n in ("pe", "act", "dve", "pool", "sp")}
        self.bufs = []
        self.nsem = 0
        self.free_dsems = []
        for e in self.engs.values():
            e.sem, e.sid = self._new_sem("e_" + e.name)

    def _new_sem(self, name):
        self.nsem += 1
        return self.nc.alloc_semaphore(f"{name}_{self.nsem}"), self.nsem

    def buf(self, name):
        b = Buf(name)
        self.bufs.append(b)
        return b

    def retire(self, bufs):
        for b in bufs:
            if b.dsem is not None:
                self.free_dsems.append((b.dsem, b.dsid, b.dcount))
                b.dsem = None
            if b in self.bufs:
                self.bufs.remove(b)

    def _wait(self, E, ev):
        if ev is None:
            return
        if E.waited.get(ev.sid, 0) >= ev.val:
            return
        E.waited[ev.sid] = ev.val
        sem, val = ev.sem, ev.val
        E.ops.append(lambda h: h.wait_ge(sem, val))

    def op(self, ename, fn, reads=(), writes=()):
        E = self.engs[ename]
        for b in reads:
            if b.w is not None and not (ename == "pe" and b.w.eng == "pe"):
                self._wait(E, b.w)
        for b in writes:
            if b.w is not None and not (ename == "pe" and b.w.eng == "pe"):
                self._wait(E, b.w)
            for ev in b.rd.values():
                if ev.eng != ename:
                    self._wait(E, ev)
        if E.count >= 30000:
            E.sem, E.sid = self._new_sem("e_" + E.name)
            E.count = 0
        E.count += 1
        sem, val = E.sem, E.count
        ev = Ev(sem, val, ename, E.sid)

        def run(h):
            inst = fn(h)
            inst.then_inc(sem, 1)

        E.ops.append(run)
        for b in reads:
            b.rd[ename] = ev
        for b in writes:
            b.w = ev
            b.rd = {}
        return ev

    def dma(self, q, out_ap, in_ap, reads=(), writes=(), sembuf=None):
        E = self.engs[q]
        for b in reads:
            self._wait(E, b.w)
        for b in writes:
            self._wait(E, b.w)
            for ev in b.rd.values():
                self._wait(E, ev)
        sb = sembuf or (writes[0] if writes else reads[0])
        if sb.dsem is None:
            if self.free_dsems:
                sb.dsem, sb.dsid, sb.dcount = self.free_dsems.pop()
            else:
                sb.dsem, sb.dsid = self._new_sem("d")
        sb.dcount += 16
        sem = sb.dsem
        ev = Ev(sem, sb.dcount, "dma", sb.dsid)
        E.ops.append(lambda h: h.dma_start(out=out_ap, in_=in_ap).then_inc(sem, 16))
        for b in reads:
            b.rd[("dma", sb.dsid)] = ev
        for b in writes:
            b.w = ev
            b.rd = {}
        return ev

    def barrier(self, engines=None, final=False):
        evs = []
        for e in self.engs.values():
            if e.count > 0:
                evs.append(Ev(e.sem, e.count, e.name, e.sid))
        for b in self.bufs:
            if getattr(b, "nobar", False) and not final:
                continue
            if b.dsem is not None and b.dcount > 0:
                evs.append(Ev(b.dsem, b.dcount, "dma", b.dsid))
        for s, sid, cnt in self.free_dsems:
            if cnt > 0:
                evs.append(Ev(s, cnt, "dma", sid))
        for en, E in self.engs.items():
            if engines is not None and en not in engines:
                continue
            for ev in evs:
                if ev.sid == E.sid:
                    continue
                self._wait(E, ev)

    def emit(self):
        nc = self.nc
        self.barrier(engines=("sp",), final=True)
        with nc.Block() as block:
            @block.tensor
            def _(h):
                for f in self.engs["pe"].ops:
                    f(h)

            @block.scalar
            def _(h):
                for f in self.engs["act"].ops:
                    f(h)

            @block.vector
            def _(h):
                for f in self.engs["dve"].ops:
                    f(h)

            @block.gpsimd
            def _(h):
                for f in self.engs["pool"].ops:
                    f(h)

            @block.sync
            def _(h):
                for f in self.engs["sp"].ops:
                    f(h)


class T:
    def __init__(self, t, b):
        self.t, self.b = t, b

    def __getitem__(self, k):
        return self.t[k]


class K:
    pass


class WParts:
    def __init__(self, parts, per):
        self.parts, self.per = parts, per
        self.bs = [p.b for p in parts]

    def __getitem__(self, key):
        sl, kc, cols = key
        return self.parts[kc // self.per][sl, kc % self.per, cols]


def AP(t, off, dims):
    th = t.t if isinstance(t, T) else t
    return bass.AP(tensor=th, offset=off, ap=[list(d) for d in dims])


def act(k, out, in_, func, reads, writes, **kw):
    return k.c.op("act", lambda h: h.activation(out=out, in_=in_, func=func, **kw), reads, writes)


def tt(k, eng, out, in0, in1, op, reads, writes):
    return k.c.op(eng, lambda h: h.tensor_tensor(out=out, in0=in0, in1=in1, op=op), reads, writes)


def ts(k, eng, out, in0, s1, s2, op0, op1, reads, writes):
    if op1 is None:
        return k.c.op(eng, lambda h: h.tensor_scalar(out=out, in0=in0, scalar1=s1, scalar2=None, op0=op0), reads, writes)
    return k.c.op(eng, lambda h: h.tensor_scalar(out=out, in0=in0, scalar1=s1, scalar2=s2, op0=op0, op1=op1), reads, writes)


def stt(k, out, in0, scalar, in1, op0, op1, reads, writes):
    return k.c.op("dve", lambda h: h.scalar_tensor_tensor(out=out, in0=in0, scalar=scalar, in1=in1, op0=op0, op1=op1),
                  reads, writes)


def cp(k, eng, out, in_, reads, writes):
    if eng == "act":
        return k.c.op("act", lambda h: h.activation(out=out, in_=in_, func=AF.Copy), reads, writes)
    return k.c.op(eng, lambda h: h.tensor_copy(out=out, in_=in_), reads, writes)


def memset(k, eng, ap, val, writes):
    return k.c.op(eng, lambda h: h.memset(ap, val), (), writes)


def recip(k, out, in_, reads, writes):
    return k.c.op("dve", lambda h: h.reciprocal(out=out, in_=in_), reads, writes)


def reduce_add(k, out, in_, reads, writes):
    return k.c.op("dve", lambda h: h.tensor_reduce(out=out, in_=in_, axis=AX.X, op=ALU.add), reads, writes)


def mms(k, lst, reads, writes):
    lst = list(lst)

    def f(h):
        i = None
        for (o, l, r, st, sp) in lst:
            i = h.matmul(o, lhsT=l, rhs=r, start=st, stop=sp)
        return i

    return k.c.op("pe", f, reads, writes)


def trs(k, lst, ident, reads, writes):
    lst = list(lst)

    def f(h):
        i = None
        for (o, a) in lst:
            n = a.shape[0]
            i = h.transpose(out=o, in_=a, identity=ident[0:n, 0:n])
        return i

    return k.c.op("pe", f, reads, writes)


class Scope:
    def __init__(self, k):
        self.k = k
        self.es = ExitStack()
        self.bufs = []

    def __enter__(self):
        self.es.__enter__()
        return self

    def sb(self, name, shape, dt):
        t = self.es.enter_context(self.k.nc.sbuf_tensor(name + f"_{self.k.uid()}", list(shape), dt))
        b = self.k.c.buf(name)
        self.bufs.append(b)
        return T(t, b)

    def __exit__(self, *a):
        self.k.c.barrier()
        self.k.c.retire(self.bufs)
        return self.es.__exit__(*a)


def _bucket(d):
    d = np.asarray(d, dtype=np.int64)
    exact = 16
    df = np.maximum(d.astype(np.float32), np.float32(1.0))
    large = exact + (np.log(df / np.float32(exact)) / np.float32(math.log(2048 / exact))
                     * np.float32(32 - exact)).astype(np.int32)
    large = np.minimum(large, 31)
    return np.where(d < exact, d, large).astype(np.int64)


def _mult(d):
    d = np.asarray(d, dtype=np.int64)
    m = ((d >= 0) & (d <= 128)).astype(np.float32)
    m += ((d >= 0) & (d <= 512) & (d % 4 == 0)).astype(np.float32)
    m += ((d >= 0) & (d <= 2048) & (d % 16 == 0)).astype(np.float32)
    return m


def make_consts():
    cst = {}
    cst["ident"] = np.eye(128, dtype=np.float32)
    d = np.arange(WROW) - 511
    ohm = np.zeros((32, WROW), np.float32)
    dd = np.maximum(d, 0)
    ohm[_bucket(dd), np.arange(WROW)] = _mult(d)
    cst["ohm"] = ohm
    ohs = np.zeros((32, 14, 8, 128), np.float32)
    kr = np.arange(128)
    for t in range(8):
        dA = 128 + t - kr
        v = (dA >= 0) & (dA <= 128)
        ohs[_bucket(np.maximum(dA, 0))[v], 0, t, kr[v]] = 1.0
        for rho in range(4):
            if t % 4 != rho:
                continue
            d2 = 512 + t - rho - 4 * kr
            v = (d2 >= 0) & (d2 <= 512)
            ohs[_bucket(np.maximum(d2, 0))[v], 1 + rho, t, kr[v]] = 1.0
        d3 = 2048 - 16 * kr
        ohs[_bucket(d3), 5 + t, t, kr] = 1.0
        for tp in range(t + 1):
            dn = t - tp
            ohs[_bucket(dn), 13, t, tp] = _mult(dn)
    cst["ohs"] = ohs.reshape(32, 14 * 8 * 128)
    sel = np.zeros((18, 3, 128), np.float32)
    sel[0, 0, :] = 1.0
    sel[1, 1, :] = 1.0
    for b in range(NSS):
        sel[2 + b, 2, b * 8:(b + 1) * 8] = 1.0
    cst["sel"] = sel.reshape(18, 3 * 128)
    invc = np.zeros((128, 4, 15), np.float32)
    for g, w in enumerate((2, 4, 8, 16)):
        for t in range(15):
            invc[:, g, t] = 1.0 / min(w, t + 1)
    cst["invc"] = invc.reshape(128, 60)
    return cst


IN_SPECS = [
    ("xp", [NPS, SEQ, DM]), ("xs", [128, DM]), ("cT", [DM, 18]),
    ("ck", [NSS, WB, AW]), ("cv", [NSS, WB, AW]), ("spool", [NSS, PCTX, AW]),
    ("w_ada", [DM, 6 * DM]), ("b_adaT", [128, 48]), ("b_row", [1, 6 * DM]),
    ("g1T", [128, 8]), ("g2T", [128, 8]), ("w_in", [DM, 2048]), ("gq2", [128, 1]), ("gk_row", [1, HD]),
    ("rel_bias", [32, NH]), ("w_pool", [4, 128, 128]), ("pscT", [128, 4]),
    ("w_out", [DM, DM]), ("w_up", [DM, FF]), ("w_down", [FF, DM]),
    ("ident", [128, 128]), ("ohm", [32, WROW]), ("ohs", [32, 14 * 8 * 128]), ("sel", [18, 3 * 128]),
    ("invc", [128, 60]),
]
OUT_SPECS = [
    ("yp", [NPS, SEQ, DM]), ("ys", [128, DM]), ("kwp", [NPS, SEQ, AW]), ("vwp", [NPS, SEQ, AW]),
    ("pp", [NPS, PCTX, AW]), ("kws", [NSS, WB, AW]), ("vws", [NSS, WB, AW]), ("pso", [NSS, PCTX, AW]),
]


def build_program(stage="full"):
    nc = bass.Bass("TRN2", target_bir_lowering=False)
    k = K()
    k.nc = nc
    k.c = Ctx(nc)
    k._uid = 0

    def uid():
        k._uid += 1
        return k._uid

    k.uid = uid
    k.stage = stage
    c = k.c
    k.din = {}
    for name, shape in IN_SPECS:
        k.din[name] = nc.dram_tensor(name, shape, F32, kind="ExternalInput")
    k.dout = {}
    for name, shape in OUT_SPECS:
        k.dout[name] = T(nc.dram_tensor(name, shape, F32, kind="ExternalOutput"), c.buf("o_" + name))
    k.wscr = T(nc.dram_tensor("wscr", [128 * 8 * WROW], BF16, kind="Internal"), c.buf("wscr"))
    k.w_in_bf = T(nc.dram_tensor("w_in_bf", [DM, 2048], BF16, kind="Internal"), c.buf("w_in_bf"))
    k.w_out_bf = T(nc.dram_tensor("w_out_bf", [DM, DM], BF16, kind="Internal"), c.buf("w_out_bf"))
    k.w_up_bf = T(nc.dram_tensor("w_up_bf", [8, 128, 8 * 512], BF16, kind="Internal"), c.buf("w_up_bf"))
    k.w_down_bf = T(nc.dram_tensor("w_down_bf", [16, 128, 4 * 512], BF16, kind="Internal"), c.buf("w_down_bf"))

    k.ps_t = [nc.alloc_psum_tensor(f"ps{i}", [128, 512], F32) for i in range(8)]
    k.ps_b = [c.buf(f"ps{i}") for i in range(8)]
    k.ps_i = 0

    k.pinned = set()

    def bank(pin=False):
        while k.ps_i in k.pinned:
            k.ps_i = (k.ps_i + 1) % 8
        i = k.ps_i
        k.ps_i = (i + 1) % 8
        if pin:
            k.pinned.add(i)
        t = T(k.ps_t[i], k.ps_b[i])
        t.idx = i
        return t

    def unpin(t):
        k.pinned.discard(t.idx)

    k.bank = bank
    k.unpin = unpin
    k.wcast = c.buf("wcast")
    k.bg = c.buf("bg")
    k.bg.nobar = True

    with Scope(k) as P:
        k.P = P
        setup(k)
        if stage in ("full", "prompt", "p1", "p12"):
            for s in range(NPS):
                prompt_seq(k, s)
                if stage in ("p1", "p12"):
                    break
        if stage in ("full", "sample", "s1"):
            sample_group(k)
    c.emit()
    return nc


def setup(k):
    c, nc, P, din = k.c, k.nc, k.P, k.din
    full = k.stage in ("full", "sample")
    k.bg_done = False
    if k.stage == "sample":
        issue_bg(k)

    k.ident = P.sb("ident", [128, 128], F32)
    k.ones = P.sb("ones", [128, 128], F32)
    k.epsT = P.sb("epsT", [128, 1], F32)
    k.adaT = P.sb("adaT", [128, 5, 8, 18], F32)
    k.gsT = P.sb("gsT", [128, 2, 8, 18], F32)
    k.gate1 = P.sb("gate1", [128, 3, DM], F32)
    k.gq8 = P.sb("gq8", [128, 3], F32)
    k.gkb = P.sb("gkb", [128, HD], F32)
    k.pscT = P.sb("pscT", [128, 4], F32)
    k.invc = P.sb("invc", [128, 60], F32)
    k.WS = P.sb("WS", [128, 112, 8], F32)
    k.wpool = P.sb("wpool", [128, 4, 128], BF16)

    c.dma("sp", k.ident[:], din["ident"].ap(), writes=[k.ident.b])
    memset(k, "dve", k.ones[:], 1.0, [k.ones.b])
    memset(k, "dve", k.epsT[:], EPS, [k.epsT.b])
    c.dma("sp", k.pscT[:], din["pscT"].ap(), writes=[k.pscT.b])
    c.dma("sp", k.invc[:], din["invc"].ap(), writes=[k.invc.b])
    c.dma("sp", k.gkb[:], AP(din["gk_row"], 0, [[0, 128], [1, HD]]), writes=[k.gkb.b])
    c.dma("pool", k.wpool[:], din["w_pool"].ap().rearrange("g c d -> c g d"), writes=[k.wpool.b])

    k.wscr_done = False
    c.dma("pool", k.w_in_bf.t.ap().rearrange("(p a) n -> p (a n)", p=128),
          din["w_in"].ap().rearrange("(p a) n -> p (a n)", p=128), writes=[k.w_in_bf.b])

    with Scope(k) as Sx:
        cTs = Sx.sb("cTs", [128, 8, 18], F32)
        scT = Sx.sb("scT", [128, 8, 18], BF16)
        brow = Sx.sb("brow", [1, 6 * DM], F32)
        g12 = [Sx.sb(f"g12_{i}", [128, 8], F32) for i in range(2)]
        gq2 = Sx.sb("gq2", [128, 1], F32)
        sel = Sx.sb("sel", [18, 3 * 128], F32)
        atms = [Sx.sb(f"atm{i}", [18, DM], F32) for i in range(3)]
        NWA = 2
        wa = [Sx.sb(f"wa{i}", [128, 8, 512], F32) for i in range(NWA)]
        wab = [Sx.sb(f"wab{i}", [128, 8, 512], BF16) for i in range(2)]
        tmp18 = Sx.sb("tmp18", [128, 8, 18], F32)
        rb = Sx.sb("rb", [32, NH], F32)
        expb = Sx.sb("expb", [32, NH], F32)
        expr = Sx.sb("expr", [32, NH, 128], F32)
        ohm = Sx.sb("ohm", [32, WROW], F32)
        ohs = Sx.sb("ohs", [32, 56 * 128], F32)
        wrep = Sx.sb("wrep", [128, NH, WROW], BF16)

        def load_wa(i):
            blk, half = divmod(i, 2)
            c.dma("sp", wa[i % NWA][:], din["w_ada"].ap()[:, blk * 1024 + half * 512:blk * 1024 + (half + 1) * 512]
                  .rearrange("(k p) n -> p k n", p=128), writes=[wa[i % NWA].b])

        c.dma("sp", rb[:], din["rel_bias"].ap(), writes=[rb.b])
        c.dma("sp", ohm[:], din["ohm"].ap(), writes=[ohm.b])
        c.dma("sp", cTs[:], din["cT"].ap().rearrange("(k p) n -> p k n", p=128), writes=[cTs.b])
        c.dma("sp", brow[:], din["b_row"].ap(), writes=[brow.b])
        c.dma("sp", g12[0][:], din["g1T"].ap(), writes=[g12[0].b])
        c.dma("sp", g12[1][:], din["g2T"].ap(), writes=[g12[1].b])
        c.dma("sp", gq2[:], din["gq2"].ap(), writes=[gq2.b])
        c.dma("sp", sel[:], din["sel"].ap(), writes=[sel.b])
        c.dma("sp", ohs[:], din["ohs"].ap()[:, 0:56 * 128], writes=[ohs.b])
        for i in range(NWA):
            load_wa(i)

        memset(k, "dve", k.gq8[:], 0.0, [k.gq8.b])
        ts(k, "dve", k.gq8[:, 0:1], gq2[:], HD ** -0.5, None, ALU.mult, None, [gq2.b], [k.gq8.b])
        ts(k, "dve", k.gq8[0:64, 1:2], gq2[0:64, :], HD ** -0.5, None, ALU.mult, None, [gq2.b], [k.gq8.b])
        ts(k, "dve", k.gq8[64:128, 2:3], gq2[64:128, :], HD ** -0.5, None, ALU.mult, None, [gq2.b], [k.gq8.b])

        act(k, expb[:], rb[:], AF.Exp, [rb.b], [expb.b])
        cp(k, "dve", expr[:], AP(expb, 0, [[NH, 32], [1, NH], [0, 128]]), [expb.b], [expr.b])
        if k.stage != "s1":
            n = 0
            for h in range(NH):
                for blk in range(WROW // 512):
                    bk = k.bank()
                    mms(k, [(bk[:, :], expr[:, h, :], ohm[:, blk * 512:(blk + 1) * 512], True, True)],
                        [expr.b, ohm.b], [bk.b])
                    cp(k, "act" if n % 2 == 0 else "dve", wrep[:, h, blk * 512:(blk + 1) * 512], bk[:, :], [bk.b], [wrep.b])
                    n += 1
            c.dma("pool", k.wscr.t.ap().rearrange("(p f) -> p f", p=128), wrep[:].rearrange("p h w -> p (h w)"),
                  reads=[wrep.b], writes=[k.wscr.b])
        for half in range(2):
            if half == 1:
                c.dma("sp", ohs[:], din["ohs"].ap()[:, 56 * 128:112 * 128], writes=[ohs.b])
            bk = k.bank()
            lst = []
            for i in range(56):
                lst.append((bk[:, i * 8:(i + 1) * 8], ohs[:, i * 128:(i + 1) * 128], expb[:, :], True, True))
            mms(k, lst, [ohs.b, expb.b], [bk.b])
            cp(k, "dve", k.WS[:, half * 56:(half + 1) * 56, :], bk[:, 0:448].rearrange("p (a b) -> p a b", b=8),
               [bk.b], [k.WS.b])

        act(k, scT[:], cTs[:], AF.Silu, [cTs.b], [scT.b])
        kind_of_blk = {0: 0, 1: 1, 3: 2, 4: 3, 5: 4}
        for i in range(12):
            blk, half = divmod(i, 2)
            w = wa[i % NWA]
            wb = wab[i % 2]
            cp(k, "act" if i % 2 == 0 else "dve", wb[:], w[:], [w.b], [wb.b])
            if i + NWA < 12:
                load_wa(i + NWA)
            bk = k.bank()
            lst = [(bk[0:18, :], scT[:, kc, :], wb[:, kc, :], kc == 0, False) for kc in range(8)]
            col0 = blk * 1024 + half * 512
            lst.append((bk[0:18, :], k.ones[0:1, 0:18], brow[0:1, col0:col0 + 512], False, True))
            mms(k, lst, [scT.b, wb.b, k.ones.b, brow.b], [bk.b])
            atm = atms[2] if blk == 2 else atms[blk % 2]
            cp(k, "dve" if half == 0 else "act", atm[0:18, half * 512:(half + 1) * 512], bk[0:18, :], [bk.b], [atm.b])
            if half == 1 and blk != 2:
                kind = kind_of_blk[blk]
                bk = k.bank()
                trs(k, [(bk[:, fcl * 18:(fcl + 1) * 18], atm[0:18, fcl * 128:(fcl + 1) * 128])
                        for fcl in range(8)], k.ident, [atm.b, k.ident.b], [bk.b])
                cp(k, "dve", k.adaT[:, kind, :, :], bk[:, 0:144].rearrange("p (a b) -> p a b", b=18), [bk.b], [k.adaT.b])
        for j, kind in enumerate((1, 3)):
            ts(k, "dve", tmp18[:], k.adaT[:, kind, :, :], 1.0, None, ALU.add, None, [k.adaT.b], [tmp18.b])
            tt(k, "dve", k.gsT[:, j, :, :], tmp18[:], AP(g12[j], 0, [[8, 128], [1, 8], [0, 18]]), ALU.mult,
               [tmp18.b, g12[j].b], [k.gsT.b])
        for G in range(3):
            for half in range(2):
                bk = k.bank()
                mms(k, [(bk[:, :], sel[0:18, G * 128:(G + 1) * 128], atms[2][0:18, half * 512:(half + 1) * 512],
                         True, True)], [sel.b, atms[2].b], [bk.b])
                cp(k, "act", k.gate1[:, G, half * 512:(half + 1) * 512], bk[:, :], [bk.b], [k.gate1.b])
        issue_wscratch(k)


def issue_wscratch(k):
    if k.wscr_done or k.stage not in ("full", "prompt", "sample"):
        return
    k.wscr_done = True
    c, din = k.c, k.din
    c.dma("pool", k.w_out_bf.t.ap().rearrange("(p a) n -> p (a n)", p=128),
          din["w_out"].ap().rearrange("(p a) n -> p (a n)", p=128), writes=[k.w_out_bf.b])
    for ffg in range(8):
        c.dma("pool", k.w_up_bf.t.ap()[ffg].rearrange("p (k n) -> p k n", k=8),
              din["w_up"].ap()[:, ffg * 512:(ffg + 1) * 512].rearrange("(k p) n -> p k n", p=128), reads=[k.wcast],
              sembuf=k.w_up_bf.b)
    for half in range(2):
        for ffg in range(8):
            c.dma("pool", k.w_down_bf.t.ap()[half * 8 + ffg].rearrange("p (j n) -> p j n", j=4),
                  din["w_down"].ap()[ffg * 512:(ffg + 1) * 512, half * 512:(half + 1) * 512]
                  .rearrange("(j p) n -> p j n", p=128), reads=[k.wcast], sembuf=k.w_down_bf.b)
    for tb in (k.w_up_bf.b, k.w_down_bf.b):
        tb.w = Ev(tb.dsem, tb.dcount, "dma", tb.dsid)


def issue_bg(k, after=None):
    if k.bg_done or k.stage not in ("full", "sample"):
        return
    k.bg_done = True
    c, din = k.c, k.din
    if after is not None:
        c._wait(c.engs["pool"], after.w)
    for b in range(NSS):
        for src, dst in ((din["ck"], k.dout["kws"]), (din["cv"], k.dout["vws"])):
            c.dma("pool", dst.t.ap()[b, 0:WB - TSEQ, :].rearrange("(r f) c -> r (f c)", f=4),
                  src.ap()[b, TSEQ:WB, :].rearrange("(r f) c -> r (f c)", f=4), reads=[k.bg])
    c.dma("pool", k.dout["pso"].t.ap()[:, 0:PCTX - TSEQ, :], din["spool"].ap()[:, TSEQ:PCTX, :], reads=[k.bg])


def rms_stats(k, x, junk, st, col):
    act(k, junk[:], x[:], AF.Square, [x.b], [junk.b, st.b], accum_out=st[:, col:col + 1])
    act(k, st[:, col + 1:col + 2], st[:, col:col + 1], AF.Sqrt, [st.b, k.epsT.b], [st.b], bias=k.epsT[:], scale=1.0 / DM)
    recip(k, st[:, col + 2:col + 3], st[:, col + 1:col + 2], [st.b], [st.b])


def head_norm(k, bk, sq, rs, out, extra_gain=None):
    act(k, sq[:], bk[:, :], AF.Square, [bk.b], [sq.b])
    reduce_add(k, rs[:, 0:8], sq[:].rearrange("p (h e) -> p h e", e=HD), [sq.b], [rs.b])
    act(k, rs[:, 8:16], rs[:, 0:8], AF.Sqrt, [rs.b, k.epsT.b], [rs.b], bias=k.epsT[:], scale=1.0 / HD)
    recip(k, rs[:, 16:24], rs[:, 8:16], [rs.b], [rs.b])
    tt(k, "dve", out[:].rearrange("p (h e) -> p h e", e=HD), bk[:, :].rearrange("p (h e) -> p h e", e=HD),
       AP(rs, 16, [[24, 128], [1, 8], [0, HD]]), ALU.mult, [bk.b, rs.b], [out.b])
    if extra_gain is not None:
        tt(k, "dve", out[:].rearrange("p (h e) -> p h e", e=HD), out[:].rearrange("p (h e) -> p h e", e=HD),
           AP(extra_gain, 0, [[HD, 128], [0, 8], [1, HD]]), ALU.mult, [out.b, extra_gain.b], [out.b])


def evac_hT(k, bk, hT, cc, ncols, kind_scale, kind_shift, col, sample, tmp=None):
    if not sample:
        act(k, hT[:, cc, 0:ncols], bk[:, 0:ncols], AF.Identity, [bk.b, k.gsT.b, k.adaT.b], [hT.b],
            scale=k.gsT[:, kind_scale, cc, col:col + 1], bias=k.adaT[:, kind_shift, cc, col:col + 1])
    else:
        gs_off = ((kind_scale * 8 + cc) * 18 + 2)
        sh_off = ((kind_shift * 8 + cc) * 18 + 2)
        tt(k, "dve", tmp[:, 0:128].rearrange("p (b t) -> p b t", t=TSEQ), bk[:, 0:128].rearrange("p (b t) -> p b t", t=TSEQ),
           AP(k.gsT, gs_off, [[2 * 8 * 18, 128], [1, NSS], [0, TSEQ]]), ALU.mult, [bk.b, k.gsT.b], [tmp.b])
        tt(k, "dve", hT[:, cc, 0:128].rearrange("p (b t) -> p b t", t=TSEQ), tmp[:, 0:128].rearrange("p (b t) -> p b t", t=TSEQ),
           AP(k.adaT, sh_off, [[5 * 8 * 18, 128], [1, NSS], [0, TSEQ]]), ALU.add, [tmp.b, k.adaT.b], [hT.b])


def mixer_in_group(k, S, g, ntiles, xsrc, col, sample, R):
    c = k.c
    W = S["w_in_bf"]
    hT = S["hT"][g % 2]
    ncols = ntiles * 128
    xts = []
    for j in range(ntiles):
        T0 = g * 4 + j
        xt = S["xt"][T0 % 4]
        c.dma("sp", xt[:], xsrc(T0), writes=[xt.b])
        st = S["st"][T0 % 4]
        rms_stats(k, xt, S["junk"], st, 0)
        ts(k, "dve", xt[:], xt[:], st[:, 2:3], None, ALU.mult, None, [xt.b, st.b], [xt.b])
        xts.append(xt)
    for cc in range(8):
        bk = k.bank()
        trs(k, [(bk[:, j * 128:(j + 1) * 128], xts[j][:, cc * 128:(cc + 1) * 128]) for j in range(ntiles)], k.ident,
            [x.b for x in xts] + [k.ident.b], [bk.b])
        evac_hT(k, bk, hT, cc, ncols, 0, 0, col, sample, tmp=S["tmpf"])
    def proj(j):
        T0 = g * 4 + j
        tok = slice(j * 128, (j + 1) * 128)
        bq, bkk, bv = k.bank(), k.bank(), k.bank()
        lst = []
        for kc in range(8):
            lst.append((bq[:, :], hT[:, kc, tok], W[:, kc, 0:512], kc == 0, kc == 7))
            lst.append((bkk[:, :], hT[:, kc, tok], W[:, kc, 512:1024], kc == 0, kc == 7))
        mms(k, lst, [hT.b] + W.bs, [bq.b, bkk.b])
        mms(k, [(bv[:, :], hT[:, kc, tok], W[:, kc, 1024:1536], kc == 0, kc == 7) for kc in range(8)], [hT.b] + W.bs, [bv.b])
        qn = S["qn"][T0 % 2]
        head_norm(k, bq, S["sq"], S["rs"][0], qn)
        kn = S["kn"][T0 % 2]
        head_norm(k, bkk, S["sq2"], S["rs"][1], kn, extra_gain=k.gkb)
        R["store_k"](T0, kn)
        vo = S["vo"][T0 % 2]
        cp(k, "act", vo[:], bv[:, :], [bv.b], [vo.b])
        R["store_v"](T0, vo)

    def trsp(j):
        T0 = g * 4 + j
        qn = S["qn"][T0 % 2]
        kn = S["kn"][T0 % 2]
        bt = k.bank()
        trs(k, [(bt[:, p * 128:(p + 1) * 128], qn[:, p * 128:(p + 1) * 128]) for p in range(4)], k.ident,
            [qn.b, k.ident.b], [bt.b])
        if R.get("qmask"):
            for par in range(2):
                act(k, R["QT"][:, par, :, T0 * 128:(T0 + 1) * 128], bt[:, :].rearrange("p (a t) -> p a t", t=128), AF.Copy,
                    [bt.b, k.gq8.b], [R["QT"].b], scale=k.gq8[:, 1 + par:2 + par])
        else:
            act(k, R["QT"][:, :, T0 * 128:(T0 + 1) * 128], bt[:, :].rearrange("p (a t) -> p a t", t=128), AF.Copy,
                [bt.b, k.gq8.b], [R["QT"].b], scale=k.gq8[:, 0:1])
        bt2 = k.bank()
        trs(k, [(bt2[:, p * 128:(p + 1) * 128], kn[:, p * 128:(p + 1) * 128]) for p in range(4)], k.ident,
            [kn.b, k.ident.b], [bt2.b])
        cp(k, "dve", R["KT"][:, :, T0 * 128:(T0 + 1) * 128], bt2[:, :].rearrange("p (a t) -> p a t", t=128),
           [bt2.b], [R["KT"].b])

    proj(0)
    for j in range(ntiles):
        if j + 1 < ntiles:
            proj(j + 1)
        trsp(j)
    uT = S["uT"][g % 2]
    for uc in range(4):
        bk = k.bank()
        mms(k, [(bk[:, 0:ncols], W[:, kc, 1536 + uc * 128:1536 + (uc + 1) * 128], hT[:, kc, 0:ncols], kc == 0, kc == 7)
                for kc in range(8)], [hT.b] + W.bs, [bk.b])
        R["store_uT"](g, uc, bk, uT)
    return hT


def pool_adds(k, S, uT, ncols, lead, done):
    pa, pb, s4 = S["pa"], S["pb"], S["s4"]
    n = lead + ncols
    u = uT
    for uc in range(4):
        if uc == 0:
            tt(k, "pool", s4[:, 0:ncols], u[:, 0, lead:n], u[:, 0, lead - 1:n - 1], ALU.add, [u.b], [s4.b])
            done(uc)
            continue
        tt(k, "pool", pa[:, 1:n], u[:, uc, 1:n], u[:, uc, 0:n - 1], ALU.add, [u.b], [pa.b])
        if uc == 1:
            tt(k, "pool", s4[:, 0:ncols], pa[:, lead:n], pa[:, lead - 2:n - 2], ALU.add, [pa.b], [s4.b])
            done(uc)
            continue
        tt(k, "pool", pb[:, 3:n], pa[:, 3:n], pa[:, 1:n - 2], ALU.add, [pa.b], [pb.b])
        if uc == 2:
            tt(k, "pool", s4[:, 0:ncols], pb[:, lead:n], pb[:, lead - 4:n - 4], ALU.add, [pb.b], [s4.b])
            done(uc)
            continue
        tt(k, "pool", pa[:, 7:n], pb[:, 7:n], pb[:, 3:n - 4], ALU.add, [pb.b], [pa.b])
        tt(k, "pool", s4[:, 0:ncols], pa[:, lead:n], pa[:, lead - 8:n - 8], ALU.add, [pa.b], [s4.b])
        done(uc)


def prompt_seq(k, s):
    c, nc, din = k.c, k.nc, k.din
    with Scope(k) as Q0:
      R = {}
      R["poolT"] = Q0.sb("poolT", [128, 4, SEQ], BF16)
      arena = Q0.sb("arena", [128, 4, SEQ], BF16)
      R["attnT"] = arena
      R["qmask"] = True
      with Scope(k) as Q:
        R["QT"] = Q.sb("QT", [128, 2, 4, SEQ], BF16)
        R["KT"] = Q.sb("KT", [128, 4, SEQ], BF16)
        R["Vx"] = Q.sb("Vx", [128, 16, 4, 192], BF16)
        Vx = R["Vx"]
        memset(k, "pool", Vx[:], 0.0, [Vx.b])
        memset(k, "pool", Vx[:, :, :, 64:65], 1.0, [Vx.b])

        with Scope(k) as S1:
            S = {}
            wB = S1.sb("w_in_hi", [128, 4, 2048], BF16)
            S["w_in_bf"] = WParts([arena, wB], 4)
            c.dma("sp", arena[:], k.w_in_bf.t.ap()[0:512, :].rearrange("(k p) n -> p k n", p=128), reads=[k.w_in_bf.b],
                  writes=[arena.b])
            c.dma("sp", wB[:], k.w_in_bf.t.ap()[512:1024, :].rearrange("(k p) n -> p k n", p=128), reads=[k.w_in_bf.b],
                  writes=[wB.b])
            S["xt"] = [S1.sb(f"xt{i}", [128, DM], F32) for i in range(4)]
            S["st"] = [S1.sb(f"st{i}", [128, 4], F32) for i in range(4)]
            S["junk"] = S1.sb("junk", [128, DM], BF16)
            S["hT"] = [S1.sb(f"hT{i}", [128, 8, 512], BF16) for i in range(1)] * 2
            S["tmpf"] = None
            S["sq"] = S1.sb("sq", [128, 512], F32)
            S["sq2"] = S1.sb("sq2", [128, 512], F32)
            S["rs"] = [S1.sb(f"rs{i}", [128, 24], F32) for i in range(2)]
            S["qn"] = [S1.sb(f"qn{i}", [128, 512], F32) for i in range(2)]
            S["kn"] = [S1.sb(f"kn{i}", [128, 512], F32) for i in range(2)]
            S["vo"] = [S1.sb(f"vo{i}", [128, 512], F32) for i in range(2)]
            S["uT"] = [S1.sb(f"uT{i}", [128, 4, 16 + 512], F32) for i in range(1)] * 2
            carry = S1.sb("carry", [128, 4, 16], F32)
            S["pa"] = S1.sb("pa", [128, 528], F32)
            S["pb"] = S1.sb("pb", [128, 528], F32)
            S["s4"] = S1.sb("s4", [128, 512], F32)
            S["pooled"] = [S1.sb(f"pooled{i}", [128, 512], BF16) for i in range(2)]
            S["t15"] = S1.sb("t15", [128, 15], F32)
            utm = S["qn"][0]
            memset(k, "pool", carry[:], 0.0, [carry.b])

            def store_k(T0, kn):
                c.dma("sp", k.dout["kwp"][s, T0 * 128:(T0 + 1) * 128, :], kn[:], reads=[kn.b])

            def store_v(T0, vo):
                c.dma("sp", k.dout["vwp"][s, T0 * 128:(T0 + 1) * 128, :], vo[:], reads=[vo.b])
                tt_src = vo[:].rearrange("p (a q e) -> p a q e", q=2, e=HD)
                cp(k, "pool", Vx[:, T0, :, 0:64], tt_src[:, :, 0, :], [vo.b], [Vx.b])
                cp(k, "pool", Vx[:, T0, :, 128:192], tt_src[:, :, 1, :], [vo.b], [Vx.b])

            def store_uT(g, uc, bk, uT):
                cp(k, "act" if uc % 2 == 0 else "dve", uT[:, uc, 16:528], bk[:, :], [bk.b], [uT.b])

            R["store_k"], R["store_v"], R["store_uT"] = store_k, store_v, store_uT

            for g in range(4):
                uT = S["uT"][0]
                cp(k, "pool", uT[:, :, 0:16], carry[:], [carry.b], [uT.b])
                hT = mixer_in_group(k, S, g, 4, lambda T0: din["xp"][s, T0 * 128:(T0 + 1) * 128, :], s, False, R)
                cp(k, "pool", carry[:], uT[:, :, 512:528], [uT.b], [carry.b])

                def done(uc, g=g, uT=uT):
                    w = (2, 4, 8, 16)[uc]
                    s4 = S["s4"]
                    pooled = S["pooled"][uc % 2]
                    stt(k, pooled[:], s4[:], 1.0 / w, uT[:, uc, 16:528], ALU.mult, ALU.subtract, [s4.b, uT.b], [pooled.b])
                    if g == 0:
                        t15 = S["t15"]
                        tt(k, "dve", t15[:], s4[:, 0:15], k.invc[:, uc * 15:(uc + 1) * 15], ALU.mult,
                           [s4.b, k.invc.b], [t15.b])
                        tt(k, "dve", pooled[:, 0:15], t15[:], uT[:, uc, 16:31], ALU.subtract, [t15.b, uT.b], [pooled.b])
                    bk = k.bank()
                    mms(k, [(bk[:, :], k.wpool[:, uc, :], pooled[:], True, True)], [k.wpool.b, pooled.b], [bk.b])
                    act(k, R["poolT"][:, uc, g * 512:(g + 1) * 512], bk[:, :], AF.Copy, [bk.b, k.pscT.b], [R["poolT"].b],
                        scale=k.pscT[:, uc:uc + 1])

                pool_adds(k, S, uT, 512, 16, done)
                if g == 3:
                    bk = k.bank()
                    W = S["w_in_bf"]
                    mms(k, [(bk[:, :], hT[:, kc, 384:512], W[:, kc, 1536:2048], kc == 0, kc == 7) for kc in range(8)],
                        [hT.b] + W.bs, [bk.b])
                    cp(k, "dve", utm[:], bk[:, :], [bk.b], [utm.b])
                    c.dma("sp", k.dout["pp"][s, :, :], utm[128 - PCTX:128, :], reads=[utm.b])
        if k.stage == "p1":
            return

        with Scope(k) as S2:
            G = S2.sb("G", [128, NH, GW], BF16)
            for h in range(NH):
                c.dma("sp", G[:, h, :], AP(k.wscr, 127 + h * WROW, [[8 * WROW - 1, 128], [1, GW]]),
                      reads=[k.wscr.b], writes=[G.b])
            issue_bg(k, after=G.b)
            NPB = 5
            pE = [S2.sb(f"pE{i}", [128, 512], BF16) for i in range(NPB)]
            pT = [S2.sb(f"pT{i}", [128, 512], BF16) for i in range(NPB)]
            rden = [S2.sb(f"rden{i}", [128, 1024], F32) for i in range(2)]
            rhl = [S2.sb(f"rhl{i}", [128, 1024], BF16) for i in range(2)]
            osb = [S2.sb(f"osb{i}", [128, 512], F32) for i in range(2)]
            ones_bf = S2.sb("ones_bf2", [128, 128], BF16)
            memset(k, "dve", ones_bf[:], 1.0, [ones_bf.b])
            QT, KT, attnT = R["QT"], R["KT"], R["attnT"]
            steps = []
            for pair in range(4):
                for par in range(2):
                    for qb in range(4):
                        nkt = 4 * qb + 4
                        for kt in range(nkt):
                            steps.append((pair, par, qb, kt, nkt))
            LAG = 2
            bos = {}
            deferred = []
            gcnt = [0]

            def front(idx):
                pair, par, qb, kt, nkt = steps[idx]
                h = 2 * pair + par
                prt = slice(par * 64, par * 64 + 64)
                if kt == 0:
                    bos[(pair, par, qb)] = k.bank(pin=True)
                i = kt - 4 * qb
                q0 = max(0, i) * 128
                xoff = 512 * qb - 128 * kt + 384
                bs = k.bank()
                mms(k, [(bs[:, q0:512], KT[:, pair, kt * 128:(kt + 1) * 128],
                         QT[:, par, pair, qb * 512 + q0:(qb + 1) * 512], True, True)], [KT.b, QT.b], [bs.b])
                e = pE[idx % NPB]
                p = pT[idx % NPB]
                act(k, e[:, q0:512], bs[:, q0:512], AF.Exp, [bs.b], [e.b])
                tt(k, "dve", p[:, q0:512], e[:, q0:512], G[:, h, xoff + q0:xoff + 512], ALU.mult, [e.b, G.b], [p.b])

            def back(idx):
                pair, par, qb, kt, nkt = steps[idx]
                prt = slice(par * 64, par * 64 + 64)
                bo = bos[(pair, par, qb)]
                p = pT[idx % NPB]
                i = kt - 4 * qb
                q0 = max(0, i) * 128
                vcol = 0 if par == 0 else 64
                mms(k, [(bo[:, q0:512], Vx[:, kt, pair, vcol:vcol + 128], p[:, q0:512], kt == 0, kt == nkt - 1)],
                    [Vx.b, p.b], [bo.b])
                if kt == nkt - 1:
                    dr = 64 if par == 0 else 0
                    gcnt[0] += 1
                    rd = rden[gcnt[0] % 2]
                    ob = osb[gcnt[0] % 2]
                    rh = rhl[gcnt[0] % 2]
                    drs = slice(dr, dr + 1)
                    act(k, rd[drs, 0:512], bo[drs, :], AF.Ln, [bo.b], [rd.b])
                    act(k, rd[drs, 512:1024], rd[drs, 0:512], AF.Exp, [rd.b], [rd.b], scale=-1.0)
                    cp(k, "dve", rh[drs, 0:512], rd[drs, 512:1024], [rd.b], [rh.b])
                    tt(k, "dve", rh[drs, 512:1024], rd[drs, 512:1024], rh[drs, 0:512], ALU.subtract, [rd.b, rh.b], [rh.b])
                    cp(k, "act", ob[prt, :], bo[prt, :], [bo.b], [ob.b])

                    def epi(pair=pair, par=par, qb=qb, prt=prt, drs=drs, rh=rh, ob=ob, bo=bo):
                        bb = k.bank()
                        mms(k, [(bb[:, :], ones_bf[drs, :], rh[drs, 0:512], True, False),
                                (bb[:, :], ones_bf[drs, :], rh[drs, 512:1024], False, True)], [ones_bf.b, rh.b], [bb.b])
                        tt(k, "dve", attnT[prt, pair, qb * 512:(qb + 1) * 512], ob[prt, :], bb[prt, :], ALU.mult,
                           [ob.b, bb.b], [attnT.b])
                        k.unpin(bo)

                    deferred.append((idx + LAG + 3, epi))

            nst = len(steps)
            for idx in range(nst + LAG + 8):
                if idx < nst:
                    front(idx)
                if LAG <= idx < nst + LAG:
                    back(idx - LAG)
                for d in [d for d in deferred if d[0] <= idx]:
                    d[1]()
                    deferred.remove(d)
            assert not deferred
        if k.stage == "p12":
            return
      if k.stage in ("p1", "p12"):
          return

      with Scope(k) as S3:
          S = ffn_alloc(k, S3)
          tail_seq(k, S, 4, 4, lambda T0: din["xp"][s, T0 * 128:(T0 + 1) * 128, :],
                   lambda T0: k.dout["yp"][s, T0 * 128:(T0 + 1) * 128, :], s, False,
                   lambda cc, T0: (R["attnT"] if cc < 4 else R["poolT"]), s)


def ffn_alloc(k, S3):
    c = k.c
    S = {}
    S["w_out_bf"] = S3.sb("w_out_sb", [128, 8, DM], BF16)
    c.dma("sp", S["w_out_bf"][:], k.w_out_bf.t.ap().rearrange("(k p) n -> p k n", p=128), reads=[k.w_out_bf.b],
          writes=[S["w_out_bf"].b])
    S["wu"] = [S3.sb(f"wu{i}", [128, 8, 512], BF16) for i in range(2)]
    S["wd"] = [S3.sb(f"wd{i}", [128, 4, 512], BF16) for i in range(2)]
    S["xt"] = [S3.sb(f"xt3_{i}", [128, DM], F32) for i in range(2)]
    S["x1"] = [S3.sb(f"x1_{i}", [128, DM], F32) for i in range(8)]
    S["xn"] = [S3.sb(f"xn2_{i}", [128, DM], F32) for i in range(4)]
    S["st"] = [S3.sb(f"st3_{i}", [128, 4], F32) for i in range(8)]
    S["junk"] = S3.sb("junk3", [128, DM], BF16)
    S["hT2"] = S3.sb("hT2", [128, 8, 512], BF16)
    S["fT"] = S3.sb("fT", [128, 32, 512], BF16)
    S["sqf"] = [S3.sb(f"sqf{i}", [128, 512], F32) for i in range(2)]
    S["yT"] = [S3.sb(f"yT{i}", [128, 512], F32) for i in range(4)]
    S["tmpf"] = S3.sb("tmpf3", [128, 512], F32)
    return S


def tail_seq(k, S, ng, ntiles, xsrc, ydst, col, sample, mixsrc, G):
    c = k.c
    ncols = ntiles * 128
    Wo = S["w_out_bf"]
    hT2 = S["hT2"]
    fT = S["fT"]

    def x1_of(g, j):
        return S["x1"][(g % 2) * 4 + j]

    def head_a(g):
        for j in range(ntiles):
            T0 = g * 4 + j
            tok = slice(T0 * 128, (T0 + 1) * 128)
            xt = S["xt"][T0 % 2]
            c.dma("sp", xt[:], xsrc(T0), writes=[xt.b])
            b0, b1 = k.bank(), k.bank()
            lst = []
            rd = [Wo.b]
            for cc in range(8):
                m = mixsrc(cc, T0)
                if m.b not in rd:
                    rd.append(m.b)
                lst.append((b0[:, :], m[:, cc % 4, tok], Wo[:, cc, 0:512], cc == 0, cc == 7))
                lst.append((b1[:, :], m[:, cc % 4, tok], Wo[:, cc, 512:1024], cc == 0, cc == 7))
            mms(k, lst, rd, [b0.b, b1.b])
            x1 = x1_of(g, j)
            xn = S["xn"][j]
            for half, bb in enumerate((b0, b1)):
                hs = slice(half * 512, (half + 1) * 512)
                tt(k, "dve", xn[:, hs], bb[:, :], k.gate1[:, G, hs], ALU.mult, [bb.b, k.gate1.b], [xn.b])
            tt(k, "pool", x1[:], xn[:], xt[:], ALU.add, [xn.b, xt.b], [x1.b])
            st = S["st"][(g % 2) * 4 + j]
            rms_stats(k, x1, S["junk"], st, 0)
            ts(k, "dve", xn[:], x1[:], st[:, 2:3], None, ALU.mult, None, [x1.b, st.b], [xn.b])

    def head_b(g):
        xns = [S["xn"][j] for j in range(ntiles)]
        for cc in range(8):
            bk = k.bank()
            trs(k, [(bk[:, j * 128:(j + 1) * 128], xns[j][:, cc * 128:(cc + 1) * 128]) for j in range(ntiles)], k.ident,
                [x.b for x in xns] + [k.ident.b], [bk.b])
            evac_hT(k, bk, hT2, cc, ncols, 1, 2, col, sample, tmp=S["tmpf"])

    def up(g):
        for ffg in range(8):
            wu = S["wu"][ffg % 2]
            c.dma("sp", wu[:].rearrange("p k n -> p (k n)"), k.w_up_bf.t.ap()[ffg], reads=[k.w_up_bf.b], writes=[wu.b])
            for fj in range(4):
                fc = ffg * 4 + fj
                bk = k.bank()
                mms(k, [(bk[:, 0:ncols], wu[:, kc, fj * 128:(fj + 1) * 128], hT2[:, kc, 0:ncols], kc == 0, kc == 7)
                        for kc in range(8)], [wu.b, hT2.b], [bk.b])
                sq = S["sqf"][fc % 2]
                act(k, sq[:, 0:ncols], bk[:, 0:ncols], AF.Square, [bk.b], [sq.b])
                stt(k, fT[:, fc, 0:ncols], bk[:, 0:ncols], 0.0, sq[:, 0:ncols], ALU.is_gt, ALU.mult, [bk.b, sq.b], [fT.b])

    def down(g, half):
        accs = [k.bank(pin=True) for _ in range(4)]
        for ffg in range(8):
            wd = S["wd"][(half * 8 + ffg) % 2]
            c.dma("sp", wd[:].rearrange("p j n -> p (j n)"), k.w_down_bf.t.ap()[half * 8 + ffg], reads=[k.w_down_bf.b],
                  writes=[wd.b])
            lst = []
            for dc in range(4):
                for fj in range(4):
                    fc = ffg * 4 + fj
                    lst.append((accs[dc][:, 0:ncols], wd[:, fj, dc * 128:(dc + 1) * 128], fT[:, fc, 0:ncols],
                                ffg == 0 and fj == 0, ffg == 7 and fj == 3))
            mms(k, lst, [wd.b, fT.b], [a.b for a in accs])
        for dc in range(4):
            dcg = half * 4 + dc
            yT = S["yT"][dc]
            if not sample:
                act(k, yT[:, 0:ncols], accs[dc][:, 0:ncols], AF.Copy, [accs[dc].b, k.adaT.b], [yT.b],
                    scale=k.adaT[:, 4, dcg, col:col + 1])
            else:
                off = ((4 * 8 + dcg) * 18 + 2)
                tt(k, "dve", yT[:, 0:128].rearrange("p (b t) -> p b t", t=TSEQ),
                   accs[dc][:, 0:128].rearrange("p (b t) -> p b t", t=TSEQ),
                   AP(k.adaT, off, [[5 * 8 * 18, 128], [1, NSS], [0, TSEQ]]), ALU.mult, [accs[dc].b, k.adaT.b], [yT.b])
        for a in accs:
            k.unpin(a)
        for j in range(ntiles):
            bk = k.bank()
            trs(k, [(bk[:, dc * 128:(dc + 1) * 128], S["yT"][dc][:, j * 128:(j + 1) * 128]) for dc in range(4)], k.ident,
                [S["yT"][dc].b for dc in range(4)] + [k.ident.b], [bk.b])
            hs = slice(half * 512, (half + 1) * 512)
            x1 = x1_of(g, j)
            tt(k, "dve", x1[:, hs], bk[:, :], x1[:, hs], ALU.add, [bk.b, x1.b], [x1.b])
            if half == 1:
                c.dma("sp", ydst(g * 4 + j), x1[:], reads=[x1.b])

    head_a(0)
    head_b(0)
    for g in range(ng):
        up(g)
        if g + 1 < ng:
            head_a(g + 1)
        down(g, 0)
        if g + 1 < ng:
            head_b(g + 1)
        down(g, 1)


def sample_group(k):
    c, nc, din = k.c, k.nc, k.din
    with Scope(k) as Q:
        R = {}
        R["QT"] = Q.sb("QTs", [128, 4, 128], BF16)
        R["KT"] = Q.sb("KTs", [128, 4, 128], BF16)
        R["poolT"] = Q.sb("poolTs", [128, 4, 128], BF16)
        R["attnT"] = Q.sb("attnTs", [128, 4, 128], BF16)
        uTs = Q.sb("uTs", [128, 4, NSS, 24], F32)
        vnew = Q.sb("vnew", [8, NSS, AW], BF16)
        utm = Q.sb("utm_s", [128, 512], F32)

        with Scope(k) as S1:
            S = {}
            wS = S1.sb("w_in_bf_s", [128, 8, 2048], BF16)
            S["w_in_bf"] = WParts([wS], 8)
            c.dma("sp", wS[:], k.w_in_bf.t.ap().rearrange("(k p) n -> p k n", p=128), reads=[k.w_in_bf.b], writes=[wS.b])
            S["xt"] = [S1.sb(f"xts{i}", [128, DM], F32) for i in range(4)]
            S["st"] = [S1.sb(f"sts{i}", [128, 4], F32) for i in range(4)]
            S["junk"] = S1.sb("junks", [128, DM], BF16)
            S["hT"] = [S1.sb(f"hTs{i}", [128, 8, 512], BF16) for i in range(1)] * 2
            S["tmpf"] = S1.sb("tmpfs", [128, 512], F32)
            S["sq"] = S1.sb("sqs", [128, 512], F32)
            S["sq2"] = S1.sb("sq2s", [128, 512], F32)
            S["rs"] = [S1.sb(f"rss{i}", [128, 24], F32) for i in range(2)]
            S["qn"] = [S1.sb(f"qns{i}", [128, 512], F32) for i in range(2)]
            S["kn"] = [S1.sb(f"kns{i}", [128, 512], F32) for i in range(2)]
            S["vo"] = [S1.sb(f"vos{i}", [128, 512], F32) for i in range(2)]
            S["uT"] = [None, None]
            spt = [S1.sb(f"spt{i}", [120, 512], F32) for i in range(2)]

            memset(k, "pool", uTs[:], 0.0, [uTs.b])
            for i in range(2):
                c.dma("sp", spt[i][:], din["spool"].ap()[i * 8:(i + 1) * 8, :, :].rearrange("b r c -> (b r) c"),
                      writes=[spt[i].b])
                bk = k.bank()
                trs(k, [(bk[:, uc * 120:(uc + 1) * 120], spt[i][:, uc * 128:(uc + 1) * 128]) for uc in range(4)], k.ident,
                    [spt[i].b, k.ident.b], [bk.b])
                cp(k, "dve", uTs[:, :, i * 8:(i + 1) * 8, 1:16],
                   bk[:, 0:480].rearrange("p (u b r) -> p u b r", u=4, b=8), [bk.b], [uTs.b])

            def store_k(T0, kn):
                for b in range(NSS):
                    c.dma("sp", k.dout["kws"][b, WB - TSEQ:WB, :], kn[b * 8:(b + 1) * 8, :], reads=[kn.b])

            def store_v(T0, vo):
                for b in range(NSS):
                    c.dma("sp", k.dout["vws"][b, WB - TSEQ:WB, :], vo[b * 8:(b + 1) * 8, :], reads=[vo.b],
                          writes=[k.dout["vws"].b])
                c.dma("pool", vnew[:], k.dout["vws"].t.ap()[:, WB - TSEQ:WB, :].rearrange("b t c -> t b c"),
                      reads=[k.dout["vws"].b], writes=[vnew.b])

            def store_uT(g, uc, bk, uT):
                cp(k, "act" if uc % 2 == 0 else "dve", uTs[:, uc, :, 16:24],
                   bk[:, 0:128].rearrange("p (b t) -> p b t", t=TSEQ), [bk.b], [uTs.b])

            R["store_k"], R["store_v"], R["store_uT"] = store_k, store_v, store_uT
            hT = mixer_in_group(k, S, 0, 1, lambda T0: din["xs"].ap(), 0, True, R)
            bk = k.bank()
            W = S["w_in_bf"]
            mms(k, [(bk[:, :], hT[:, kc, 0:128], W[:, kc, 1536:2048], kc == 0, kc == 7) for kc in range(8)],
                [hT.b] + W.bs, [bk.b])
            cp(k, "dve", utm[:], bk[:, :], [bk.b], [utm.b])
            for b in range(NSS):
                c.dma("sp", k.dout["pso"][b, PCTX - TSEQ:PCTX, :], utm[b * 8:(b + 1) * 8, :], reads=[utm.b])

            pa = S1.sb("pas", [128, NSS, 24], F32)
            pb = S1.sb("pbs", [128, NSS, 24], F32)
            s4 = S1.sb("s4s", [128, 4, NSS, 8], F32)
            pooled = S1.sb("pooleds", [128, 4, NSS * 8], BF16)
            n = 24
            u = uTs
            tt(k, "pool", s4[:, 0, :, :], u[:, 0, :, 16:n], u[:, 0, :, 15:n - 1], ALU.add, [u.b], [s4.b])
            for uc in (1, 2, 3):
                tt(k, "pool", pa[:, :, 1:n], u[:, uc, :, 1:n], u[:, uc, :, 0:n - 1], ALU.add, [u.b], [pa.b])
                if uc == 1:
                    tt(k, "pool", s4[:, 1, :, :], pa[:, :, 16:n], pa[:, :, 14:n - 2], ALU.add, [pa.b], [s4.b])
                    continue
                tt(k, "pool", pb[:, :, 3:n], pa[:, :, 3:n], pa[:, :, 1:n - 2], ALU.add, [pa.b], [pb.b])
                if uc == 2:
                    tt(k, "pool", s4[:, 2, :, :], pb[:, :, 16:n], pb[:, :, 12:n - 4], ALU.add, [pb.b], [s4.b])
                    continue
                tt(k, "pool", pa[:, :, 7:n], pb[:, :, 7:n], pb[:, :, 3:n - 4], ALU.add, [pb.b], [pa.b])
                tt(k, "pool", s4[:, 3, :, :], pa[:, :, 16:n], pa[:, :, 8:n - 8], ALU.add, [pa.b], [s4.b])
            for uc, w in enumerate((2, 4, 8, 16)):
                stt(k, pooled[:, uc, :].rearrange("p (b t) -> p b t", t=TSEQ), s4[:, uc, :, :], 1.0 / w, u[:, uc, :, 16:24],
                    ALU.mult, ALU.subtract, [s4.b, u.b], [pooled.b])
            for uc in range(4):
                bk = k.bank()
                mms(k, [(bk[:, 0:128], k.wpool[:, uc, :], pooled[:, uc, :], True, True)], [k.wpool.b, pooled.b], [bk.b])
                act(k, R["poolT"][:, uc, :], bk[:, 0:128], AF.Copy, [bk.b, k.pscT.b], [R["poolT"].b],
                    scale=k.pscT[:, uc:uc + 1])
        if k.stage == "s1":
            return

        with Scope(k) as S2:
            sample_attention(k, S2, R, vnew)

        with Scope(k) as S3:
            S = ffn_alloc(k, S3)
            tail_seq(k, S, 1, 1, lambda T0: din["xs"].ap(), lambda T0: k.dout["ys"].t.ap(), 0, True,
                     lambda cc, T0: (R["attnT"] if cc < 4 else R["poolT"]), 2)


def sample_attention(k, S2, R, vnew):
    c, din = k.c, k.din
    QT, KT, attnT = R["QT"], R["KT"], R["attnT"]
    qbd = S2.sb("qbd", [128, 4, NSS, 16], BF16)
    parm = S2.sb("parm", [128, 2], F32)
    bmask = S2.sb("bmask", [64, NH], F32)
    ones_bf = S2.sb("ones_bf", [128, 1], BF16)
    memset(k, "dve", parm[:], 0.0, [parm.b])
    memset(k, "dve", parm[0:64, 0:1], 1.0, [parm.b])
    memset(k, "dve", parm[64:128, 1:2], 1.0, [parm.b])
    memset(k, "dve", ones_bf[:], 1.0, [ones_bf.b])
    tt_id = k.ident[0:64, 0:64].rearrange("p (h t) -> p h t", t=TSEQ)
    reduce_add(k, bmask[:], tt_id, [k.ident.b], [bmask.b])
    tt(k, "dve", qbd[:].rearrange("p a b (q t) -> p (a b) q t", q=2),
       AP(QT, 0, [[4 * 128, 128], [8, 64], [0, 2], [1, 8]]),
       AP(parm, 0, [[2, 128], [0, 64], [1, 2], [0, 8]]), ALU.mult, [QT.b, parm.b], [qbd.b])

    NB = 3
    kc3 = [S2.sb(f"kc3_{i}", [128, 8, AW], F32) for i in range(NB)]
    kc2 = [S2.sb(f"kc2_{i}", [128, 4, AW], F32) for i in range(NB)]
    kc1 = [S2.sb(f"kc1_{i}", [128, AW], F32) for i in range(NB)]
    vc3 = [S2.sb(f"vc3_{i}", [128, 8, AW], BF16) for i in range(NB)]
    vc2 = [S2.sb(f"vc2_{i}", [128, 4, AW], BF16) for i in range(NB)]
    vc1 = [S2.sb(f"vc1_{i}", [128, AW], BF16) for i in range(NB)]
    NK = 4
    ktb = [S2.sb(f"ktb{i}", [128, 4, 128], BF16) for i in range(NK)]
    pE = [S2.sb(f"pEs{i}", [128, 64], F32) for i in range(NK)]
    pT = [S2.sb(f"pTs{i}", [128, 64], BF16) for i in range(NK)]
    od = [S2.sb(f"od{i}", [64, 8, HD], F32) for i in range(2)]
    o2 = [S2.sb(f"o2{i}", [64, 128], F32) for i in range(2)]
    dn = [S2.sb(f"dn{i}", [64, 2], F32) for i in range(2)]
    ck, cv = din["ck"], din["cv"]

    def load_seq(b):
        i = b % NB
        c.dma("sp", kc3[i][:], ck.ap()[b].rearrange("(i j) c -> i j c", j=16)[:, 0:8, :], writes=[kc3[i].b])
        c.dma("sp", kc2[i][:], ck.ap()[b, 1536:2048, :].rearrange("(i j) c -> i j c", j=4), writes=[kc2[i].b])
        c.dma("sp", kc1[i][:], ck.ap()[b, 1920:2048, :], writes=[kc1[i].b])
        c.dma("pool", vc3[i][:], cv.ap()[b].rearrange("(i j) c -> i j c", j=16)[:, 0:8, :], writes=[vc3[i].b])
        c.dma("pool", vc2[i][:], cv.ap()[b, 1536:2048, :].rearrange("(i j) c -> i j c", j=4), writes=[vc2[i].b])
        c.dma("pool", vc1[i][:], cv.ap()[b, 1920:2048, :], writes=[vc1[i].b])

    def tile_of(b, ti):
        i = b % NB
        if ti == 0:
            return kc1[i], kc1[i][:, :], vc1[i], vc1[i][:, :], 0
        if ti <= 4:
            rho = ti - 1
            return kc2[i], kc2[i][:, rho, :], vc2[i], vc2[i][:, rho, :], 1 + rho
        t0 = ti - 5
        return kc3[i], kc3[i][:, t0, :], vc3[i], vc3[i][:, t0, :], 5 + t0

    NT = 14
    steps = [(b, ti) for b in range(NSS) for ti in range(NT)]
    acc = {}
    deferred = []

    def stA(i):
        b, ti = steps[i]
        if ti == 0 and b + 1 < NSS:
            load_seq(b + 1)
        if ti == NT - 1:
            return
        kt_t, kt_ap, vt_t, vt_ap, tau = tile_of(b, ti)
        bt = k.bank()
        trs(k, [(bt[:, p * 128:(p + 1) * 128], kt_ap[:, p * 128:(p + 1) * 128]) for p in range(4)], k.ident,
            [kt_t.b, k.ident.b], [bt.b])
        kb = ktb[i % NK]
        cp(k, "act", kb[:], bt[:, :].rearrange("p (a t) -> p a t", t=128), [bt.b], [kb.b])

    def stB(i):
        b, ti = steps[i]
        e = pE[i % NK]
        p_ = pT[i % NK]
        bs = k.bank()
        if ti < NT - 1:
            tau = tile_of(b, ti)[4]
            kb = ktb[i % NK]
            mms(k, [(bs[:, p * 16:(p + 1) * 16], kb[:, p, :], qbd[:, p, b, :], True, True) for p in range(4)],
                [kb.b, qbd.b], [bs.b])
            act(k, e[:], bs[:, 0:64], AF.Exp, [bs.b], [e.b])
            tt(k, "dve", p_[:].rearrange("p (h t) -> p h t", t=TSEQ), e[:].rearrange("p (h t) -> p h t", t=TSEQ),
               AP(k.WS, tau * 64, [[896, 128], [1, NH], [8, TSEQ]]), ALU.mult, [e.b, k.WS.b], [p_.b])
        else:
            mms(k, [(bs[0:8, p * 16:(p + 1) * 16], KT[:, p, b * 8:(b + 1) * 8], qbd[:, p, b, :], True, True)
                    for p in range(4)], [KT.b, qbd.b], [bs.b])
            act(k, e[0:8, :], bs[0:8, 0:64], AF.Exp, [bs.b], [e.b])
            tt(k, "dve", p_[0:8, :].rearrange("p (h t) -> p h t", t=TSEQ), e[0:8, :].rearrange("p (h t) -> p h t", t=TSEQ),
               AP(k.WS, 13 * 64, [[896, 8], [1, NH], [8, TSEQ]]), ALU.mult, [e.b, k.WS.b], [p_.b])

    def stC(i):
        b, ti = steps[i]
        p_ = pT[i % NK]
        if ti == 0:
            acc[b] = (k.bank(pin=True), k.bank(pin=True))
        bo, bd = acc[b]
        if ti < NT - 1:
            kt_t, kt_ap, vt_t, vt_ap, tau = tile_of(b, ti)
            mms(k, [(bo[0:64, :], p_[:, :], vt_ap, ti == 0, False),
                    (bd[0:64, 0:1], p_[:, :], ones_bf[:, 0:1], ti == 0, False)], [p_.b, vt_t.b, ones_bf.b], [bo.b, bd.b])
            return
        mms(k, [(bo[0:64, :], p_[0:8, :], vnew[0:8, b, :], False, True),
                (bd[0:64, 0:1], p_[0:8, :], ones_bf[0:8, 0:1], False, True)], [p_.b, vnew.b, ones_bf.b], [bo.b, bd.b])
        od_, o2_, dn_ = od[b % 2], o2[b % 2], dn[b % 2]
        tt(k, "dve", od_[:], bo[0:64, :].rearrange("p (h e) -> p h e", e=HD),
           AP(bmask, 0, [[NH, 64], [1, NH], [0, HD]]), ALU.mult, [bo.b, bmask.b], [od_.b])
        reduce_add(k, o2_[:, 0:64], od_[:].rearrange("p h e -> p e h"), [od_.b], [o2_.b])
        recip(k, dn_[:, 0:1], bd[0:64, 0:1], [bd.b], [dn_.b])
        ts(k, "dve", o2_[:, 0:64], o2_[:, 0:64], dn_[:, 0:1], None, ALU.mult, None, [o2_.b, dn_.b], [o2_.b])
        cp(k, "dve", o2_[:, 64:128], o2_[:, 0:64], [o2_.b], [o2_.b])
        k.unpin(bo)
        k.unpin(bd)

        def epi(b=b, o2_=o2_):
            bt = k.bank()
            trs(k, [(bt[:, 0:64], o2_[:, :])], k.ident, [o2_.b, k.ident.b], [bt.b])
            for par in range(2):
                prt = slice(par * 64, par * 64 + 64)
                src = bt[prt, 0:64].rearrange("p (a q t) -> p a q t", q=2, t=TSEQ)[:, :, par, :]
                cp(k, "act", attnT[prt, :, b * 8:(b + 1) * 8], src, [bt.b], [attnT.b])

        deferred.append((i + 4, epi))

    load_seq(0)
    n = len(steps)
    for i in range(n + 8):
        if i < n:
            stA(i)
        if 1 <= i < n + 1:
            stB(i - 1)
        if 2 <= i < n + 2:
            stC(i - 2)
        for d in [d for d in deferred if d[0] <= i]:
            d[1]()
            deferred.remove(d)
    assert not deferred


_PROG = {}


def get_program(stage="full"):
    if stage not in _PROG:
        _PROG[stage] = build_program(stage)
    return _PROG[stage]


def make_core_inputs(inp, core, cst, nps=NPS, nss=NSS):
    f = lambda a: np.ascontiguousarray(np.asarray(a, dtype=np.float32))
    ps = slice(core * nps, (core + 1) * nps)
    ss = slice(core * nss, (core + 1) * nss)
    cc = np.concatenate([inp["c_prompt"][ps], inp["c_sample"][ss]], axis=0)
    m = {
        "xp": f(inp["x_prompt"][ps]),
        "xs": f(inp["x_sample"][ss].reshape(nss * TSEQ, DM)),
        "cT": f(cc.T),
        "ck": f(inp["cache_k"][0, ss].reshape(nss, WB, AW)),
        "cv": f(inp["cache_v"][0, ss].reshape(nss, WB, AW)),
        "spool": f(inp["state_pool"][0, ss]),
        "w_ada": f(inp["w_ada"][0]),
        "b_adaT": f(inp["b_ada"][0].reshape(48, 128).T),
        "b_row": f(inp["b_ada"][0].reshape(1, -1)),
        "g1T": f(inp["norm1_g"][0].reshape(8, 128).T),
        "g2T": f(inp["norm2_g"][0].reshape(8, 128).T),
        "w_in": f(inp["w_in"][0]),
        "gq2": f(np.tile(inp["q_norm_g"][0], 2).reshape(128, 1)),
        "gk_row": f(inp["k_norm_g"][0].reshape(1, HD)),
        "rel_bias": f(inp["rel_bias"]),
        "w_pool": f(inp["w_pool"][0]),
        "pscT": f(inp["pool_scale"][0].reshape(4, 128).T),
        "w_out": f(inp["w_out"][0]),
        "w_up": f(inp["w_up"][0]),
        "w_down": f(inp["w_down"][0]),
    }
    m.update(cst)
    return m


def run_cores(inp, cores, stage="full"):
    cst = make_consts()
    nc = get_program(stage)
    in_maps = [make_core_inputs(inp, cid, cst) for cid in cores]
    res = run_bass_kernel_spmd(nc, in_maps, core_ids=list(range(len(cores))))
    return res.results


def kernel(**inputs):
    inp = {k_: np.asarray(v) for k_, v in inputs.items()}
    res = run_cores(inp, list(range(NCORES)), "full")
    B = NCORES * NPS
    DB = NCORES * NSS
    yp = np.concatenate([np.asarray(r["yp"]) for r in res], axis=0).reshape(B, SEQ, DM)
    ys = np.concatenate([np.asarray(r["ys"]).reshape(NSS, TSEQ, DM) for r in res], axis=0)
    kwp = np.concatenate([np.asarray(r["kwp"]) for r in res], axis=0).reshape(1, B, SEQ, NH, HD)
    vwp = np.concatenate([np.asarray(r["vwp"]) for r in res], axis=0).reshape(1, B, SEQ, NH, HD)
    pp = np.concatenate([np.asarray(r["pp"]) for r in res], axis=0).reshape(1, B, PCTX, AW)
    kws = np.concatenate([np.asarray(r["kws"]) for r in res], axis=0).reshape(1, DB, WB, NH, HD)
    vws = np.concatenate([np.asarray(r["vws"]) for r in res], axis=0).reshape(1, DB, WB, NH, HD)
    pso = np.concatenate([np.asarray(r["pso"]) for r in res], axis=0).reshape(1, DB, PCTX, AW)
    return tuple(np.ascontiguousarray(a, dtype=np.float32) for a in (yp, ys, kwp, vwp, pp, kws, vws, pso))
```

```python
import math
from contextlib import ExitStack

import numpy as np
import ml_dtypes
import concourse.bass as bass
import concourse.mybir as mybir
from concourse.bass_utils import run_bass_kernel_spmd

F32 = mybir.dt.float32
BF16 = mybir.dt.bfloat16
AF = mybir.ActivationFunctionType
ALU = mybir.AluOpType
AX = mybir.AxisListType

NCORES = 8
NPS = 2
NSS = 16
SEQ = 2048
DM = 1024
AW = 512
NH = 8
HD = 64
FF = 4096
TSEQ = 8
WB = 2048
PCTX = 15
EPS = 1e-6
GW = 2432
WROW = 2560


class Ev:
    __slots__ = ("sem", "val", "eng", "sid")

    def __init__(self, sem, val, eng, sid):
        self.sem, self.val, self.eng, self.sid = sem, val, eng, sid


class Buf:
    def __init__(self, name):
        self.name = name
        self.w = None
        self.rd = {}
        self.dsem = None
        self.dsid = None
        self.dcount = 0


class Eng:
    def __init__(self, name):
        self.name = name
        self.ops = []
        self.sem = None
        self.sid = None
        self.count = 0
        self.waited = {}


class Ctx:
    def __init__(self, nc):
        self.nc = nc
        self.engs = {n: Eng(n) for n in ("pe", "act", "dve", "pool", "sp")}
        self.bufs = []
        self.nsem = 0
        self.free_dsems = []
        for e in self.engs.values():
            e.sem, e.sid = self._new_sem("e_" + e.name)

    def _new_sem(self, name):
        self.nsem += 1
        return self.nc.alloc_semaphore(f"{name}_{self.nsem}"), self.nsem

    def buf(self, name):
        b = Buf(name)
        self.bufs.append(b)
        return b

    def retire(self, bufs):
        for b in bufs:
            if b.dsem is not None:
                self.free_dsems.append((b.dsem, b.dsid, b.dcount))
                b.dsem = None
            if b in self.bufs:
                self.bufs.remove(b)

    def _wait(self, E, ev):
        if ev is None:
            return
        if E.waited.get(ev.sid, 0) >= ev.val:
            return
        E.waited[ev.sid] = ev.val
        sem, val = ev.sem, ev.val
        E.ops.append(lambda h: h.wait_ge(sem, val))

    def op(self, ename, fn, reads=(), writes=()):
        E = self.engs[ename]
        for b in reads:
            if b.w is not None and not (ename == "pe" and b.w.eng == "pe"):
                self._wait(E, b.w)
        for b in writes:
            if b.w is not None and not (ename == "pe" and b.w.eng == "pe"):
                self._wait(E, b.w)
            for ev in b.rd.values():
                if ev.eng != ename:
                    self._wait(E, ev)
        if E.count >= 30000:
            E.sem, E.sid = self._new_sem("e_" + E.name)
            E.count = 0
        E.count += 1
        sem, val = E.sem, E.count
        ev = Ev(sem, val, ename, E.sid)

        def run(h):
            inst = fn(h)
            inst.then_inc(sem, 1)

        E.ops.append(run)
        for b in reads:
            b.rd[ename] = ev
        for b in writes:
            b.w = ev
            b.rd = {}
        return ev

    def dma(self, q, out_ap, in_ap, reads=(), writes=(), sembuf=None):
        E = self.engs[q]
        for b in reads:
            self._wait(E, b.w)
        for b in writes:
            self._wait(E, b.w)
            for ev in b.rd.values():
                self._wait(E, ev)
        sb = sembuf or (writes[0] if writes else reads[0])
        if sb.dsem is None:
            if self.free_dsems:
                sb.dsem, sb.dsid, sb.dcount = self.free_dsems.pop()
            else:
                sb.dsem, sb.dsid = self._new_sem("d")
        sb.dcount += 16
        sem = sb.dsem
        ev = Ev(sem, sb.dcount, "dma", sb.dsid)
        E.ops.append(lambda h: h.dma_start(out=out_ap, in_=in_ap).then_inc(sem, 16))
        for b in reads:
            b.rd[("dma", sb.dsid)] = ev
        for b in writes:
            b.w = ev
            b.rd = {}
        return ev

    def barrier(self, engines=None, final=False):
        evs = []
        for e in self.engs.values():
            if e.count > 0:
                evs.append(Ev(e.sem, e.count, e.name, e.sid))
        for b in self.bufs:
            if getattr(b, "nobar", False) and not final:
                continue
            if b.dsem is not None and b.dcount > 0:
                evs.append(Ev(b.dsem, b.dcount, "dma", b.dsid))
        for s, sid, cnt in self.free_dsems:
            if cnt > 0:
                evs.append(Ev(s, cnt, "dma", sid))
        for en, E in self.engs.items():
            if engines is not None and en not in engines:
                continue
            for ev in evs:
                if ev.sid == E.sid:
                    continue
                self._wait(E, ev)

    def emit(self):
        nc = self.nc
        self.barrier(engines=("sp",), final=True)
        with nc.Block() as block:
            @block.tensor
            def _(h):
                for f in self.engs["pe"].ops:
                    f(h)

            @block.scalar
            def _(h):
                for f in self.engs["act"].ops:
                    f(h)

            @block.vector
            def _(h):
                for f in self.engs["dve"].ops:
                    f(h)

            @block.gpsimd
            def _(h):
                for f in self.engs["pool"].ops:
                    f(h)

            @block.sync
            def _(h):
                for f in self.engs["sp"].ops:
                    f(h)


class T:
    def __init__(self, t, b):
        self.t, self.b = t, b

    def __getitem__(self, k):
        return self.t[k]


class K:
    pass


class WParts:
    def __init__(self, parts, per):
        self.parts, self.per = parts, per
        self.bs = [p.b for p in parts]

    def __getitem__(self, key):
        sl, kc, cols = key
        return self.parts[kc // self.per][sl, kc % self.per, cols]


def AP(t, off, dims):
    th = t.t if isinstance(t, T) else t
    return bass.AP(tensor=th, offset=off, ap=[list(d) for d in dims])


def act(k, out, in_, func, reads, writes, **kw):
    return k.c.op("act", lambda h: h.activation(out=out, in_=in_, func=func, **kw), reads, writes)


def tt(k, eng, out, in0, in1, op, reads, writes):
    return k.c.op(eng, lambda h: h.tensor_tensor(out=out, in0=in0, in1=in1, op=op), reads, writes)


def ts(k, eng, out, in0, s1, s2, op0, op1, reads, writes):
    if op1 is None:
        return k.c.op(eng, lambda h: h.tensor_scalar(out=out, in0=in0, scalar1=s1, scalar2=None, op0=op0), reads, writes)
    return k.c.op(eng, lambda h: h.tensor_scalar(out=out, in0=in0, scalar1=s1, scalar2=s2, op0=op0, op1=op1), reads, writes)


def stt(k, out, in0, scalar, in1, op0, op1, reads, writes):
    return k.c.op("dve", lambda h: h.scalar_tensor_tensor(out=out, in0=in0, scalar=scalar, in1=in1, op0=op0, op1=op1),
                  reads, writes)


def cp(k, eng, out, in_, reads, writes):
    if eng == "act":
        return k.c.op("act", lambda h: h.activation(out=out, in_=in_, func=AF.Copy), reads, writes)
    return k.c.op(eng, lambda h: h.tensor_copy(out=out, in_=in_), reads, writes)


def memset(k, eng, ap, val, writes):
    return k.c.op(eng, lambda h: h.memset(ap, val), (), writes)


def recip(k, out, in_, reads, writes):
    return k.c.op("dve", lambda h: h.reciprocal(out=out, in_=in_), reads, writes)


def reduce_add(k, out, in_, reads, writes):
    return k.c.op("dve", lambda h: h.tensor_reduce(out=out, in_=in_, axis=AX.X, op=ALU.add), reads, writes)


def mms(k, lst, reads, writes):
    lst = list(lst)

    def f(h):
        i = None
        for (o, l, r, st, sp) in lst:
            i = h.matmul(o, lhsT=l, rhs=r, start=st, stop=sp)
        return i

    return k.c.op("pe", f, reads, writes)


def trs(k, lst, ident, reads, writes):
    lst = list(lst)

    def f(h):
        i = None
        for (o, a) in lst:
            n = a.shape[0]
            i = h.transpose(out=o, in_=a, identity=ident[0:n, 0:n])
        return i

    return k.c.op("pe", f, reads, writes)


class Scope:
    def __init__(self, k):
        self.k = k
        self.es = ExitStack()
        self.bufs = []

    def __enter__(self):
        self.es.__enter__()
        return self

    def sb(self, name, shape, dt):
        t = self.es.enter_context(self.k.nc.sbuf_tensor(name + f"_{self.k.uid()}", list(shape), dt))
        b = self.k.c.buf(name)
        self.bufs.append(b)
        return T(t, b)

    def __exit__(self, *a):
        self.k.c.barrier()
        self.k.c.retire(self.bufs)
        return self.es.__exit__(*a)


def _bucket(d):
    d = np.asarray(d, dtype=np.int64)
    exact = 16
    df = np.maximum(d.astype(np.float32), np.float32(1.0))
    large = exact + (np.log(df / np.float32(exact)) / np.float32(math.log(2048 / exact))
                     * np.float32(32 - exact)).astype(np.int32)
    large = np.minimum(large, 31)
    return np.where(d < exact, d, large).astype(np.int64)


def _mult(d):
    d = np.asarray(d, dtype=np.int64)
    m = ((d >= 0) & (d <= 128)).astype(np.float32)
    m += ((d >= 0) & (d <= 512) & (d % 4 == 0)).astype(np.float32)
    m += ((d >= 0) & (d <= 2048) & (d % 16 == 0)).astype(np.float32)
    return m


def make_consts():
    cst = {}
    cst["ident"] = np.eye(128, dtype=np.float32)
    d = np.arange(WROW) - 511
    ohm = np.zeros((32, WROW), np.float32)
    dd = np.maximum(d, 0)
    ohm[_bucket(dd), np.arange(WROW)] = _mult(d)
    cst["ohm"] = ohm
    ohs = np.zeros((32, 14, 8, 128), np.float32)
    kr = np.arange(128)
    for t in range(8):
        dA = 128 + t - kr
        v = (dA >= 0) & (dA <= 128)
        ohs[_bucket(np.maximum(dA, 0))[v], 0, t, kr[v]] = 1.0
        for rho in range(4):
            if t % 4 != rho:
                continue
            d2 = 512 + t - rho - 4 * kr
            v = (d2 >= 0) & (d2 <= 512)
            ohs[_bucket(np.maximum(d2, 0))[v], 1 + rho, t, kr[v]] = 1.0
        d3 = 2048 - 16 * kr
        ohs[_bucket(d3), 5 + t, t, kr] = 1.0
        for tp in range(t + 1):
            dn = t - tp
            ohs[_bucket(dn), 13, t, tp] = _mult(dn)
    cst["ohs"] = ohs.reshape(32, 14 * 8 * 128)
    sel = np.zeros((18, 3, 128), np.float32)
    sel[0, 0, :] = 1.0
    sel[1, 1, :] = 1.0
    for b in range(NSS):
        sel[2 + b, 2, b * 8:(b + 1) * 8] = 1.0
    cst["sel"] = sel.reshape(18, 3 * 128)
    invc = np.zeros((128, 4, 15), np.float32)
    for g, w in enumerate((2, 4, 8, 16)):
        for t in range(15):
            invc[:, g, t] = 1.0 / min(w, t + 1)
    cst["invc"] = invc.reshape(128, 60)
    return cst


IN_SPECS = [
    ("xp", [NPS, SEQ, DM]), ("xs", [128, DM]), ("cT", [DM, 18]),
    ("ck", [NSS, WB, AW]), ("cv", [NSS, WB, AW]), ("spool", [NSS, PCTX, AW]),
    ("w_ada", [DM, 6 * DM]), ("b_adaT", [128, 48]), ("b_row", [1, 6 * DM]),
    ("g1T", [128, 8]), ("g2T", [128, 8]), ("w_in", [DM, 2048]), ("gq2", [128, 1]), ("gk_row", [1, HD]),
    ("rel_bias", [32, NH]), ("w_pool", [4, 128, 128]), ("pscT", [128, 4]),
    ("w_out", [DM, DM]), ("w_up", [DM, FF]), ("w_down", [FF, DM]),
    ("ident", [128, 128]), ("ohm", [32, WROW]), ("ohs", [32, 14 * 8 * 128]), ("sel", [18, 3 * 128]),
    ("invc", [128, 60]),
]
OUT_SPECS = [
    ("yp", [NPS, SEQ, DM]), ("ys", [128, DM]), ("kwp", [NPS, SEQ, AW]), ("vwp", [NPS, SEQ, AW]),
    ("pp", [NPS, PCTX, AW]), ("kws", [NSS, WB, AW]), ("vws", [NSS, WB, AW]), ("pso", [NSS, PCTX, AW]),
]


def build_program(stage="full"):
    nc = bass.Bass("TRN2", target_bir_lowering=False)
    k = K()
    k.nc = nc
    k.c = Ctx(nc)
    k._uid = 0

    def uid():
        k._uid += 1
        return k._uid

    k.uid = uid
    k.stage = stage
    c = k.c
    k.din = {}
    for name, shape in IN_SPECS:
        k.din[name] = nc.dram_tensor(name, shape, F32, kind="ExternalInput")
    k.dout = {}
    for name, shape in OUT_SPECS:
        k.dout[name] = T(nc.dram_tensor(name, shape, F32, kind="ExternalOutput"), c.buf("o_" + name))
    k.wscr = T(nc.dram_tensor("wscr", [128 * 8 * WROW], BF16, kind="Internal"), c.buf("wscr"))
    k.w_in_bf = T(nc.dram_tensor("w_in_bf", [DM, 2048], BF16, kind="Internal"), c.buf("w_in_bf"))
    k.w_out_bf = T(nc.dram_tensor("w_out_bf", [DM, DM], BF16, kind="Internal"), c.buf("w_out_bf"))
    k.w_up_bf = T(nc.dram_tensor("w_up_bf", [DM, FF], BF16, kind="Internal"), c.buf("w_up_bf"))
    k.w_down_bf = T(nc.dram_tensor("w_down_bf", [FF, DM], BF16, kind="Internal"), c.buf("w_down_bf"))

    k.ps_t = [nc.alloc_psum_tensor(f"ps{i}", [128, 512], F32) for i in range(8)]
    k.ps_b = [c.buf(f"ps{i}") for i in range(8)]
    k.ps_i = 0

    k.pinned = set()

    def bank(pin=False):
        while k.ps_i in k.pinned:
            k.ps_i = (k.ps_i + 1) % 8
        i = k.ps_i
        k.ps_i = (i + 1) % 8
        if pin:
            k.pinned.add(i)
        t = T(k.ps_t[i], k.ps_b[i])
        t.idx = i
        return t

    def unpin(t):
        k.pinned.discard(t.idx)

    k.bank = bank
    k.unpin = unpin
    k.wcast = c.buf("wcast")
    k.bg = c.buf("bg")
    k.bg.nobar = True

    with Scope(k) as P:
        k.P = P
        setup(k)
        if stage in ("full", "prompt", "p1", "p12"):
            for s in range(NPS):
                prompt_seq(k, s)
                if stage in ("p1", "p12"):
                    break
        if stage in ("full", "sample", "s1"):
            sample_group(k)
    c.emit()
    return nc


def setup(k):
    c, nc, P, din = k.c, k.nc, k.P, k.din
    full = k.stage in ("full", "sample")
    k.bg_done = False

    k.ident = P.sb("ident", [128, 128], F32)
    k.ones = P.sb("ones", [128, 128], F32)
    k.epsT = P.sb("epsT", [128, 1], F32)
    k.adaT = P.sb("adaT", [128, 5, 8, 18], F32)
    k.gsT = P.sb("gsT", [128, 2, 8, 18], F32)
    k.gate1 = P.sb("gate1", [128, 3, DM], F32)
    k.gq8 = P.sb("gq8", [128, 3], F32)
    k.gkb = P.sb("gkb", [128, HD], F32)
    k.pscT = P.sb("pscT", [128, 4], F32)
    k.invc = P.sb("invc", [128, 60], F32)
    k.WS = P.sb("WS", [128, 112, 8], F32)
    k.wpool = P.sb("wpool", [128, 4, 128], BF16)

    c.dma("sp", k.ident[:], din["ident"].ap(), writes=[k.ident.b])
    memset(k, "dve", k.ones[:], 1.0, [k.ones.b])
    memset(k, "dve", k.epsT[:], EPS, [k.epsT.b])
    c.dma("sp", k.pscT[:], din["pscT"].ap(), writes=[k.pscT.b])
    c.dma("sp", k.invc[:], din["invc"].ap(), writes=[k.invc.b])
    c.dma("sp", k.gkb[:], AP(din["gk_row"], 0, [[0, 128], [1, HD]]), writes=[k.gkb.b])
    c.dma("pool", k.wpool[:], din["w_pool"].ap().rearrange("g c d -> c g d"), writes=[k.wpool.b])

    k.wscr_done = False
    c.dma("pool", k.w_in_bf.t.ap().rearrange("(p a) n -> p (a n)", p=128),
          din["w_in"].ap().rearrange("(p a) n -> p (a n)", p=128), writes=[k.w_in_bf.b])

    with Scope(k) as Sx:
        cTs = Sx.sb("cTs", [128, 8, 18], F32)
        scT = Sx.sb("scT", [128, 8, 18], BF16)
        brow = Sx.sb("brow", [1, 6 * DM], F32)
        g12 = [Sx.sb(f"g12_{i}", [128, 8], F32) for i in range(2)]
        gq2 = Sx.sb("gq2", [128, 1], F32)
        sel = Sx.sb("sel", [18, 3 * 128], F32)
        atms = [Sx.sb(f"atm{i}", [18, DM], F32) for i in range(3)]
        NWA = 2
        wa = [Sx.sb(f"wa{i}", [128, 8, 512], F32) for i in range(NWA)]
        wab = [Sx.sb(f"wab{i}", [128, 8, 512], BF16) for i in range(2)]
        tmp18 = Sx.sb("tmp18", [128, 8, 18], F32)
        rb = Sx.sb("rb", [32, NH], F32)
        expb = Sx.sb("expb", [32, NH], F32)
        expr = Sx.sb("expr", [32, NH, 128], F32)
        ohm = Sx.sb("ohm", [32, WROW], F32)
        ohs = Sx.sb("ohs", [32, 56 * 128], F32)
        wrep = Sx.sb("wrep", [128, NH, WROW], BF16)

        def load_wa(i):
            blk, half = divmod(i, 2)
            c.dma("sp", wa[i % NWA][:], din["w_ada"].ap()[:, blk * 1024 + half * 512:blk * 1024 + (half + 1) * 512]
                  .rearrange("(k p) n -> p k n", p=128), writes=[wa[i % NWA].b])

        c.dma("sp", rb[:], din["rel_bias"].ap(), writes=[rb.b])
        c.dma("sp", ohm[:], din["ohm"].ap(), writes=[ohm.b])
        c.dma("sp", cTs[:], din["cT"].ap().rearrange("(k p) n -> p k n", p=128), writes=[cTs.b])
        c.dma("sp", brow[:], din["b_row"].ap(), writes=[brow.b])
        c.dma("sp", g12[0][:], din["g1T"].ap(), writes=[g12[0].b])
        c.dma("sp", g12[1][:], din["g2T"].ap(), writes=[g12[1].b])
        c.dma("sp", gq2[:], din["gq2"].ap(), writes=[gq2.b])
        c.dma("sp", sel[:], din["sel"].ap(), writes=[sel.b])
        c.dma("sp", ohs[:], din["ohs"].ap()[:, 0:56 * 128], writes=[ohs.b])
        for i in range(NWA):
            load_wa(i)

        memset(k, "dve", k.gq8[:], 0.0, [k.gq8.b])
        ts(k, "dve", k.gq8[:, 0:1], gq2[:], HD ** -0.5, None, ALU.mult, None, [gq2.b], [k.gq8.b])
        ts(k, "dve", k.gq8[0:64, 1:2], gq2[0:64, :], HD ** -0.5, None, ALU.mult, None, [gq2.b], [k.gq8.b])
        ts(k, "dve", k.gq8[64:128, 2:3], gq2[64:128, :], HD ** -0.5, None, ALU.mult, None, [gq2.b], [k.gq8.b])

        act(k, expb[:], rb[:], AF.Exp, [rb.b], [expb.b])
        cp(k, "dve", expr[:], AP(expb, 0, [[NH, 32], [1, NH], [0, 128]]), [expb.b], [expr.b])
        if k.stage != "s1":
            n = 0
            for h in range(NH):
                for blk in range(WROW // 512):
                    bk = k.bank()
                    mms(k, [(bk[:, :], expr[:, h, :], ohm[:, blk * 512:(blk + 1) * 512], True, True)],
                        [expr.b, ohm.b], [bk.b])
                    cp(k, "act" if n % 2 == 0 else "dve", wrep[:, h, blk * 512:(blk + 1) * 512], bk[:, :], [bk.b], [wrep.b])
                    n += 1
            c.dma("pool", k.wscr.t.ap().rearrange("(p f) -> p f", p=128), wrep[:].rearrange("p h w -> p (h w)"),
                  reads=[wrep.b], writes=[k.wscr.b])
        for half in range(2):
            if half == 1:
                c.dma("sp", ohs[:], din["ohs"].ap()[:, 56 * 128:112 * 128], writes=[ohs.b])
            bk = k.bank()
            lst = []
            for i in range(56):
                lst.append((bk[:, i * 8:(i + 1) * 8], ohs[:, i * 128:(i + 1) * 128], expb[:, :], True, True))
            mms(k, lst, [ohs.b, expb.b], [bk.b])
            cp(k, "dve", k.WS[:, half * 56:(half + 1) * 56, :], bk[:, 0:448].rearrange("p (a b) -> p a b", b=8),
               [bk.b], [k.WS.b])

        act(k, scT[:], cTs[:], AF.Silu, [cTs.b], [scT.b])
        kind_of_blk = {0: 0, 1: 1, 3: 2, 4: 3, 5: 4}
        for i in range(12):
            blk, half = divmod(i, 2)
            w = wa[i % NWA]
            wb = wab[i % 2]
            cp(k, "act" if i % 2 == 0 else "dve", wb[:], w[:], [w.b], [wb.b])
            if i + NWA < 12:
                load_wa(i + NWA)
            bk = k.bank()
            lst = [(bk[0:18, :], scT[:, kc, :], wb[:, kc, :], kc == 0, False) for kc in range(8)]
            col0 = blk * 1024 + half * 512
            lst.append((bk[0:18, :], k.ones[0:1, 0:18], brow[0:1, col0:col0 + 512], False, True))
            mms(k, lst, [scT.b, wb.b, k.ones.b, brow.b], [bk.b])
            atm = atms[2] if blk == 2 else atms[blk % 2]
            cp(k, "dve" if half == 0 else "act", atm[0:18, half * 512:(half + 1) * 512], bk[0:18, :], [bk.b], [atm.b])
            if half == 1 and blk != 2:
                kind = kind_of_blk[blk]
                bk = k.bank()
                trs(k, [(bk[:, fcl * 18:(fcl + 1) * 18], atm[0:18, fcl * 128:(fcl + 1) * 128])
                        for fcl in range(8)], k.ident, [atm.b, k.ident.b], [bk.b])
                cp(k, "dve", k.adaT[:, kind, :, :], bk[:, 0:144].rearrange("p (a b) -> p a b", b=18), [bk.b], [k.adaT.b])
        for j, kind in enumerate((1, 3)):
            ts(k, "dve", tmp18[:], k.adaT[:, kind, :, :], 1.0, None, ALU.add, None, [k.adaT.b], [tmp18.b])
            tt(k, "dve", k.gsT[:, j, :, :], tmp18[:], AP(g12[j], 0, [[8, 128], [1, 8], [0, 18]]), ALU.mult,
               [tmp18.b, g12[j].b], [k.gsT.b])
        for G in range(3):
            for half in range(2):
                bk = k.bank()
                mms(k, [(bk[:, :], sel[0:18, G * 128:(G + 1) * 128], atms[2][0:18, half * 512:(half + 1) * 512],
                         True, True)], [sel.b, atms[2].b], [bk.b])
                cp(k, "act", k.gate1[:, G, half * 512:(half + 1) * 512], bk[:, :], [bk.b], [k.gate1.b])
        issue_wscratch(k)


def issue_wscratch(k):
    if k.wscr_done or k.stage not in ("full", "prompt", "sample"):
        return
    k.wscr_done = True
    c, din = k.c, k.din
    for src, dst in ((din["w_out"], k.w_out_bf), (din["w_up"], k.w_up_bf), (din["w_down"], k.w_down_bf)):
        c.dma("pool", dst.t.ap().rearrange("(p a) n -> p (a n)", p=128),
              src.ap().rearrange("(p a) n -> p (a n)", p=128), writes=[dst.b])


def issue_bg(k):
    if k.bg_done or k.stage not in ("full", "sample"):
        return
    k.bg_done = True
    c, din = k.c, k.din
    for b in range(NSS):
        for src, dst in ((din["ck"], k.dout["kws"]), (din["cv"], k.dout["vws"])):
            c.dma("pool", dst.t.ap()[b, 0:WB - TSEQ, :].rearrange("(p r) c -> p (r c)", p=15),
                  src.ap()[b, TSEQ:WB, :].rearrange("(p r) c -> p (r c)", p=15), reads=[k.bg])
    c.dma("pool", k.dout["pso"].t.ap()[:, 0:PCTX - TSEQ, :], din["spool"].ap()[:, TSEQ:PCTX, :], reads=[k.bg])


def rms_stats(k, x, junk, st, col):
    act(k, junk[:], x[:], AF.Square, [x.b], [junk.b, st.b], accum_out=st[:, col:col + 1])
    act(k, st[:, col + 1:col + 2], st[:, col:col + 1], AF.Sqrt, [st.b, k.epsT.b], [st.b], bias=k.epsT[:], scale=1.0 / DM)
    recip(k, st[:, col + 2:col + 3], st[:, col + 1:col + 2], [st.b], [st.b])


def head_norm(k, bk, sq, rs, out, extra_gain=None):
    act(k, sq[:], bk[:, :], AF.Square, [bk.b], [sq.b])
    reduce_add(k, rs[:, 0:8], sq[:].rearrange("p (h e) -> p h e", e=HD), [sq.b], [rs.b])
    act(k, rs[:, 8:16], rs[:, 0:8], AF.Sqrt, [rs.b, k.epsT.b], [rs.b], bias=k.epsT[:], scale=1.0 / HD)
    recip(k, rs[:, 16:24], rs[:, 8:16], [rs.b], [rs.b])
    tt(k, "dve", out[:].rearrange("p (h e) -> p h e", e=HD), bk[:, :].rearrange("p (h e) -> p h e", e=HD),
       AP(rs, 16, [[24, 128], [1, 8], [0, HD]]), ALU.mult, [bk.b, rs.b], [out.b])
    if extra_gain is not None:
        tt(k, "dve", out[:].rearrange("p (h e) -> p h e", e=HD), out[:].rearrange("p (h e) -> p h e", e=HD),
           AP(extra_gain, 0, [[HD, 128], [0, 8], [1, HD]]), ALU.mult, [out.b, extra_gain.b], [out.b])


def evac_hT(k, bk, hT, cc, ncols, kind_scale, kind_shift, col, sample, tmp=None):
    if not sample:
        act(k, hT[:, cc, 0:ncols], bk[:, 0:ncols], AF.Identity, [bk.b, k.gsT.b, k.adaT.b], [hT.b],
            scale=k.gsT[:, kind_scale, cc, col:col + 1], bias=k.adaT[:, kind_shift, cc, col:col + 1])
    else:
        gs_off = ((kind_scale * 8 + cc) * 18 + 2)
        sh_off = ((kind_shift * 8 + cc) * 18 + 2)
        tt(k, "dve", tmp[:, 0:128].rearrange("p (b t) -> p b t", t=TSEQ), bk[:, 0:128].rearrange("p (b t) -> p b t", t=TSEQ),
           AP(k.gsT, gs_off, [[2 * 8 * 18, 128], [1, NSS], [0, TSEQ]]), ALU.mult, [bk.b, k.gsT.b], [tmp.b])
        tt(k, "dve", hT[:, cc, 0:128].rearrange("p (b t) -> p b t", t=TSEQ), tmp[:, 0:128].rearrange("p (b t) -> p b t", t=TSEQ),
           AP(k.adaT, sh_off, [[5 * 8 * 18, 128], [1, NSS], [0, TSEQ]]), ALU.add, [tmp.b, k.adaT.b], [hT.b])


def mixer_in_group(k, S, g, ntiles, xsrc, col, sample, R):
    c = k.c
    W = S["w_in_bf"]
    hT = S["hT"][g % 2]
    ncols = ntiles * 128
    xts = []
    for j in range(ntiles):
        T0 = g * 4 + j
        xt = S["xt"][T0 % 4]
        c.dma("sp", xt[:], xsrc(T0), writes=[xt.b])
        st = S["st"][T0 % 4]
        rms_stats(k, xt, S["junk"], st, 0)
        ts(k, "dve", xt[:], xt[:], st[:, 2:3], None, ALU.mult, None, [xt.b, st.b], [xt.b])
        xts.append(xt)
    for cc in range(8):
        bk = k.bank()
        trs(k, [(bk[:, j * 128:(j + 1) * 128], xts[j][:, cc * 128:(cc + 1) * 128]) for j in range(ntiles)], k.ident,
            [x.b for x in xts] + [k.ident.b], [bk.b])
        evac_hT(k, bk, hT, cc, ncols, 0, 0, col, sample, tmp=S["tmpf"])
    def proj(j):
        T0 = g * 4 + j
        tok = slice(j * 128, (j + 1) * 128)
        bq, bkk, bv = k.bank(), k.bank(), k.bank()
        lst = []
        for kc in range(8):
            lst.append((bq[:, :], hT[:, kc, tok], W[:, kc, 0:512], kc == 0, kc == 7))
            lst.append((bkk[:, :], hT[:, kc, tok], W[:, kc, 512:1024], kc == 0, kc == 7))
        mms(k, lst, [hT.b] + W.bs, [bq.b, bkk.b])
        mms(k, [(bv[:, :], hT[:, kc, tok], W[:, kc, 1024:1536], kc == 0, kc == 7) for kc in range(8)], [hT.b] + W.bs, [bv.b])
        qn = S["qn"][T0 % 2]
        head_norm(k, bq, S["sq"], S["rs"][0], qn)
        kn = S["kn"][T0 % 2]
        head_norm(k, bkk, S["sq2"], S["rs"][1], kn, extra_gain=k.gkb)
        R["store_k"](T0, kn)
        vo = S["vo"][T0 % 2]
        cp(k, "act", vo[:], bv[:, :], [bv.b], [vo.b])
        R["store_v"](T0, vo)

    def trsp(j):
        T0 = g * 4 + j
        qn = S["qn"][T0 % 2]
        kn = S["kn"][T0 % 2]
        bt = k.bank()
        trs(k, [(bt[:, p * 128:(p + 1) * 128], qn[:, p * 128:(p + 1) * 128]) for p in range(4)], k.ident,
            [qn.b, k.ident.b], [bt.b])
        if R.get("qmask"):
            for par in range(2):
                act(k, R["QT"][:, par, :, T0 * 128:(T0 + 1) * 128], bt[:, :].rearrange("p (a t) -> p a t", t=128), AF.Copy,
                    [bt.b, k.gq8.b], [R["QT"].b], scale=k.gq8[:, 1 + par:2 + par])
        else:
            act(k, R["QT"][:, :, T0 * 128:(T0 + 1) * 128], bt[:, :].rearrange("p (a t) -> p a t", t=128), AF.Copy,
                [bt.b, k.gq8.b], [R["QT"].b], scale=k.gq8[:, 0:1])
        bt2 = k.bank()
        trs(k, [(bt2[:, p * 128:(p + 1) * 128], kn[:, p * 128:(p + 1) * 128]) for p in range(4)], k.ident,
            [kn.b, k.ident.b], [bt2.b])
        cp(k, "dve", R["KT"][:, :, T0 * 128:(T0 + 1) * 128], bt2[:, :].rearrange("p (a t) -> p a t", t=128),
           [bt2.b], [R["KT"].b])

    proj(0)
    for j in range(ntiles):
        if j + 1 < ntiles:
            proj(j + 1)
        trsp(j)
    uT = S["uT"][g % 2]
    for uc in range(4):
        bk = k.bank()
        mms(k, [(bk[:, 0:ncols], W[:, kc, 1536 + uc * 128:1536 + (uc + 1) * 128], hT[:, kc, 0:ncols], kc == 0, kc == 7)
                for kc in range(8)], [hT.b] + W.bs, [bk.b])
        R["store_uT"](g, uc, bk, uT)
    return hT


def pool_adds(k, S, uT, ncols, lead, done):
    pa, pb, s4 = S["pa"], S["pb"], S["s4"]
    n = lead + ncols
    u = uT
    for uc in range(4):
        if uc == 0:
            tt(k, "pool", s4[:, 0:ncols], u[:, 0, lead:n], u[:, 0, lead - 1:n - 1], ALU.add, [u.b], [s4.b])
            done(uc)
            continue
        tt(k, "pool", pa[:, 1:n], u[:, uc, 1:n], u[:, uc, 0:n - 1], ALU.add, [u.b], [pa.b])
        if uc == 1:
            tt(k, "pool", s4[:, 0:ncols], pa[:, lead:n], pa[:, lead - 2:n - 2], ALU.add, [pa.b], [s4.b])
            done(uc)
            continue
        tt(k, "pool", pb[:, 3:n], pa[:, 3:n], pa[:, 1:n - 2], ALU.add, [pa.b], [pb.b])
        if uc == 2:
            tt(k, "pool", s4[:, 0:ncols], pb[:, lead:n], pb[:, lead - 4:n - 4], ALU.add, [pb.b], [s4.b])
            done(uc)
            continue
        tt(k, "pool", pa[:, 7:n], pb[:, 7:n], pb[:, 3:n - 4], ALU.add, [pb.b], [pa.b])
        tt(k, "pool", s4[:, 0:ncols], pa[:, lead:n], pa[:, lead - 8:n - 8], ALU.add, [pa.b], [s4.b])
        done(uc)


def prompt_seq(k, s):
    c, nc, din = k.c, k.nc, k.din
    with Scope(k) as Q0:
      R = {}
      R["poolT"] = Q0.sb("poolT", [128, 4, SEQ], BF16)
      arena = Q0.sb("arena", [128, 4, SEQ], BF16)
      R["attnT"] = arena
      R["qmask"] = True
      with Scope(k) as Q:
        R["QT"] = Q.sb("QT", [128, 2, 4, SEQ], BF16)
        R["KT"] = Q.sb("KT", [128, 4, SEQ], BF16)
        R["Vx"] = Q.sb("Vx", [128, 16, 4, 192], BF16)
        Vx = R["Vx"]
        memset(k, "pool", Vx[:], 0.0, [Vx.b])
        memset(k, "pool", Vx[:, :, :, 64:65], 1.0, [Vx.b])

        with Scope(k) as S1:
            S = {}
            wB = S1.sb("w_in_hi", [128, 4, 2048], BF16)
            S["w_in_bf"] = WParts([arena, wB], 4)
            c.dma("sp", arena[:], k.w_in_bf.t.ap()[0:512, :].rearrange("(k p) n -> p k n", p=128), reads=[k.w_in_bf.b],
                  writes=[arena.b])
            c.dma("sp", wB[:], k.w_in_bf.t.ap()[512:1024, :].rearrange("(k p) n -> p k n", p=128), reads=[k.w_in_bf.b],
                  writes=[wB.b])
            S["xt"] = [S1.sb(f"xt{i}", [128, DM], F32) for i in range(4)]
            S["st"] = [S1.sb(f"st{i}", [128, 4], F32) for i in range(4)]
            S["junk"] = S1.sb("junk", [128, DM], BF16)
            S["hT"] = [S1.sb(f"hT{i}", [128, 8, 512], BF16) for i in range(1)] * 2
            S["tmpf"] = None
            S["sq"] = S1.sb("sq", [128, 512], F32)
            S["sq2"] = S1.sb("sq2", [128, 512], F32)
            S["rs"] = [S1.sb(f"rs{i}", [128, 24], F32) for i in range(2)]
            S["qn"] = [S1.sb(f"qn{i}", [128, 512], F32) for i in range(2)]
            S["kn"] = [S1.sb(f"kn{i}", [128, 512], F32) for i in range(2)]
            S["vo"] = [S1.sb(f"vo{i}", [128, 512], F32) for i in range(2)]
            S["uT"] = [S1.sb(f"uT{i}", [128, 4, 16 + 512], F32) for i in range(1)] * 2
            carry = S1.sb("carry", [128, 4, 16], F32)
            S["pa"] = S1.sb("pa", [128, 528], F32)
            S["pb"] = S1.sb("pb", [128, 528], F32)
            S["s4"] = S1.sb("s4", [128, 512], F32)
            S["pooled"] = [S1.sb(f"pooled{i}", [128, 512], BF16) for i in range(2)]
            S["t15"] = S1.sb("t15", [128, 15], F32)
            utm = S["qn"][0]
            memset(k, "pool", carry[:], 0.0, [carry.b])

            def store_k(T0, kn):
                c.dma("sp", k.dout["kwp"][s, T0 * 128:(T0 + 1) * 128, :], kn[:], reads=[kn.b])

            def store_v(T0, vo):
                c.dma("sp", k.dout["vwp"][s, T0 * 128:(T0 + 1) * 128, :], vo[:], reads=[vo.b])
                tt_src = vo[:].rearrange("p (a q e) -> p a q e", q=2, e=HD)
                cp(k, "pool", Vx[:, T0, :, 0:64], tt_src[:, :, 0, :], [vo.b], [Vx.b])
                cp(k, "pool", Vx[:, T0, :, 128:192], tt_src[:, :, 1, :], [vo.b], [Vx.b])

            def store_uT(g, uc, bk, uT):
                cp(k, "act" if uc % 2 == 0 else "dve", uT[:, uc, 16:528], bk[:, :], [bk.b], [uT.b])

            R["store_k"], R["store_v"], R["store_uT"] = store_k, store_v, store_uT

            for g in range(4):
                uT = S["uT"][0]
                cp(k, "pool", uT[:, :, 0:16], carry[:], [carry.b], [uT.b])
                hT = mixer_in_group(k, S, g, 4, lambda T0: din["xp"][s, T0 * 128:(T0 + 1) * 128, :], s, False, R)
                cp(k, "pool", carry[:], uT[:, :, 512:528], [uT.b], [carry.b])

                def done(uc, g=g, uT=uT):
                    w = (2, 4, 8, 16)[uc]
                    s4 = S["s4"]
                    pooled = S["pooled"][uc % 2]
                    stt(k, pooled[:], s4[:], 1.0 / w, uT[:, uc, 16:528], ALU.mult, ALU.subtract, [s4.b, uT.b], [pooled.b])
                    if g == 0:
                        t15 = S["t15"]
                        tt(k, "dve", t15[:], s4[:, 0:15], k.invc[:, uc * 15:(uc + 1) * 15], ALU.mult,
                           [s4.b, k.invc.b], [t15.b])
                        tt(k, "dve", pooled[:, 0:15], t15[:], uT[:, uc, 16:31], ALU.subtract, [t15.b, uT.b], [pooled.b])
                    bk = k.bank()
                    mms(k, [(bk[:, :], k.wpool[:, uc, :], pooled[:], True, True)], [k.wpool.b, pooled.b], [bk.b])
                    act(k, R["poolT"][:, uc, g * 512:(g + 1) * 512], bk[:, :], AF.Copy, [bk.b, k.pscT.b], [R["poolT"].b],
                        scale=k.pscT[:, uc:uc + 1])

                pool_adds(k, S, uT, 512, 16, done)
                if g == 3:
                    bk = k.bank()
                    W = S["w_in_bf"]
                    mms(k, [(bk[:, :], hT[:, kc, 384:512], W[:, kc, 1536:2048], kc == 0, kc == 7) for kc in range(8)],
                        [hT.b] + W.bs, [bk.b])
                    cp(k, "dve", utm[:], bk[:, :], [bk.b], [utm.b])
                    c.dma("sp", k.dout["pp"][s, :, :], utm[128 - PCTX:128, :], reads=[utm.b])
        if k.stage == "p1":
            return

        with Scope(k) as S2:
            G = S2.sb("G", [128, NH, GW], BF16)
            for h in range(NH):
                c.dma("sp", G[:, h, :], AP(k.wscr, 127 + h * WROW, [[8 * WROW - 1, 128], [1, GW]]),
                      reads=[k.wscr.b], writes=[G.b])
            c._wait(c.engs["pool"], G.b.w)
            issue_bg(k)
            NPB = 5
            pE = [S2.sb(f"pE{i}", [128, 512], BF16) for i in range(NPB)]
            pT = [S2.sb(f"pT{i}", [128, 512], BF16) for i in range(NPB)]
            rden = [S2.sb(f"rden{i}", [128, 1024], F32) for i in range(2)]
            rhl = [S2.sb(f"rhl{i}", [128, 1024], BF16) for i in range(2)]
            osb = [S2.sb(f"osb{i}", [128, 512], F32) for i in range(2)]
            ones_bf = S2.sb("ones_bf2", [128, 128], BF16)
            memset(k, "dve", ones_bf[:], 1.0, [ones_bf.b])
            QT, KT, attnT = R["QT"], R["KT"], R["attnT"]
            steps = []
            for pair in range(4):
                for par in range(2):
                    for qb in range(4):
                        nkt = 4 * qb + 4
                        for kt in range(nkt):
                            steps.append((pair, par, qb, kt, nkt))
            LAG = 2
            bos = {}
            deferred = []
            gcnt = [0]

            def front(idx):
                pair, par, qb, kt, nkt = steps[idx]
                h = 2 * pair + par
                prt = slice(par * 64, par * 64 + 64)
                if kt == 0:
                    bos[(pair, par, qb)] = k.bank(pin=True)
                i = kt - 4 * qb
                q0 = max(0, i) * 128
                xoff = 512 * qb - 128 * kt + 384
                bs = k.bank()
                mms(k, [(bs[:, q0:512], KT[:, pair, kt * 128:(kt + 1) * 128],
                         QT[:, par, pair, qb * 512 + q0:(qb + 1) * 512], True, True)], [KT.b, QT.b], [bs.b])
                e = pE[idx % NPB]
                p = pT[idx % NPB]
                act(k, e[:, q0:512], bs[:, q0:512], AF.Exp, [bs.b], [e.b])
                tt(k, "dve", p[:, q0:512], e[:, q0:512], G[:, h, xoff + q0:xoff + 512], ALU.mult, [e.b, G.b], [p.b])

            def back(idx):
                pair, par, qb, kt, nkt = steps[idx]
                prt = slice(par * 64, par * 64 + 64)
                bo = bos[(pair, par, qb)]
                p = pT[idx % NPB]
                i = kt - 4 * qb
                q0 = max(0, i) * 128
                vcol = 0 if par == 0 else 64
                mms(k, [(bo[:, q0:512], Vx[:, kt, pair, vcol:vcol + 128], p[:, q0:512], kt == 0, kt == nkt - 1)],
                    [Vx.b, p.b], [bo.b])
                if kt == nkt - 1:
                    dr = 64 if par == 0 else 0
                    gcnt[0] += 1
                    rd = rden[gcnt[0] % 2]
                    ob = osb[gcnt[0] % 2]
                    rh = rhl[gcnt[0] % 2]
                    drs = slice(dr, dr + 1)
                    act(k, rd[drs, 0:512], bo[drs, :], AF.Ln, [bo.b], [rd.b])
                    act(k, rd[drs, 512:1024], rd[drs, 0:512], AF.Exp, [rd.b], [rd.b], scale=-1.0)
                    cp(k, "dve", rh[drs, 0:512], rd[drs, 512:1024], [rd.b], [rh.b])
                    tt(k, "dve", rh[drs, 512:1024], rd[drs, 512:1024], rh[drs, 0:512], ALU.subtract, [rd.b, rh.b], [rh.b])
                    cp(k, "act", ob[prt, :], bo[prt, :], [bo.b], [ob.b])

                    def epi(pair=pair, par=par, qb=qb, prt=prt, drs=drs, rh=rh, ob=ob, bo=bo):
                        bb = k.bank()
                        mms(k, [(bb[:, :], ones_bf[drs, :], rh[drs, 0:512], True, False),
                                (bb[:, :], ones_bf[drs, :], rh[drs, 512:1024], False, True)], [ones_bf.b, rh.b], [bb.b])
                        tt(k, "dve", attnT[prt, pair, qb * 512:(qb + 1) * 512], ob[prt, :], bb[prt, :], ALU.mult,
                           [ob.b, bb.b], [attnT.b])
                        k.unpin(bo)

                    deferred.append((idx + LAG + 3, epi))

            nst = len(steps)
            for idx in range(nst + LAG + 8):
                if idx < nst:
                    front(idx)
                if LAG <= idx < nst + LAG:
                    back(idx - LAG)
                for d in [d for d in deferred if d[0] <= idx]:
                    d[1]()
                    deferred.remove(d)
            assert not deferred
        if k.stage == "p12":
            return
      if k.stage in ("p1", "p12"):
          return

      with Scope(k) as S3:
          S = ffn_alloc(k, S3)
          tail_seq(k, S, 4, 4, lambda T0: din["xp"][s, T0 * 128:(T0 + 1) * 128, :],
                   lambda T0: k.dout["yp"][s, T0 * 128:(T0 + 1) * 128, :], s, False,
                   lambda cc, T0: (R["attnT"] if cc < 4 else R["poolT"]), s)


def ffn_alloc(k, S3):
    c = k.c
    S = {}
    S["w_out_bf"] = S3.sb("w_out_sb", [128, 8, DM], BF16)
    c.dma("sp", S["w_out_bf"][:], k.w_out_bf.t.ap().rearrange("(k p) n -> p k n", p=128), reads=[k.w_out_bf.b],
          writes=[S["w_out_bf"].b])
    S["wu"] = [S3.sb(f"wu{i}", [128, 8, 512], BF16) for i in range(2)]
    S["wd"] = [S3.sb(f"wd{i}", [128, 4, 512], BF16) for i in range(2)]
    S["xt"] = [S3.sb(f"xt3_{i}", [128, DM], F32) for i in range(2)]
    S["x1"] = [S3.sb(f"x1_{i}", [128, DM], F32) for i in range(8)]
    S["xn"] = [S3.sb(f"xn2_{i}", [128, DM], F32) for i in range(4)]
    S["st"] = [S3.sb(f"st3_{i}", [128, 4], F32) for i in range(8)]
    S["junk"] = S3.sb("junk3", [128, DM], BF16)
    S["hT2"] = S3.sb("hT2", [128, 8, 512], BF16)
    S["fT"] = S3.sb("fT", [128, 32, 512], BF16)
    S["sqf"] = [S3.sb(f"sqf{i}", [128, 512], F32) for i in range(2)]
    S["yT"] = [S3.sb(f"yT{i}", [128, 512], F32) for i in range(4)]
    S["tmpf"] = S3.sb("tmpf3", [128, 512], F32)
    return S


def tail_seq(k, S, ng, ntiles, xsrc, ydst, col, sample, mixsrc, G):
    c = k.c
    ncols = ntiles * 128
    Wo = S["w_out_bf"]
    hT2 = S["hT2"]
    fT = S["fT"]

    def x1_of(g, j):
        return S["x1"][(g % 2) * 4 + j]

    def head_a(g):
        for j in range(ntiles):
            T0 = g * 4 + j
            tok = slice(T0 * 128, (T0 + 1) * 128)
            xt = S["xt"][T0 % 2]
            c.dma("sp", xt[:], xsrc(T0), writes=[xt.b])
            b0, b1 = k.bank(), k.bank()
            lst = []
            rd = [Wo.b]
            for cc in range(8):
                m = mixsrc(cc, T0)
                if m.b not in rd:
                    rd.append(m.b)
                lst.append((b0[:, :], m[:, cc % 4, tok], Wo[:, cc, 0:512], cc == 0, cc == 7))
                lst.append((b1[:, :], m[:, cc % 4, tok], Wo[:, cc, 512:1024], cc == 0, cc == 7))
            mms(k, lst, rd, [b0.b, b1.b])
            x1 = x1_of(g, j)
            xn = S["xn"][j]
            for half, bb in enumerate((b0, b1)):
                hs = slice(half * 512, (half + 1) * 512)
                tt(k, "dve", xn[:, hs], bb[:, :], k.gate1[:, G, hs], ALU.mult, [bb.b, k.gate1.b], [xn.b])
            tt(k, "pool", x1[:], xn[:], xt[:], ALU.add, [xn.b, xt.b], [x1.b])
            st = S["st"][(g % 2) * 4 + j]
            rms_stats(k, x1, S["junk"], st, 0)
            ts(k, "dve", xn[:], x1[:], st[:, 2:3], None, ALU.mult, None, [x1.b, st.b], [xn.b])

    def head_b(g):
        xns = [S["xn"][j] for j in range(ntiles)]
        for cc in range(8):
            bk = k.bank()
            trs(k, [(bk[:, j * 128:(j + 1) * 128], xns[j][:, cc * 128:(cc + 1) * 128]) for j in range(ntiles)], k.ident,
                [x.b for x in xns] + [k.ident.b], [bk.b])
            evac_hT(k, bk, hT2, cc, ncols, 1, 2, col, sample, tmp=S["tmpf"])

    def up(g):
        for ffg in range(8):
            wu = S["wu"][ffg % 2]
            c.dma("sp", wu[:], k.w_up_bf.t.ap()[:, ffg * 512:(ffg + 1) * 512].rearrange("(k p) n -> p k n", p=128),
                  reads=[k.w_up_bf.b], writes=[wu.b])
            for fj in range(4):
                fc = ffg * 4 + fj
                bk = k.bank()
                mms(k, [(bk[:, 0:ncols], wu[:, kc, fj * 128:(fj + 1) * 128], hT2[:, kc, 0:ncols], kc == 0, kc == 7)
                        for kc in range(8)], [wu.b, hT2.b], [bk.b])
                sq = S["sqf"][fc % 2]
                act(k, sq[:, 0:ncols], bk[:, 0:ncols], AF.Square, [bk.b], [sq.b])
                stt(k, fT[:, fc, 0:ncols], bk[:, 0:ncols], 0.0, sq[:, 0:ncols], ALU.is_gt, ALU.mult, [bk.b, sq.b], [fT.b])

    def down(g, half):
        accs = [k.bank(pin=True) for _ in range(4)]
        for ffg in range(8):
            wd = S["wd"][(half * 8 + ffg) % 2]
            c.dma("sp", wd[:], k.w_down_bf.t.ap()[ffg * 512:(ffg + 1) * 512, half * 512:(half + 1) * 512]
                  .rearrange("(j p) n -> p j n", p=128), reads=[k.w_down_bf.b], writes=[wd.b])
            lst = []
            for dc in range(4):
                for fj in range(4):
                    fc = ffg * 4 + fj
                    lst.append((accs[dc][:, 0:ncols], wd[:, fj, dc * 128:(dc + 1) * 128], fT[:, fc, 0:ncols],
                                ffg == 0 and fj == 0, ffg == 7 and fj == 3))
            mms(k, lst, [wd.b, fT.b], [a.b for a in accs])
        for dc in range(4):
            dcg = half * 4 + dc
            yT = S["yT"][dc]
            if not sample:
                act(k, yT[:, 0:ncols], accs[dc][:, 0:ncols], AF.Copy, [accs[dc].b, k.adaT.b], [yT.b],
                    scale=k.adaT[:, 4, dcg, col:col + 1])
            else:
                off = ((4 * 8 + dcg) * 18 + 2)
                tt(k, "dve", yT[:, 0:128].rearrange("p (b t) -> p b t", t=TSEQ),
                   accs[dc][:, 0:128].rearrange("p (b t) -> p b t", t=TSEQ),
                   AP(k.adaT, off, [[5 * 8 * 18, 128], [1, NSS], [0, TSEQ]]), ALU.mult, [accs[dc].b, k.adaT.b], [yT.b])
        for a in accs:
            k.unpin(a)
        for j in range(ntiles):
            bk = k.bank()
            trs(k, [(bk[:, dc * 128:(dc + 1) * 128], S["yT"][dc][:, j * 128:(j + 1) * 128]) for dc in range(4)], k.ident,
                [S["yT"][dc].b for dc in range(4)] + [k.ident.b], [bk.b])
            hs = slice(half * 512, (half + 1) * 512)
            x1 = x1_of(g, j)
            tt(k, "dve", x1[:, hs], bk[:, :], x1[:, hs], ALU.add, [bk.b, x1.b], [x1.b])
            if half == 1:
                c.dma("sp", ydst(g * 4 + j), x1[:], reads=[x1.b])

    head_a(0)
    head_b(0)
    for g in range(ng):
        up(g)
        if g + 1 < ng:
            head_a(g + 1)
        down(g, 0)
        if g + 1 < ng:
            head_b(g + 1)
        down(g, 1)


def sample_group(k):
    c, nc, din = k.c, k.nc, k.din
    issue_bg(k)
    with Scope(k) as Q:
        R = {}
        R["QT"] = Q.sb("QTs", [128, 4, 128], BF16)
        R["KT"] = Q.sb("KTs", [128, 4, 128], BF16)
        R["poolT"] = Q.sb("poolTs", [128, 4, 128], BF16)
        R["attnT"] = Q.sb("attnTs", [128, 4, 128], BF16)
        uTs = Q.sb("uTs", [128, 4, NSS, 24], F32)
        vnew = Q.sb("vnew", [8, NSS, AW], BF16)
        utm = Q.sb("utm_s", [128, 512], F32)

        with Scope(k) as S1:
            S = {}
            wS = S1.sb("w_in_bf_s", [128, 8, 2048], BF16)
            S["w_in_bf"] = WParts([wS], 8)
            c.dma("sp", wS[:], k.w_in_bf.t.ap().rearrange("(k p) n -> p k n", p=128), reads=[k.w_in_bf.b], writes=[wS.b])
            S["xt"] = [S1.sb(f"xts{i}", [128, DM], F32) for i in range(4)]
            S["st"] = [S1.sb(f"sts{i}", [128, 4], F32) for i in range(4)]
            S["junk"] = S1.sb("junks", [128, DM], BF16)
            S["hT"] = [S1.sb(f"hTs{i}", [128, 8, 512], BF16) for i in range(1)] * 2
            S["tmpf"] = S1.sb("tmpfs", [128, 512], F32)
            S["sq"] = S1.sb("sqs", [128, 512], F32)
            S["sq2"] = S1.sb("sq2s", [128, 512], F32)
            S["rs"] = [S1.sb(f"rss{i}", [128, 24], F32) for i in range(2)]
            S["qn"] = [S1.sb(f"qns{i}", [128, 512], F32) for i in range(2)]
            S["kn"] = [S1.sb(f"kns{i}", [128, 512], F32) for i in range(2)]
            S["vo"] = [S1.sb(f"vos{i}", [128, 512], F32) for i in range(2)]
            S["uT"] = [None, None]
            spt = [S1.sb(f"spt{i}", [120, 512], F32) for i in range(2)]

            memset(k, "pool", uTs[:], 0.0, [uTs.b])
            for i in range(2):
                c.dma("sp", spt[i][:], din["spool"].ap()[i * 8:(i + 1) * 8, :, :].rearrange("b r c -> (b r) c"),
                      writes=[spt[i].b])
                bk = k.bank()
                trs(k, [(bk[:, uc * 120:(uc + 1) * 120], spt[i][:, uc * 128:(uc + 1) * 128]) for uc in range(4)], k.ident,
                    [spt[i].b, k.ident.b], [bk.b])
                cp(k, "dve", uTs[:, :, i * 8:(i + 1) * 8, 1:16],
                   bk[:, 0:480].rearrange("p (u b r) -> p u b r", u=4, b=8), [bk.b], [uTs.b])

            def store_k(T0, kn):
                for b in range(NSS):
                    c.dma("sp", k.dout["kws"][b, WB - TSEQ:WB, :], kn[b * 8:(b + 1) * 8, :], reads=[kn.b])

            def store_v(T0, vo):
                for b in range(NSS):
                    c.dma("sp", k.dout["vws"][b, WB - TSEQ:WB, :], vo[b * 8:(b + 1) * 8, :], reads=[vo.b],
                          writes=[k.dout["vws"].b])
                c.dma("pool", vnew[:], k.dout["vws"].t.ap()[:, WB - TSEQ:WB, :].rearrange("b t c -> t b c"),
                      reads=[k.dout["vws"].b], writes=[vnew.b])

            def store_uT(g, uc, bk, uT):
                cp(k, "act" if uc % 2 == 0 else "dve", uTs[:, uc, :, 16:24],
                   bk[:, 0:128].rearrange("p (b t) -> p b t", t=TSEQ), [bk.b], [uTs.b])

            R["store_k"], R["store_v"], R["store_uT"] = store_k, store_v, store_uT
            hT = mixer_in_group(k, S, 0, 1, lambda T0: din["xs"].ap(), 0, True, R)
            bk = k.bank()
            W = S["w_in_bf"]
            mms(k, [(bk[:, :], hT[:, kc, 0:128], W[:, kc, 1536:2048], kc == 0, kc == 7) for kc in range(8)],
                [hT.b] + W.bs, [bk.b])
            cp(k, "dve", utm[:], bk[:, :], [bk.b], [utm.b])
            for b in range(NSS):
                c.dma("sp", k.dout["pso"][b, PCTX - TSEQ:PCTX, :], utm[b * 8:(b + 1) * 8, :], reads=[utm.b])

            pa = S1.sb("pas", [128, NSS, 24], F32)
            pb = S1.sb("pbs", [128, NSS, 24], F32)
            s4 = S1.sb("s4s", [128, 4, NSS, 8], F32)
            pooled = S1.sb("pooleds", [128, 4, NSS * 8], BF16)
            n = 24
            u = uTs
            tt(k, "pool", s4[:, 0, :, :], u[:, 0, :, 16:n], u[:, 0, :, 15:n - 1], ALU.add, [u.b], [s4.b])
            for uc in (1, 2, 3):
                tt(k, "pool", pa[:, :, 1:n], u[:, uc, :, 1:n], u[:, uc, :, 0:n - 1], ALU.add, [u.b], [pa.b])
                if uc == 1:
                    tt(k, "pool", s4[:, 1, :, :], pa[:, :, 16:n], pa[:, :, 14:n - 2], ALU.add, [pa.b], [s4.b])
                    continue
                tt(k, "pool", pb[:, :, 3:n], pa[:, :, 3:n], pa[:, :, 1:n - 2], ALU.add, [pa.b], [pb.b])
                if uc == 2:
                    tt(k, "pool", s4[:, 2, :, :], pb[:, :, 16:n], pb[:, :, 12:n - 4], ALU.add, [pb.b], [s4.b])
                    continue
                tt(k, "pool", pa[:, :, 7:n], pb[:, :, 7:n], pb[:, :, 3:n - 4], ALU.add, [pb.b], [pa.b])
                tt(k, "pool", s4[:, 3, :, :], pa[:, :, 16:n], pa[:, :, 8:n - 8], ALU.add, [pa.b], [s4.b])
            for uc, w in enumerate((2, 4, 8, 16)):
                stt(k, pooled[:, uc, :].rearrange("p (b t) -> p b t", t=TSEQ), s4[:, uc, :, :], 1.0 / w, u[:, uc, :, 16:24],
                    ALU.mult, ALU.subtract, [s4.b, u.b], [pooled.b])
            for uc in range(4):
                bk = k.bank()
                mms(k, [(bk[:, 0:128], k.wpool[:, uc, :], pooled[:, uc, :], True, True)], [k.wpool.b, pooled.b], [bk.b])
                act(k, R["poolT"][:, uc, :], bk[:, 0:128], AF.Copy, [bk.b, k.pscT.b], [R["poolT"].b],
                    scale=k.pscT[:, uc:uc + 1])
        if k.stage == "s1":
            return

        with Scope(k) as S2:
            sample_attention(k, S2, R, vnew)

        with Scope(k) as S3:
            S = ffn_alloc(k, S3)
            tail_seq(k, S, 1, 1, lambda T0: din["xs"].ap(), lambda T0: k.dout["ys"].t.ap(), 0, True,
                     lambda cc, T0: (R["attnT"] if cc < 4 else R["poolT"]), 2)


def sample_attention(k, S2, R, vnew):
    c, din = k.c, k.din
    QT, KT, attnT = R["QT"], R["KT"], R["attnT"]
    qbd = S2.sb("qbd", [128, 4, NSS, 16], BF16)
    parm = S2.sb("parm", [128, 2], F32)
    bmask = S2.sb("bmask", [64, NH], F32)
    ones_bf = S2.sb("ones_bf", [128, 1], BF16)
    memset(k, "dve", parm[:], 0.0, [parm.b])
    memset(k, "dve", parm[0:64, 0:1], 1.0, [parm.b])
    memset(k, "dve", parm[64:128, 1:2], 1.0, [parm.b])
    memset(k, "dve", ones_bf[:], 1.0, [ones_bf.b])
    tt_id = k.ident[0:64, 0:64].rearrange("p (h t) -> p h t", t=TSEQ)
    reduce_add(k, bmask[:], tt_id, [k.ident.b], [bmask.b])
    tt(k, "dve", qbd[:].rearrange("p a b (q t) -> p (a b) q t", q=2),
       AP(QT, 0, [[4 * 128, 128], [8, 64], [0, 2], [1, 8]]),
       AP(parm, 0, [[2, 128], [0, 64], [1, 2], [0, 8]]), ALU.mult, [QT.b, parm.b], [qbd.b])

    NB = 3
    kc3 = [S2.sb(f"kc3_{i}", [128, 8, AW], F32) for i in range(NB)]
    kc2 = [S2.sb(f"kc2_{i}", [128, 4, AW], F32) for i in range(NB)]
    kc1 = [S2.sb(f"kc1_{i}", [128, AW], F32) for i in range(NB)]
    vc3 = [S2.sb(f"vc3_{i}", [128, 8, AW], BF16) for i in range(NB)]
    vc2 = [S2.sb(f"vc2_{i}", [128, 4, AW], BF16) for i in range(NB)]
    vc1 = [S2.sb(f"vc1_{i}", [128, AW], BF16) for i in range(NB)]
    NK = 4
    ktb = [S2.sb(f"ktb{i}", [128, 4, 128], BF16) for i in range(NK)]
    pE = [S2.sb(f"pEs{i}", [128, 64], F32) for i in range(NK)]
    pT = [S2.sb(f"pTs{i}", [128, 64], BF16) for i in range(NK)]
    od = [S2.sb(f"od{i}", [64, 8, HD], F32) for i in range(2)]
    o2 = [S2.sb(f"o2{i}", [64, 128], F32) for i in range(2)]
    dn = [S2.sb(f"dn{i}", [64, 2], F32) for i in range(2)]
    ck, cv = din["ck"], din["cv"]

    def load_seq(b):
        i = b % NB
        c.dma("sp", kc3[i][:], ck.ap()[b].rearrange("(i j) c -> i j c", j=16)[:, 0:8, :], writes=[kc3[i].b])
        c.dma("sp", kc2[i][:], ck.ap()[b, 1536:2048, :].rearrange("(i j) c -> i j c", j=4), writes=[kc2[i].b])
        c.dma("sp", kc1[i][:], ck.ap()[b, 1920:2048, :], writes=[kc1[i].b])
        c.dma("pool", vc3[i][:], cv.ap()[b].rearrange("(i j) c -> i j c", j=16)[:, 0:8, :], writes=[vc3[i].b])
        c.dma("pool", vc2[i][:], cv.ap()[b, 1536:2048, :].rearrange("(i j) c -> i j c", j=4), writes=[vc2[i].b])
        c.dma("pool", vc1[i][:], cv.ap()[b, 1920:2048, :], writes=[vc1[i].b])

    def tile_of(b, ti):
        i = b % NB
        if ti == 0:
            return kc1[i], kc1[i][:, :], vc1[i], vc1[i][:, :], 0
        if ti <= 4:
            rho = ti - 1
            return kc2[i], kc2[i][:, rho, :], vc2[i], vc2[i][:, rho, :], 1 + rho
        t0 = ti - 5
        return kc3[i], kc3[i][:, t0, :], vc3[i], vc3[i][:, t0, :], 5 + t0

    NT = 14
    steps = [(b, ti) for b in range(NSS) for ti in range(NT)]
    acc = {}
    deferred = []

    def stA(i):
        b, ti = steps[i]
        if ti == 0 and b + 1 < NSS:
            load_seq(b + 1)
        if ti == NT - 1:
            return
        kt_t, kt_ap, vt_t, vt_ap, tau = tile_of(b, ti)
        bt = k.bank()
        trs(k, [(bt[:, p * 128:(p + 1) * 128], kt_ap[:, p * 128:(p + 1) * 128]) for p in range(4)], k.ident,
            [kt_t.b, k.ident.b], [bt.b])
        kb = ktb[i % NK]
        cp(k, "act", kb[:], bt[:, :].rearrange("p (a t) -> p a t", t=128), [bt.b], [kb.b])

    def stB(i):
        b, ti = steps[i]
        e = pE[i % NK]
        p_ = pT[i % NK]
        bs = k.bank()
        if ti < NT - 1:
            tau = tile_of(b, ti)[4]
            kb = ktb[i % NK]
            mms(k, [(bs[:, p * 16:(p + 1) * 16], kb[:, p, :], qbd[:, p, b, :], True, True) for p in range(4)],
                [kb.b, qbd.b], [bs.b])
            act(k, e[:], bs[:, 0:64], AF.Exp, [bs.b], [e.b])
            tt(k, "dve", p_[:].rearrange("p (h t) -> p h t", t=TSEQ), e[:].rearrange("p (h t) -> p h t", t=TSEQ),
               AP(k.WS, tau * 64, [[896, 128], [1, NH], [8, TSEQ]]), ALU.mult, [e.b, k.WS.b], [p_.b])
        else:
            mms(k, [(bs[0:8, p * 16:(p + 1) * 16], KT[:, p, b * 8:(b + 1) * 8], qbd[:, p, b, :], True, True)
                    for p in range(4)], [KT.b, qbd.b], [bs.b])
            act(k, e[0:8, :], bs[0:8, 0:64], AF.Exp, [bs.b], [e.b])
            tt(k, "dve", p_[0:8, :].rearrange("p (h t) -> p h t", t=TSEQ), e[0:8, :].rearrange("p (h t) -> p h t", t=TSEQ),
               AP(k.WS, 13 * 64, [[896, 8], [1, NH], [8, TSEQ]]), ALU.mult, [e.b, k.WS.b], [p_.b])

    def stC(i):
        b, ti = steps[i]
        p_ = pT[i % NK]
        if ti == 0:
            acc[b] = (k.bank(pin=True), k.bank(pin=True))
        bo, bd = acc[b]
        if ti < NT - 1:
            kt_t, kt_ap, vt_t, vt_ap, tau = tile_of(b, ti)
            mms(k, [(bo[0:64, :], p_[:, :], vt_ap, ti == 0, False),
                    (bd[0:64, 0:1], p_[:, :], ones_bf[:, 0:1], ti == 0, False)], [p_.b, vt_t.b, ones_bf.b], [bo.b, bd.b])
            return
        mms(k, [(bo[0:64, :], p_[0:8, :], vnew[0:8, b, :], False, True),
                (bd[0:64, 0:1], p_[0:8, :], ones_bf[0:8, 0:1], False, True)], [p_.b, vnew.b, ones_bf.b], [bo.b, bd.b])
        od_, o2_, dn_ = od[b % 2], o2[b % 2], dn[b % 2]
        tt(k, "dve", od_[:], bo[0:64, :].rearrange("p (h e) -> p h e", e=HD),
           AP(bmask, 0, [[NH, 64], [1, NH], [0, HD]]), ALU.mult, [bo.b, bmask.b], [od_.b])
        reduce_add(k, o2_[:, 0:64], od_[:].rearrange("p h e -> p e h"), [od_.b], [o2_.b])
        recip(k, dn_[:, 0:1], bd[0:64, 0:1], [bd.b], [dn_.b])
        ts(k, "dve", o2_[:, 0:64], o2_[:, 0:64], dn_[:, 0:1], None, ALU.mult, None, [o2_.b, dn_.b], [o2_.b])
        cp(k, "dve", o2_[:, 64:128], o2_[:, 0:64], [o2_.b], [o2_.b])
        k.unpin(bo)
        k.unpin(bd)

        def epi(b=b, o2_=o2_):
            bt = k.bank()
            trs(k, [(bt[:, 0:64], o2_[:, :])], k.ident, [o2_.b, k.ident.b], [bt.b])
            for par in range(2):
                prt = slice(par * 64, par * 64 + 64)
                src = bt[prt, 0:64].rearrange("p (a q t) -> p a q t", q=2, t=TSEQ)[:, :, par, :]
                cp(k, "act", attnT[prt, :, b * 8:(b + 1) * 8], src, [bt.b], [attnT.b])

        deferred.append((i + 4, epi))

    load_seq(0)
    n = len(steps)
    for i in range(n + 8):
        if i < n:
            stA(i)
        if 1 <= i < n + 1:
            stB(i - 1)
        if 2 <= i < n + 2:
            stC(i - 2)
        for d in [d for d in deferred if d[0] <= i]:
            d[1]()
            deferred.remove(d)
    assert not deferred


_PROG = {}


def get_program(stage="full"):
    if stage not in _PROG:
        _PROG[stage] = build_program(stage)
    return _PROG[stage]


def make_core_inputs(inp, core, cst, nps=NPS, nss=NSS):
    f = lambda a: np.ascontiguousarray(np.asarray(a, dtype=np.float32))
    ps = slice(core * nps, (core + 1) * nps)
    ss = slice(core * nss, (core + 1) * nss)
    cc = np.concatenate([inp["c_prompt"][ps], inp["c_sample"][ss]], axis=0)
    m = {
        "xp": f(inp["x_prompt"][ps]),
        "xs": f(inp["x_sample"][ss].reshape(nss * TSEQ, DM)),
        "cT": f(cc.T),
        "ck": f(inp["cache_k"][0, ss].reshape(nss, WB, AW)),
        "cv": f(inp["cache_v"][0, ss].reshape(nss, WB, AW)),
        "spool": f(inp["state_pool"][0, ss]),
        "w_ada": f(inp["w_ada"][0]),
        "b_adaT": f(inp["b_ada"][0].reshape(48, 128).T),
        "b_row": f(inp["b_ada"][0].reshape(1, -1)),
        "g1T": f(inp["norm1_g"][0].reshape(8, 128).T),
        "g2T": f(inp["norm2_g"][0].reshape(8, 128).T),
        "w_in": f(inp["w_in"][0]),
        "gq2": f(np.tile(inp["q_norm_g"][0], 2).reshape(128, 1)),
        "gk_row": f(inp["k_norm_g"][0].reshape(1, HD)),
        "rel_bias": f(inp["rel_bias"]),
        "w_pool": f(inp["w_pool"][0]),
        "pscT": f(inp["pool_scale"][0].reshape(4, 128).T),
        "w_out": f(inp["w_out"][0]),
        "w_up": f(inp["w_up"][0]),
        "w_down": f(inp["w_down"][0]),
    }
    m.update(cst)
    return m


def run_cores(inp, cores, stage="full"):
    cst = make_consts()
    nc = get_program(stage)
    in_maps = [make_core_inputs(inp, cid, cst) for cid in cores]
    res = run_bass_kernel_spmd(nc, in_maps, core_ids=list(range(len(cores))))
    return res.results


def kernel(**inputs):
    inp = {k_: np.asarray(v) for k_, v in inputs.items()}
    res = run_cores(inp, list(range(NCORES)), "full")
    B = NCORES * NPS
    DB = NCORES * NSS
    yp = np.concatenate([np.asarray(r["yp"]) for r in res], axis=0).reshape(B, SEQ, DM)
    ys = np.concatenate([np.asarray(r["ys"]).reshape(NSS, TSEQ, DM) for r in res], axis=0)
    kwp = np.concatenate([np.asarray(r["kwp"]) for r in res], axis=0).reshape(1, B, SEQ, NH, HD)
    vwp = np.concatenate([np.asarray(r["vwp"]) for r in res], axis=0).reshape(1, B, SEQ, NH, HD)
    pp = np.concatenate([np.asarray(r["pp"]) for r in res], axis=0).reshape(1, B, PCTX, AW)
    kws = np.concatenate([np.asarray(r["kws"]) for r in res], axis=0).reshape(1, DB, WB, NH, HD)
    vws = np.concatenate([np.asarray(r["vws"]) for r in res], axis=0).reshape(1, DB, WB, NH, HD)
    pso = np.concatenate([np.asarray(r["pso"]) for r in res], axis=0).reshape(1, DB, PCTX, AW)
    return tuple(np.ascontiguousarray(a, dtype=np.float32) for a in (yp, ys, kwp, vwp, pp, kws, vws, pso))
```
